# Optimizing a Trainium2 kernel written in Bass

```python
import jax, jax.numpy as jnp
from jax import lax
import numpy as np

D_MODEL = 1024
BATCH = 8
SEQ = 4096
DEPTH = 4

GRID_W = 64
CTX_LEN = 256
N_MIXERS = 3
EPS = 1e-6
D_FF = -(-8 * D_MODEL // (3 * 256)) * 256
D_RNN = (4 * D_MODEL // 3) // 256 * 256
LRU_BLOCK_W = 128
LRU_BLOCKS = D_RNN // LRU_BLOCK_W
LRU_C = 8.0
CONV_W = 4
CONV_LEFT = 2
NA_HEAD_DIM = 64
NA_HEADS = D_MODEL // NA_HEAD_DIM
WIN_ROWS = 8
WIN_COLS = 16
HG_DK = 128
HG_HEADS = D_MODEL // HG_DK
HG_DV = D_MODEL // HG_HEADS
HG_CHUNK = 64

kernel_name = 'hybrid_rglru_natten_hgrn2_prefix_trunk'


def rms_norm(x, g):
    xf = x.astype(jnp.float32)
    y = xf * lax.rsqrt(jnp.mean(xf * xf, axis=-1, keepdims=True) + EPS)
    return (y * g.astype(jnp.float32)).astype(x.dtype)


def modulate(h, shift, scale):
    return h * (1 + scale) + shift


def swiglu(h, w_gu, w_down):
    a, b = jnp.split(h @ w_gu, 2, axis=-1)
    return (jax.nn.silu(a) * b) @ w_down


def short_conv(u, w, b):
    L = u.shape[1]
    up = jnp.pad(u, ((0, 0), (CONV_LEFT, CONV_W - 1 - CONV_LEFT), (0, 0)))
    out = b + w[0] * up[:, 0:L]
    for j in range(1, CONV_W):
        out = out + w[j] * up[:, j:j + L]
    return out


def linear_scan(a, b, h0):
    b = b.at[:, 0].add(a[:, 0] * h0)

    def combine(e1, e2):
        a1, b1 = e1
        a2, b2 = e2
        return a1 * a2, a2 * b1 + b2

    _, h = lax.associative_scan(combine, (a, b), axis=1)
    return h


def flip_seq(t, d):
    return jnp.flip(t, axis=1) if d == 1 else t


def rglru_gates(u, gate_w, gate_b, lam):
    B, L, _ = u.shape
    ub = u.reshape(B, L, LRU_BLOCKS, LRU_BLOCK_W)
    z = jnp.einsum('blnj,gnjk->gblnk', ub, gate_w).reshape(2, B, L, D_RNN)
    z = (z + gate_b[:, None, None, :]).astype(jnp.float32)
    r = jax.nn.sigmoid(z[0])
    i = jax.nn.sigmoid(z[1])
    log_a = -LRU_C * r * jax.nn.softplus(-lam.astype(jnp.float32))
    a = jnp.exp(log_a)
    b = jnp.sqrt(-jnp.expm1(2.0 * log_a)) * i * u.astype(jnp.float32)
    return a, b


def rglru_mixer(h_ctx, h_lat, w_in, conv_w, conv_b, gate_w, gate_b, lam, w_out, need_ctx_out):
    w_gate, w_rec = w_in[:, :D_RNN], w_in[:, D_RNN:]
    u_ctx = short_conv(h_ctx @ w_rec, conv_w, conv_b)
    u_lat = short_conv(h_lat @ w_rec, conv_w, conv_b)
    h0 = jnp.zeros((h_ctx.shape[0], D_RNN), jnp.float32)
    rec_ctx = jnp.zeros(u_ctx.shape, jnp.float32)
    rec_lat = jnp.zeros(u_lat.shape, jnp.float32)
    for d in range(2):
        hc = linear_scan(*rglru_gates(flip_seq(u_ctx, d), gate_w[d], gate_b[d], lam[d]), h0)
        hx = linear_scan(*rglru_gates(flip_seq(u_lat, d), gate_w[d], gate_b[d], lam[d]), hc[:, -1])
        rec_lat = rec_lat + flip_seq(hx, d)
        if need_ctx_out:
            rec_ctx = rec_ctx + flip_seq(hc, d)

    def readout(h, rec):
        return (rec.astype(h.dtype) * jax.nn.gelu(h @ w_gate)) @ w_out

    y_ctx = readout(h_ctx, rec_ctx) if need_ctx_out else None
    return y_ctx, readout(h_lat, rec_lat)


def na_mixer(h_ctx, h_lat, w_qkv, q_g, k_g, rpb, w_o, need_ctx_out):
    B, L, _ = h_lat.shape
    n_ctx = h_ctx.shape[1]
    rows = L // GRID_W
    kr = min(WIN_ROWS, rows)
    n_loc = kr * WIN_COLS
    scale = NA_HEAD_DIM ** -0.5

    def heads(t):
        return t.reshape(t.shape[0], t.shape[1], NA_HEADS, NA_HEAD_DIM)

    kv_c = h_ctx @ w_qkv[:, D_MODEL:]
    k_c = rms_norm(heads(kv_c[..., :D_MODEL]), k_g)
    v_c = heads(kv_c[..., D_MODEL:])

    def grid(t):
        return t.reshape(B, rows, GRID_W, NA_HEADS, NA_HEAD_DIM)

    q, k, v = jnp.split(h_lat @ w_qkv, 3, axis=-1)
    q = grid(rms_norm(heads(q), q_g))
    k = grid(rms_norm(heads(k), k_g))
    v = grid(heads(v))

    col_start = np.clip(np.arange(GRID_W) - WIN_COLS // 2, 0, GRID_W - WIN_COLS)
    col_idx = col_start[:, None] + np.arange(WIN_COLS)[None, :]
    col_off = col_idx - np.arange(GRID_W)[:, None] + (WIN_COLS - 1)

    def one_row(args):
        r, q_r = args
        rs = jnp.clip(r - kr // 2, 0, rows - kr)
        k_n = lax.dynamic_slice_in_dim(k, rs, kr, axis=1)[:, :, col_idx]
        v_n = lax.dynamic_slice_in_dim(v, rs, kr, axis=1)[:, :, col_idx]
        row_off = rs + jnp.arange(kr) - r + (WIN_ROWS - 1)
        bias = rpb[:, row_off[None, :, None], col_off[:, None, :]]
        s_loc = (jnp.einsum('bqhd,brqkhd->bhqrk', q_r, k_n).astype(jnp.float32) * scale
                 + bias.astype(jnp.float32)[None])
        s_ctx = jnp.einsum('bqhd,bchd->bhqc', q_r, k_c).astype(jnp.float32) * scale
        logits = jnp.concatenate([s_loc.reshape(B, NA_HEADS, GRID_W, n_loc), s_ctx], axis=-1)
        p = jax.nn.softmax(logits, axis=-1).astype(v.dtype)
        p_loc = p[..., :n_loc].reshape(B, NA_HEADS, GRID_W, kr, WIN_COLS)
        return (jnp.einsum('bhqrk,brqkhd->bqhd', p_loc, v_n)
                + jnp.einsum('bhqc,bchd->bqhd', p[..., n_loc:], v_c))

    o = lax.map(one_row, (jnp.arange(rows), jnp.moveaxis(q, 1, 0)))
    y_lat = jnp.moveaxis(o, 0, 1).reshape(B, L, D_MODEL) @ w_o
    y_ctx = None
    if need_ctx_out:
        q_c = rms_norm(heads(h_ctx @ w_qkv[:, :D_MODEL]), q_g)
        s = jnp.einsum('bqhd,bkhd->bhqk', q_c, k_c).astype(jnp.float32) * scale
        p = jax.nn.softmax(s, axis=-1).astype(v_c.dtype)
        y_ctx = jnp.einsum('bhqk,bkhd->bqhd', p, v_c).reshape(B, n_ctx, D_MODEL) @ w_o
    return y_ctx, y_lat


def hgrn2_lower_bounds(lb_logits):
    p = jax.nn.softmax(lb_logits.astype(jnp.float32), axis=0)
    return jnp.cumsum(p, axis=0) - p[0:1]


def gla_chunk_scan(q, k, v, log_f, s0):
    B, L, H, _ = q.shape
    n = L // HG_CHUNK

    def chunks(t):
        return jnp.transpose(t.reshape(B, n, HG_CHUNK, H, t.shape[-1]), (1, 0, 3, 2, 4))

    causal = np.tril(np.ones((HG_CHUNK, HG_CHUNK), bool))[None, None, :, :, None]

    def step(S, inp):
        qc, kc, vc, gc = inp
        b = jnp.cumsum(gc, axis=2)
        o = jnp.einsum('bhtk,bhkv->bhtv', qc * jnp.exp(b), S)
        dec = jnp.exp(jnp.where(causal, b[:, :, :, None, :] - b[:, :, None, :, :], -jnp.inf))
        A = jnp.einsum('bhtk,bhsk,bhtsk->bhts', qc, kc, dec)
        o = o + jnp.einsum('bhts,bhsv->bhtv', A, vc)
        b_end = b[:, :, -1:, :]
        S = (jnp.exp(b_end[:, :, 0, :, None]) * S
             + jnp.einsum('bhsk,bhsv->bhkv', kc * jnp.exp(b_end - b), vc))
        return S, o

    S, o = lax.scan(step, s0, (chunks(q), chunks(k), chunks(v), chunks(log_f)))
    return jnp.transpose(o, (1, 0, 3, 2, 4)).reshape(B, L, H, v.shape[-1]), S


def hgrn2_mixer(h_ctx, h_lat, w_in, lb, norm_g, w_o, need_ctx_out):
    lbh = lb.reshape(HG_HEADS, HG_DK)

    def project(h):
        B, L, _ = h.shape

        def heads(t, dd):
            return t.reshape(B, L, HG_HEADS, dd).astype(jnp.float32)

        q, i, g, z_fwd, z_bwd = jnp.split(h @ w_in, 5, axis=-1)
        gates = []
        for z in (z_fwd, z_bwd):
            z = heads(z, HG_DK)
            k = (1 - lbh) * jax.nn.sigmoid(-z)
            log_f = jnp.log(lbh + (1 - lbh) * jax.nn.sigmoid(z))
            gates.append((k, log_f))
        return heads(jax.nn.silu(q), HG_DK), heads(i, HG_DV), g, gates

    q_c, v_c, g_c, gates_c = project(h_ctx)
    q_x, v_x, g_x, gates_x = project(h_lat)
    s0 = jnp.zeros((h_lat.shape[0], HG_HEADS, HG_DK, HG_DV), jnp.float32)
    o_c = jnp.zeros(v_c.shape, jnp.float32)
    o_x = jnp.zeros(v_x.shape, jnp.float32)
    for d in range(2):
        oc, s_ctx = gla_chunk_scan(flip_seq(q_c, d), flip_seq(gates_c[d][0], d), flip_seq(v_c, d),
                                   flip_seq(gates_c[d][1], d), s0)
        ox, _ = gla_chunk_scan(flip_seq(q_x, d), flip_seq(gates_x[d][0], d), flip_seq(v_x, d),
                               flip_seq(gates_x[d][1], d), s_ctx)
        o_x = o_x + flip_seq(ox, d)
        if need_ctx_out:
            o_c = o_c + flip_seq(oc, d)

    def readout(o, g):
        o = rms_norm(o, norm_g).reshape(g.shape).astype(g.dtype)
        return (o * jax.nn.silu(g)) @ w_o

    y_ctx = readout(o_c, g_c) if need_ctx_out else None
    return y_ctx, readout(o_x, g_x)


def setup_inputs(seed: int = 0) -> dict:
    key = jax.random.key(seed)
    ks = iter(jax.random.split(key, 32))

    def nrm(shape, scale):
        return scale * jax.random.normal(next(ks), shape, jnp.float32)

    n_a, n_b, n_c = [len(range(kind, DEPTH, N_MIXERS)) for kind in range(N_MIXERS)]
    u = jax.random.uniform(next(ks), (n_a, 2, D_RNN), jnp.float32, 0.9, 0.999)
    s = u ** (1.0 / LRU_C)
    inp = {}
    inp['x'] = nrm((BATCH, SEQ, D_MODEL), 1.0)
    inp['c'] = nrm((BATCH, D_MODEL), 1.0)
    inp['ctx'] = nrm((BATCH, CTX_LEN, D_MODEL), 1.0)
    inp['c_ctx'] = nrm((D_MODEL,), 1.0)
    inp['mod_w'] = nrm((DEPTH, D_MODEL, 6 * D_MODEL), D_MODEL ** -0.5)
    inp['mod_b'] = nrm((DEPTH, 6 * D_MODEL), 0.02)
    inp['norm_mix_g'] = 1.0 + nrm((DEPTH, D_MODEL), 0.02)
    inp['norm_ffn_g'] = 1.0 + nrm((DEPTH, D_MODEL), 0.02)
    inp['ffn_w_gu'] = nrm((DEPTH, D_MODEL, 2 * D_FF), D_MODEL ** -0.5)
    inp['ffn_w_down'] = nrm((DEPTH, D_FF, D_MODEL), D_FF ** -0.5)
    inp['lru_w_in'] = nrm((n_a, D_MODEL, 2 * D_RNN), D_MODEL ** -0.5)
    inp['lru_conv_w'] = nrm((n_a, CONV_W, D_RNN), CONV_W ** -0.5)
    inp['lru_conv_b'] = nrm((n_a, D_RNN), 0.02)
    inp['lru_gate_w'] = nrm((n_a, 2, 2, LRU_BLOCKS, LRU_BLOCK_W, LRU_BLOCK_W), LRU_BLOCK_W ** -0.5)
    inp['lru_gate_b'] = nrm((n_a, 2, 2, D_RNN), 0.02)
    inp['lru_lambda'] = jnp.log(s) - jnp.log1p(-s)
    inp['lru_w_out'] = nrm((n_a, D_RNN, D_MODEL), D_RNN ** -0.5)
    inp['na_w_qkv'] = nrm((n_b, D_MODEL, 3 * D_MODEL), D_MODEL ** -0.5)
    inp['na_q_norm_g'] = 1.0 + nrm((n_b, NA_HEAD_DIM), 0.02)
    inp['na_k_norm_g'] = 1.0 + nrm((n_b, NA_HEAD_DIM), 0.02)
    inp['na_rpb'] = nrm((n_b, NA_HEADS, 2 * WIN_ROWS - 1, 2 * WIN_COLS - 1), 0.05)
    inp['na_w_o'] = nrm((n_b, D_MODEL, D_MODEL), D_MODEL ** -0.5)
    inp['hg_w_in'] = nrm((n_c, D_MODEL, 5 * D_MODEL), D_MODEL ** -0.5)
    inp['hg_lb_logits'] = nrm((DEPTH, D_MODEL), 0.1)
    inp['hg_norm_g'] = 1.0 + nrm((n_c, HG_DV), 0.02)
    inp['hg_w_o'] = nrm((n_c, D_MODEL, D_MODEL), D_MODEL ** -0.5)
    return inp


def reference(x, c, ctx, c_ctx, mod_w, mod_b, norm_mix_g, norm_ffn_g, ffn_w_gu, ffn_w_down,
              lru_w_in, lru_conv_w, lru_conv_b, lru_gate_w, lru_gate_b, lru_lambda, lru_w_out,
              na_w_qkv, na_q_norm_g, na_k_norm_g, na_rpb, na_w_o,
              hg_w_in, hg_lb_logits, hg_norm_g, hg_w_o):
    lower_bounds = hgrn2_lower_bounds(hg_lb_logits)
    act_lat = jax.nn.silu(c)
    act_ctx = jax.nn.silu(c_ctx)
    for layer in range(DEPTH):
        kind, slot = layer % N_MIXERS, layer // N_MIXERS
        need_ctx = layer < DEPTH - 1
        mx = jnp.split((act_lat @ mod_w[layer] + mod_b[layer])[:, None, :], 6, axis=-1)
        mc = jnp.split(act_ctx @ mod_w[layer] + mod_b[layer], 6, axis=-1)
        hx = modulate(rms_norm(x, norm_mix_g[layer]), mx[0], mx[1])
        hc = modulate(rms_norm(ctx, norm_mix_g[layer]), mc[0], mc[1])
        if kind == 0:
            yc, yx = rglru_mixer(hc, hx, lru_w_in[slot], lru_conv_w[slot], lru_conv_b[slot],
                                 lru_gate_w[slot], lru_gate_b[slot], lru_lambda[slot], lru_w_out[slot],
                                 need_ctx)
        elif kind == 1:
            yc, yx = na_mixer(hc, hx, na_w_qkv[slot], na_q_norm_g[slot], na_k_norm_g[slot],
                              na_rpb[slot], na_w_o[slot], need_ctx)
        else:
            yc, yx = hgrn2_mixer(hc, hx, hg_w_in[slot], lower_bounds[layer], hg_norm_g[slot],
                                 hg_w_o[slot], need_ctx)
        x = x + mx[2] * yx
        x = x + mx[5] * swiglu(modulate(rms_norm(x, norm_ffn_g[layer]), mx[3], mx[4]),
                               ffn_w_gu[layer], ffn_w_down[layer])
        if need_ctx:
            ctx = ctx + mc[2] * yc
            ctx = ctx + mc[5] * swiglu(modulate(rms_norm(ctx, norm_ffn_g[layer]), mc[3], mc[4]),
                                       ffn_w_gu[layer], ffn_w_down[layer])
    return x
```

```python
import numpy as np
import concourse.bass as bass
import concourse.mybir as mybir
from concourse.bass_utils import run_bass_kernel_spmd

F32 = mybir.dt.float32
BF16 = mybir.dt.bfloat16
AF = mybir.ActivationFunctionType
ALU = mybir.AluOpType

D = 1024
KC = 8
SEQ = 4096
NCTX = 256
NTOK = SEQ + NCTX
DEPTH = 4
DFF = 2816
NFF = 22
DRNN = 1280
NRC = 10
EPS = 1e-6
EPOCH = 30000
TT = [(0, 256)] + [(256 + 512 * i, 512) for i in range(8)]
ARENA_WORDS = 52000


class Buf:
    __slots__ = ("ap", "name", "w", "r")

    def __init__(self, ap, name):
        self.ap = ap
        self.name = name
        self.w = None
        self.r = {}


class KB:
    def __init__(self):
        self.nc = bass.Bass("TRN2", target_bir_lowering=False)
        nc = self.nc
        self.E = {"pe": nc.tensor, "act": nc.scalar, "dve": nc.vector, "pool": nc.gpsimd, "sp": nc.sync}
        self.tick = dict.fromkeys(self.E, 0)
        self.known = {e: {} for e in self.E}
        self.sems = {}
        self.ndma = 24
        self.dma_cnt = [0] * self.ndma
        self.dma_rr = 0
        self.arena = nc.alloc_sbuf_tensor("arena", [128, ARENA_WORDS], F32)
        self.off = 0
        self.psum = [Buf(nc.alloc_psum_tensor(f"ps{i}", [128, 512], F32)[:], f"ps{i}") for i in range(8)]
        self.ps_rr = 0
        self.ps_lo = 0
        self.ps_hi = 8
        self.nops = 0

    def alloc(self, name, free_shape, dtype=F32):
        n = int(np.prod(free_shape))
        words = (n + 1) // 2 if dtype == BF16 else n
        words = (words + 7) // 8 * 8
        assert self.off + words <= ARENA_WORDS, f"arena overflow at {name}: {self.off}+{words}"
        ap = self.arena[:, self.off:self.off + words]
        self.off += words
        if dtype == BF16:
            ap = ap.bitcast(BF16)
        ap = ap[:, 0:n]
        if len(free_shape) > 1:
            names = [f"d{i}" for i in range(len(free_shape))]
            kw = {nm: int(free_shape[i]) for i, nm in enumerate(names) if i > 0}
            ap = ap.rearrange(f"p ({' '.join(names)}) -> p {' '.join(names)}", **kw)
        return Buf(ap, name)

    def ps(self):
        if not (self.ps_lo <= self.ps_rr < self.ps_hi):
            self.ps_rr = self.ps_lo
        b = self.psum[self.ps_rr]
        self.ps_rr += 1
        if self.ps_rr >= self.ps_hi:
            self.ps_rr = self.ps_lo
        return b

    def _sem(self, key):
        s = self.sems.get(key)
        if s is None:
            s = self.nc.alloc_semaphore(name=f"s_{key[0]}_{key[1]}")
            self.sems[key] = s
        return s

    def _deps(self, reads, writes):
        deps = {}

        def add(ev):
            if ev is not None and deps.get(ev[0], 0) < ev[1]:
                deps[ev[0]] = ev[1]

        for b in reads:
            add(b.w)
        for b in writes:
            add(b.w)
            for k, v in b.r.items():
                add((k, v))
        return deps

    def _wait(self, e, deps):
        kn = self.known[e]
        for key, val in deps.items():
            if kn.get(key, 0) >= val:
                continue
            self.E[e].wait_ge(self._sem(key), val)
            kn[key] = val

    def _mark(self, ev, reads, writes):
        for b in reads:
            if b.r.get(ev[0], 0) < ev[1]:
                b.r[ev[0]] = ev[1]
        for b in writes:
            b.w = ev
            b.r = {}

    def op(self, e, fn, reads=(), writes=()):
        deps = self._deps(reads, writes)
        if e == "pe":
            deps = {k: v for k, v in deps.items() if k[0] != "pe"}
        self._wait(e, deps)
        ins = fn(self.E[e])
        self.tick[e] += 1
        t = self.tick[e]
        ep = (t - 1) // EPOCH
        key = (e, ep)
        ev = (key, t - ep * EPOCH)
        ins.then_inc(self._sem(key), 1)
        self._mark(ev, reads, writes)
        self.nops += 1
        return ev

    def dma(self, q, out, in_, reads=(), writes=()):
        j = self.dma_rr
        self.dma_rr = (j + 1) % self.ndma
        deps = self._deps(reads, writes)
        key = ("dma", j)
        if self.dma_cnt[j] > 0:
            deps[key] = max(deps.get(key, 0), 16 * self.dma_cnt[j])
        self._wait(q, deps)
        ins = self.E[q].dma_start(out=out, in_=in_)
        self.dma_cnt[j] += 1
        ev = (key, 16 * self.dma_cnt[j])
        ins.then_inc(self._sem(key), 16)
        self._mark(ev, reads, writes)
        return ev

    def all_events(self):
        deps = {}
        for e, t in self.tick.items():
            if t > 0:
                ep = (t - 1) // EPOCH
                deps[(e, ep)] = t - ep * EPOCH
        for j, c in enumerate(self.dma_cnt):
            if c > 0:
                deps[("dma", j)] = 16 * c
        return deps

    def barrier(self):
        deps = self.all_events()
        for e in self.E:
            self._wait(e, dict(deps))

    def reset_arena(self, off):
        self.barrier()
        self.off = off

    def finish(self):
        self._wait("sp", self.all_events())


def build(cfg):
    kb = KB()
    nc = kb.nc
    phases = cfg["phases"]
    IN = {}

    def din(name, shape, dt=F32):
        IN[name] = nc.dram_tensor(name, list(shape), dt, kind="ExternalInput").ap()
        return IN[name]

    xT_in = din("xT", [D, NTOK])
    cT = din("cT", [128, KC, 2])
    mod_w = din("mod_w", [DEPTH, D, 6 * D])
    mod_b = din("mod_b", [128, DEPTH, 48])
    norm_g = din("norm_g", [128, DEPTH, 2, KC])
    ffn_w_gu = din("ffn_w_gu", [DEPTH, D, 2 * DFF])
    ffn_w_down = din("ffn_w_down", [DEPTH, DFF, D])
    lru_w_in = din("lru_w_in", [2, D, 2 * DRNN])
    lru_conv = din("lru_conv", [128, 2, NRC, 5])
    lru_gate_w = din("lru_gate_w", [2, 2, 2, NRC, 128, 128])
    lru_gvec = din("lru_gvec", [128, 2, 2, NRC, 3])
    lru_w_out = din("lru_w_out", [2, DRNN, D])
    hg_w_in = din("hg_w_in", [1, D, 5 * D])
    hg_lb = din("hg_lb", [128, KC, DEPTH])
    hg_ng = din("hg_ng", [128, 1])
    hg_w_o = din("hg_w_o", [1, D, D])
    hgc = din("hgc", [128, 2, 128])
    na_w_qkv = din("na_w_qkv", [1, D, 3 * D])
    na_w_o = din("na_w_o", [1, D, D])
    na_qkg = din("na_qkg", [128, 2])
    na_rp = din("na_rp", [16, 64, 15, 64])
    na_cmask = din("na_cmask", [128, 15, 64])
    nac = din("nac", [128, 3, 128])
    consts = din("consts", [128, 512])
    outT = nc.dram_tensor("outT", [D, SEQ], F32, kind="ExternalOutput").ap()
    XT = nc.dram_tensor("XTs", [D, NTOK], F32, kind="Internal").ap()
    MT = nc.dram_tensor("MTs", [DRNN, NTOK], BF16, kind="Internal").ap()

    XTv = XT.rearrange("(kc p) t -> p kc t", p=128)
    XTb = [[Buf(None, f"XT{i}_{k}") for k in range(KC)] for i in range(len(TT))]
    MTb = [Buf(None, f"MT{i}") for i in range(NRC)]
    last_layer = max(l for _, l in phases)

    ident_f = kb.alloc("ident_f", [128], F32)
    ones_f = kb.alloc("ones_f", [128], F32)
    ident_b = kb.alloc("ident_b", [128], BF16)
    ones_b = kb.alloc("ones_b", [128], BF16)
    MOD = kb.alloc("MOD", [DEPTH, 48, 2], F32)
    NG = kb.alloc("NG", [DEPTH, 2, KC], F32)
    GS = kb.alloc("GS", [DEPTH, 2, KC, 2], F32)
    kb.dma("sp", ident_f.ap, consts[:, 0:128], writes=[ident_f])
    kb.dma("sp", ones_f.ap, consts[:, 128:256], writes=[ones_f])
    kb.dma("sp", NG.ap, norm_g, writes=[NG])
    kb.op("dve", lambda v: v.tensor_copy(out=ident_b.ap, in_=ident_f.ap), [ident_f], [ident_b])
    kb.op("dve", lambda v: v.tensor_copy(out=ones_b.ap, in_=ones_f.ap), [ones_f], [ones_b])
    epsb = kb.alloc("epsb", [1], F32)
    kb.op("pool", lambda g: g.memset(epsb.ap, EPS), [], [epsb])
    base_off = kb.off

    for ti, (t0, n) in enumerate(TT):
        kb.dma("sp", XT[:, t0:t0 + n], xT_in[:, t0:t0 + n], writes=XTb[ti])
    c_raw = kb.alloc("c_raw", [KC, 2], F32)
    c_act = kb.alloc("c_act", [KC, 2], F32)
    mb = kb.alloc("mb", [DEPTH, 48], F32)
    kb.dma("sp", c_raw.ap, cT, writes=[c_raw])
    kb.dma("sp", mb.ap, mod_b, writes=[mb])
    kb.op("act", lambda a: a.activation(out=c_act.ap, in_=c_raw.ap, func=AF.Silu), [c_raw], [c_act])
    layers_needed = sorted(set(l for _, l in phases))
    mw = [kb.alloc(f"mw{i}", [KC, 1024], F32) for i in range(2)]
    cnt = 0
    for l in layers_needed:
        for v in range(6):
            w = mw[cnt % 2]
            cnt += 1
            kb.dma("sp", w.ap, mod_w[l].rearrange("(kc p) f -> p kc f", p=128)[:, :, v * 1024:(v + 1) * 1024],
                   writes=[w])
            for mc in range(8):
                p = kb.ps()

                def mm(pe, w=w, mc=mc, p=p):
                    for kc in range(KC):
                        ins = pe.matmul(p.ap[:, 0:2], lhsT=w.ap[:, kc, mc * 128:(mc + 1) * 128],
                                        rhs=c_act.ap[:, kc, :], start=(kc == 0), stop=(kc == KC - 1))
                    return ins
                kb.op("pe", mm, [w, c_act], [p])
                ch = v * 8 + mc
                kb.op("dve", lambda d, p=p, l=l, ch=ch: d.tensor_scalar(
                    out=MOD.ap[:, l, ch, :], in0=p.ap[:, 0:2], scalar1=mb.ap[:, l, ch:ch + 1], scalar2=None,
                    op0=ALU.add), [p, mb], [MOD])
        for which, vs in ((0, 1), (1, 4)):
            for kc in range(KC):
                kb.op("dve", lambda d, l=l, which=which, vs=vs, kc=kc: d.tensor_scalar(
                    out=GS.ap[:, l, which, kc, :], in0=MOD.ap[:, l, vs * 8 + kc, :], scalar1=1.0,
                    scalar2=NG.ap[:, l, which, kc:kc + 1], op0=ALU.add, op1=ALU.mult), [MOD, NG], [GS])

    def modv(l, v, kc, ty):
        return MOD.ap[:, l, v * 8 + kc, ty:ty + 1]

    def norm_tile(l, which, ti, xs, sq, rs, out_fn, tmp):
        t0, n = TT[ti]
        ty = 1 if ti == 0 else 0
        kb.dma("sp", xs.ap[:, :, 0:n], XTv[:, :, t0:t0 + n], reads=XTb[ti], writes=[xs])
        kb.op("act", lambda a: a.activation(out=sq.ap[:, :, 0:n], in_=xs.ap[:, :, 0:n], func=AF.Square), [xs], [sq])
        p = kb.ps()

        def mm(pe):
            for kc in range(KC):
                ins = pe.matmul(p.ap[:, 0:n], lhsT=ones_b.ap, rhs=sq.ap[:, kc, 0:n], start=(kc == 0), stop=(kc == KC - 1))
            return ins
        kb.op("pe", mm, [ones_b, sq], [p])
        kb.op("act", lambda a: a.activation(out=rs.ap[:, 0:n], in_=p.ap[:, 0:n], func=AF.Sqrt, scale=1.0 / D, bias=EPS_AP()),
              [p, epsb], [rs])
        kb.op("dve", lambda d: d.reciprocal(out=rs.ap[:, 0:n], in_=rs.ap[:, 0:n]), [rs], [rs])
        sh_v = 0 if which == 0 else 3
        for kc in range(KC):
            tb = tmp[kc % len(tmp)]
            kb.op("dve", lambda d, kc=kc, tb=tb: d.tensor_tensor(out=tb.ap[:, 0:n], in0=xs.ap[:, kc, 0:n], in1=rs.ap[:, 0:n],
                                                                op=ALU.mult), [xs, rs], [tb])
            oap, ob = out_fn(kc)
            kb.op("act", lambda a, kc=kc, tb=tb, oap=oap: a.activation(
                out=oap, in_=tb.ap[:, 0:n], func=AF.Identity, scale=GS.ap[:, l, which, kc, ty:ty + 1],
                bias=modv(l, sh_v, kc, ty)), [tb, GS, MOD], [ob])

    def EPS_AP():
        return epsb.ap[:, 0:1]

    def load_w(dst, dst_ap, src_ap):
        kb.dma("pool", dst_ap, src_ap, writes=[dst])

    def resid(l, gate_v, ti, mc, ypsum, xres):
        t0, n = TT[ti]
        ty = 1 if ti == 0 else 0
        kb.dma("sp", xres.ap[:, 0:n], XT[mc * 128:(mc + 1) * 128, t0:t0 + n], reads=[XTb[ti][mc]], writes=[xres])
        kb.op("dve", lambda d: d.scalar_tensor_tensor(out=xres.ap[:, 0:n], in0=ypsum.ap[:, 0:n], scalar=modv(l, gate_v, mc, ty),
                                                      in1=xres.ap[:, 0:n], op0=ALU.mult, op1=ALU.add), [ypsum, xres, MOD], [xres])
        kb.dma("sp", XT[mc * 128:(mc + 1) * 128, t0:t0 + n], xres.ap[:, 0:n], reads=[xres], writes=[XTb[ti][mc]])

    def ffn_phase(l):
        kb.reset_arena(base_off)
        TMAX = 1536
        H2 = kb.alloc("H2", [KC, TMAX], BF16)
        G = kb.alloc("G", [NFF, TMAX], BF16)
        xs = kb.alloc("xs", [KC, 512], F32)
        sq = kb.alloc("sq", [KC, 512], BF16)
        rs = kb.alloc("rs", [512], F32)
        tmp = [kb.alloc(f"tmp{i}", [512], F32) for i in range(2)]
        wgu = [kb.alloc(f"wgu{i}", [KC, 2, 256], BF16) for i in range(2)]
        wdn = [kb.alloc(f"wdn{i}", [NFF, 256], BF16) for i in range(2)]
        xres = [kb.alloc(f"xres{i}", [512], F32) for i in range(3)]
        sil = [kb.alloc(f"sil{i}", [512], F32) for i in range(2)]
        wguv = ffn_w_gu[l].rearrange("(kc p) (two f) -> p kc two f", p=128, two=2)
        wdnv = ffn_w_down[l].rearrange("(kc p) f -> p kc f", p=128)
        sts = [[0, 1, 2], [3, 4, 5], [6, 7, 8]]
        ngu = 0
        ndn = 0
        nxr = 0
        for st in sts:
            if l == DEPTH - 1 and False:
                pass
            offs = {}
            o = 0
            for ti in st:
                offs[ti] = o
                o += TT[ti][1]
            for ti in st:
                n = TT[ti][1]
                norm_tile(l, 1, ti, xs, sq, rs, lambda kc, ti=ti, n=n: (H2.ap[:, kc, offs[ti]:offs[ti] + n], H2), tmp)
            for jg in range(NFF // 2):
                w = wgu[ngu % 2]
                ngu += 1
                for two in range(2):
                    load_w(w, w.ap[:, :, two, :], wguv[:, :, two, jg * 256:(jg + 1) * 256])
                for jj in range(2):
                    j = jg * 2 + jj
                    for ti in st:
                        n = TT[ti][1]
                        pa = kb.ps()
                        pb = kb.ps()

                        def mm(pe, w=w, jj=jj, ti=ti, n=n, pa=pa, pb=pb):
                            for two, p in ((0, pa), (1, pb)):
                                for kc in range(KC):
                                    ins = pe.matmul(p.ap[:, 0:n], lhsT=w.ap[:, kc, two, jj * 128:(jj + 1) * 128],
                                                    rhs=H2.ap[:, kc, offs[ti]:offs[ti] + n], start=(kc == 0), stop=(kc == KC - 1))
                            return ins
                        kb.op("pe", mm, [w, H2], [pa, pb])
                        s = sil[(j + ti) % 2]
                        kb.op("act", lambda a, pa=pa, s=s, n=n: a.activation(out=s.ap[:, 0:n], in_=pa.ap[:, 0:n], func=AF.Silu), [pa], [s])
                        kb.op("dve", lambda d, pb=pb, s=s, n=n, j=j, ti=ti: d.tensor_tensor(
                            out=G.ap[:, j, offs[ti]:offs[ti] + n], in0=s.ap[:, 0:n], in1=pb.ap[:, 0:n], op=ALU.mult), [s, pb], [G])
            for mg in range(4):
                w = wdn[ndn % 2]
                ndn += 1
                load_w(w, w.ap, wdnv[:, :, mg * 256:(mg + 1) * 256])
                for mm_ in range(2):
                    mc = mg * 2 + mm_
                    for ti in st:
                        n = TT[ti][1]
                        p = kb.ps()

                        def mm(pe, w=w, mm_=mm_, ti=ti, n=n, p=p):
                            for k in range(NFF):
                                ins = pe.matmul(p.ap[:, 0:n], lhsT=w.ap[:, k, mm_ * 128:(mm_ + 1) * 128],
                                                rhs=G.ap[:, k, offs[ti]:offs[ti] + n], start=(k == 0), stop=(k == NFF - 1))
                            return ins
                        kb.op("pe", mm, [w, G], [p])
                        xr = xres[nxr % 3]
                        nxr += 1
                        resid(l, 5, ti, mc, p, xr)

    def rglru_phase(l):
        slot = l // 3
        kb.reset_arena(base_off)
        HT = [kb.alloc(f"HT{ti}", [KC, TT[ti][1]], BF16) for ti in range(len(TT))]
        o1 = kb.off
        xs = kb.alloc("xs", [KC, 512], F32)
        sq = kb.alloc("sq", [KC, 512], BF16)
        rs = kb.alloc("rs", [512], F32)
        tmp = [kb.alloc(f"tmp{i}", [512], F32) for i in range(2)]
        for ti in range(len(TT)):
            n = TT[ti][1]
            norm_tile(l, 0, ti, xs, sq, rs, lambda kc, ti=ti, n=n: (HT[ti].ap[:, kc, 0:n], HT[ti]), tmp)
        kb.reset_arena(o1)
        U = kb.alloc("U", [NTOK], F32)
        A = kb.alloc("A", [NTOK], F32)
        T1 = kb.alloc("T1", [NTOK], F32)
        T2 = kb.alloc("T2", [NTOK], F32)
        H0 = kb.alloc("H0", [NTOK], F32)
        Ub = kb.alloc("Ub", [NTOK], BF16)
        win = [kb.alloc(f"win{i}", [KC, 2, 128], BF16) for i in range(2)]
        gw = [kb.alloc(f"gw{i}", [2, 2, 128], BF16) for i in range(2)]
        cv = kb.alloc("cv", [2, NRC, 5], F32)
        gv = kb.alloc("gv", [2, 2, NRC, 3], F32)
        cexp = kb.alloc("cexp", [2, NRC], F32)
        gtmp = [kb.alloc(f"gtmp{i}", [512], F32) for i in range(2)]
        mt = [kb.alloc(f"mt{i}", [512], BF16) for i in range(2)]
        kb.dma("sp", cv.ap, lru_conv, writes=[cv])
        kb.dma("sp", gv.ap, lru_gvec, writes=[gv])
        for d_ in range(2):
            kb.op("act", lambda a, d_=d_: a.activation(out=cexp.ap[:, d_, :], in_=gv.ap[:, slot, d_, :, 2], func=AF.Exp, scale=-1.0), [gv], [cexp])
            kb.op("act", lambda a, d_=d_: a.activation(out=cexp.ap[:, d_, :], in_=cexp.ap[:, d_, :], func=AF.Ln, bias=1.0), [cexp], [cexp])
            kb.op("dve", lambda v, d_=d_: v.tensor_scalar(out=cexp.ap[:, d_, :], in0=cexp.ap[:, d_, :], scalar1=-8.0, scalar2=None, op0=ALU.mult), [cexp], [cexp])
        winv = lru_w_in[slot].rearrange("(kc p) (two f) -> p kc two f", p=128, two=2)
        segs = [(0, NCTX), (NCTX, NTOK)]
        for n_ in range(NRC):
            w = win[n_ % 2]
            g = gw[n_ % 2]
            for two in range(2):
                load_w(w, w.ap[:, :, two, :], winv[:, :, two, n_ * 128:(n_ + 1) * 128])
                load_w(g, g.ap[:, two, :, :], lru_gate_w[slot, two, :, n_].rearrange("g j k -> j g k"))
            for ti, (t0, n) in enumerate(TT):
                p = kb.ps()

                def mm(pe, w=w, ti=ti, n=n, p=p):
                    for kc in range(KC):
                        ins = pe.matmul(p.ap[:, 0:n], lhsT=w.ap[:, kc, 1, :], rhs=HT[ti].ap[:, kc, 0:n], start=(kc == 0), stop=(kc == KC - 1))
                    return ins
                kb.op("pe", mm, [w, HT[ti]], [p])
                kb.op("act", lambda a, p=p, t0=t0, n=n: a.activation(out=A.ap[:, t0:t0 + n], in_=p.ap[:, 0:n], func=AF.Copy), [p], [A])
            kb.op("dve", lambda v, n_=n_: v.tensor_scalar(out=U.ap, in0=A.ap, scalar1=cv.ap[:, slot, n_, 2:3], scalar2=cv.ap[:, slot, n_, 4:5],
                                                          op0=ALU.mult, op1=ALU.add), [A, cv], [U])
            for j in (0, 1, 3):
                s = j - 2
                for (a0, a1) in segs:
                    lo = max(a0, a0 - s)
                    hi = min(a1, a1 - s)
                    kb.op("dve", lambda v, n_=n_, j=j, s=s, lo=lo, hi=hi: v.scalar_tensor_tensor(
                        out=U.ap[:, lo:hi], in0=A.ap[:, lo + s:hi + s], scalar=cv.ap[:, slot, n_, j:j + 1], in1=U.ap[:, lo:hi],
                        op0=ALU.mult, op1=ALU.add), [A, U, cv], [U])
            kb.op("act", lambda a: a.activation(out=Ub.ap, in_=U.ap, func=AF.Copy), [U], [Ub])
            for d_ in range(2):
                for ti, (t0, n) in enumerate(TT):
                    pr = kb.ps()
                    pi = kb.ps()

                    def mm(pe, g=g, d_=d_, t0=t0, n=n, pr=pr, pi=pi):
                        pe.matmul(pr.ap[:, 0:n], lhsT=g.ap[:, d_, 0, :], rhs=Ub.ap[:, t0:t0 + n], start=True, stop=True)
                        return pe.matmul(pi.ap[:, 0:n], lhsT=g.ap[:, d_, 1, :], rhs=Ub.ap[:, t0:t0 + n], start=True, stop=True)
                    kb.op("pe", mm, [g, Ub], [pr, pi])
                    kb.op("act", lambda a, pr=pr, t0=t0, n=n, d_=d_, n_=n_: a.activation(
                        out=A.ap[:, t0:t0 + n], in_=pr.ap[:, 0:n], func=AF.Sigmoid, bias=gv.ap[:, slot, d_, n_, 0:1]), [pr, gv], [A])
                    kb.op("act", lambda a, pi=pi, t0=t0, n=n, d_=d_, n_=n_: a.activation(
                        out=T2.ap[:, t0:t0 + n], in_=pi.ap[:, 0:n], func=AF.Sigmoid, bias=gv.ap[:, slot, d_, n_, 1:2]), [pi, gv], [T2])
                kb.op("act", lambda a, d_=d_, n_=n_: a.activation(out=A.ap, in_=A.ap, func=AF.Exp, scale=cexp.ap[:, d_, n_:n_ + 1]), [A, cexp], [A])
                kb.op("dve", lambda v: v.tensor_tensor(out=T1.ap, in0=A.ap, in1=A.ap, op=ALU.mult), [A], [T1])
                kb.op("act", lambda a: a.activation(out=T1.ap, in_=T1.ap, func=AF.Sqrt, scale=-1.0, bias=1.0), [T1], [T1])
                kb.op("pool", lambda v: v.tensor_tensor(out=T2.ap, in0=T2.ap, in1=U.ap, op=ALU.mult), [T2, U], [T2])
                kb.op("dve", lambda v: v.tensor_tensor(out=T2.ap, in0=T2.ap, in1=T1.ap, op=ALU.mult), [T2, T1], [T2])
                if d_ == 0:
                    kb.op("dve", lambda v: v.tensor_tensor_scan(out=H0.ap, data0=A.ap, data1=T2.ap, initial=0.0, op0=ALU.mult, op1=ALU.add),
                          [A, T2], [H0])
                else:
                    kb.op("dve", lambda v: v.tensor_tensor_scan(out=T1.ap[:, NCTX - 1::-1] if False else T1.ap[:, 0:NCTX][:, ::-1],
                                                                data0=A.ap[:, 0:NCTX][:, ::-1], data1=T2.ap[:, 0:NCTX][:, ::-1],
                                                                initial=0.0, op0=ALU.mult, op1=ALU.add), [A, T2], [T1])
                    kb.op("dve", lambda v: v.tensor_tensor_scan(out=T1.ap[:, NCTX:NTOK][:, ::-1], data0=A.ap[:, NCTX:NTOK][:, ::-1],
                                                                data1=T2.ap[:, NCTX:NTOK][:, ::-1], initial=T1.ap[:, 0:1],
                                                                op0=ALU.mult, op1=ALU.add), [A, T2, T1], [T1])
            kb.op("pool", lambda v: v.tensor_tensor(out=H0.ap, in0=H0.ap, in1=T1.ap, op=ALU.add), [H0, T1], [H0])
            for ti, (t0, n) in enumerate(TT):
                m = mt[ti % 2]
                p = kb.ps()

                def mm(pe, w=w, ti=ti, n=n, p=p):
                    for kc in range(KC):
                        ins = pe.matmul(p.ap[:, 0:n], lhsT=w.ap[:, kc, 0, :], rhs=HT[ti].ap[:, kc, 0:n], start=(kc == 0), stop=(kc == KC - 1))
                    return ins
                kb.op("pe", mm, [w, HT[ti]], [p])
                gt = gtmp[ti % 2]
                kb.op("act", lambda a, p=p, n=n, gt=gt: a.activation(out=gt.ap[:, 0:n], in_=p.ap[:, 0:n], func=AF.Gelu_apprx_tanh), [p], [gt])
                kb.op("dve", lambda v, gt=gt, t0=t0, n=n, m=m: v.tensor_tensor(out=m.ap[:, 0:n], in0=gt.ap[:, 0:n], in1=H0.ap[:, t0:t0 + n],
                                                                              op=ALU.mult), [gt, H0], [m])
                kb.dma("sp", MT[n_ * 128:(n_ + 1) * 128, t0:t0 + n], m.ap[:, 0:n], reads=[m], writes=[MTb[n_]])
        out_proj(l, lru_w_out[slot], NRC)

    def out_proj(l, w_dram, nk):
        kb.reset_arena(base_off)
        wo = kb.alloc("wo", [nk, D], BF16)
        load_w(wo, wo.ap, w_dram.rearrange("(n p) f -> p n f", p=128))
        mts = [kb.alloc(f"mts{i}", [nk, 512], BF16) for i in range(2)]
        xres = [kb.alloc(f"xres{i}", [512], F32) for i in range(3)]
        MTv = MT.rearrange("(n p) t -> p n t", p=128)
        nxr = 0
        for ti, (t0, n) in enumerate(TT):
            ms = mts[ti % 2]
            kb.dma("sp", ms.ap[:, :, 0:n], MTv[:, 0:nk, t0:t0 + n], reads=MTb[0:nk], writes=[ms])
            for mc in range(KC):
                p = kb.ps()

                def mm(pe, ms=ms, mc=mc, n=n, p=p):
                    for k in range(nk):
                        ins = pe.matmul(p.ap[:, 0:n], lhsT=wo.ap[:, k, mc * 128:(mc + 1) * 128], rhs=ms.ap[:, k, 0:n],
                                        start=(k == 0), stop=(k == nk - 1))
                    return ins
                kb.op("pe", mm, [wo, ms], [p])
                xr = xres[nxr % 3]
                nxr += 1
                resid(l, 2, ti, mc, p, xr)

    def norm_all(l):
        HT = [kb.alloc(f"HT{ti}", [KC, TT[ti][1]], BF16) for ti in range(len(TT))]
        o1 = kb.off
        xs = kb.alloc("xs", [KC, 512], F32)
        sq = kb.alloc("sq", [KC, 512], BF16)
        rs = kb.alloc("rs", [512], F32)
        tmp = [kb.alloc(f"tmp{i}", [512], F32) for i in range(2)]
        for ti in range(len(TT)):
            n = TT[ti][1]
            norm_tile(l, 0, ti, xs, sq, rs, lambda kc, ti=ti, n=n: (HT[ti].ap[:, kc, 0:n], HT[ti]), tmp)
        kb.reset_arena(o1)
        return HT

    def hgrn2_phase(l):
        slot = 0
        kb.reset_arena(base_off)
        HT = norm_all(l)
        NB = NTOK // 128
        HC = kb.alloc("HC", [2, 128], F32)
        kb.dma("sp", HC.ap, hgc, writes=[HC])
        maskb = kb.alloc("maskb", [2, 128], BF16)
        kb.op("dve", lambda v: v.tensor_copy(out=maskb.ap, in_=HC.ap), [HC], [maskb])
        ng = kb.alloc("hng", [1], F32)
        kb.dma("sp", ng.ap, hg_ng, writes=[ng])
        lbl = kb.alloc("lbl", [KC, DEPTH], F32)
        lbv = kb.alloc("lbv", [KC, 4], F32)
        kb.dma("sp", lbl.ap, hg_lb, writes=[lbl])
        kb.op("act", lambda a: a.activation(out=lbl.ap, in_=lbl.ap, func=AF.Exp), [lbl], [lbl])
        kb.op("dve", lambda v: v.tensor_tensor(out=lbv.ap[:, :, 2], in0=lbl.ap[:, :, 0], in1=lbl.ap[:, :, 1], op=ALU.add), [lbl], [lbv])
        kb.op("dve", lambda v: v.tensor_tensor(out=lbv.ap[:, :, 2], in0=lbv.ap[:, :, 2], in1=lbl.ap[:, :, 2], op=ALU.add), [lbl, lbv], [lbv])
        kb.op("dve", lambda v: v.tensor_tensor(out=lbv.ap[:, :, 2], in0=lbv.ap[:, :, 2], in1=lbl.ap[:, :, 3], op=ALU.add), [lbl, lbv], [lbv])
        kb.op("dve", lambda v: v.memset(lbv.ap[:, :, 3], 0.0), [lbv], [lbv])
        for j in range(1, l + 1):
            kb.op("dve", lambda v, j=j: v.tensor_tensor(out=lbv.ap[:, :, 3], in0=lbv.ap[:, :, 3], in1=lbl.ap[:, :, j], op=ALU.add), [lbl, lbv], [lbv])
        kb.op("dve", lambda v: v.reciprocal(out=lbv.ap[:, :, 2], in_=lbv.ap[:, :, 2]), [lbv], [lbv])
        kb.op("dve", lambda v: v.tensor_tensor(out=lbv.ap[:, :, 0], in0=lbv.ap[:, :, 3], in1=lbv.ap[:, :, 2], op=ALU.mult), [lbv], [lbv])
        kb.op("dve", lambda v: v.tensor_scalar(out=lbv.ap[:, :, 1], in0=lbv.ap[:, :, 0], scalar1=-1.0, scalar2=1.0, op0=ALU.mult, op1=ALU.add), [lbv], [lbv])

        whd = kb.alloc("whd", [KC, 5, 128], BF16)
        QT = kb.alloc("QT", [NTOK], BF16)
        Vtok = kb.alloc("Vtok", [NB, 128], BF16)
        O = kb.alloc("O", [NTOK], F32)
        KT = [kb.alloc(f"KT{d}", [NTOK], BF16) for d in range(2)]
        Bc = [kb.alloc(f"Bc{d}", [NTOK], F32) for d in range(2)]
        S = [kb.alloc(f"S{d}", [128], F32) for d in range(2)]
        ft = [kb.alloc(f"ft{i}", [512], F32) for i in range(2)]
        lt = [kb.alloc(f"lt{i}", [512], F32) for i in range(2)]
        E1 = [kb.alloc(f"E1_{i}", [128], F32) for i in range(2)]
        E2 = [kb.alloc(f"E2_{i}", [128], F32) for i in range(2)]
        EB = [kb.alloc(f"EB_{i}", [4], F32) for i in range(2)]
        Qt = [kb.alloc(f"Qt{i}", [128], BF16) for i in range(2)]
        Kt = [kb.alloc(f"Kt{i}", [128], BF16) for i in range(2)]
        KtT = [kb.alloc(f"KtT{i}", [128], BF16) for i in range(2)]
        Sp = [kb.alloc(f"Sp{i}", [128], BF16) for i in range(2)]
        Am = [kb.alloc(f"Am{i}", [128], BF16) for i in range(2)]
        tS = [kb.alloc(f"tS{i}", [128], F32) for i in range(2)]
        sgt = ft
        ogt = lt
        sqo = [kb.alloc(f"sqo{i}", [512], BF16) for i in range(2)]
        rso = [kb.alloc(f"rso{i}", [512], F32) for i in range(2)]
        ogb = [kb.alloc(f"ogb{i}", [512], BF16) for i in range(2)]
        wv = hg_w_in[slot].rearrange("(kc p) (v f) -> p kc v f", p=128, v=5)
        order = [list(range(NB)), [1, 0] + list(range(NB - 1, 1, -1))]
        for hd in range(8):
            for v_ in range(5):
                load_w(whd, whd.ap[:, :, v_, :], wv[:, :, v_, hd * 128:(hd + 1) * 128])
            kb.op("pool", lambda g: g.memset(O.ap, 0.0), [], [O])
            for d in range(2):
                kb.op("pool", lambda g, d=d: g.memset(S[d].ap, 0.0), [], [S[d]])
            for ti, (t0, n) in enumerate(TT):
                def proj(v_, ti=ti, n=n):
                    p = kb.ps()

                    def mm(pe):
                        for kc in range(KC):
                            ins = pe.matmul(p.ap[:, 0:n], lhsT=whd.ap[:, kc, v_, :], rhs=HT[ti].ap[:, kc, 0:n], start=(kc == 0), stop=(kc == KC - 1))
                        return ins
                    kb.op("pe", mm, [whd, HT[ti]], [p])
                    return p
                p = proj(0)
                kb.op("act", lambda a, p=p, t0=t0, n=n: a.activation(out=QT.ap[:, t0:t0 + n], in_=p.ap[:, 0:n], func=AF.Silu), [p], [QT])
                for d in range(2):
                    p = proj(3 + d)
                    f_ = ft[d]
                    l_ = lt[d]
                    kb.op("act", lambda a, p=p, n=n, f_=f_: a.activation(out=f_.ap[:, 0:n], in_=p.ap[:, 0:n], func=AF.Sigmoid), [p], [f_])
                    kb.op("dve", lambda v, n=n, f_=f_, hd=hd: v.tensor_scalar(out=f_.ap[:, 0:n], in0=f_.ap[:, 0:n], scalar1=lbv.ap[:, hd, 1:2],
                                                                            scalar2=lbv.ap[:, hd, 0:1], op0=ALU.mult, op1=ALU.add), [f_, lbv], [f_])
                    kb.op("act", lambda a, n=n, f_=f_, l_=l_: a.activation(out=l_.ap[:, 0:n], in_=f_.ap[:, 0:n], func=AF.Ln), [f_], [l_])
                    kb.op("dve", lambda v, n=n, f_=f_, d=d, t0=t0: v.tensor_scalar(out=KT[d].ap[:, t0:t0 + n], in0=f_.ap[:, 0:n], scalar1=-1.0, scalar2=1.0,
                                                                                  op0=ALU.mult, op1=ALU.add), [f_], [KT[d]])
                    for bi in range(n // 128):
                        c0 = t0 + bi * 128
                        if d == 0:
                            kb.op("dve", lambda v, l_=l_, bi=bi, c0=c0: v.tensor_tensor_scan(
                                out=Bc[0].ap[:, c0:c0 + 128], data0=ones_f.ap, data1=l_.ap[:, bi * 128:(bi + 1) * 128], initial=0.0,
                                op0=ALU.mult, op1=ALU.add), [ones_f, l_], [Bc[0]])
                        else:
                            kb.op("dve", lambda v, l_=l_, bi=bi, c0=c0: v.tensor_tensor_scan(
                                out=Bc[1].ap[:, c0:c0 + 128][:, ::-1], data0=ones_f.ap, data1=l_.ap[:, bi * 128:(bi + 1) * 128][:, ::-1], initial=0.0,
                                op0=ALU.mult, op1=ALU.add), [ones_f, l_], [Bc[1]])
                for bi in range(n // 128):
                    blk = t0 // 128 + bi
                    p = kb.ps()

                    def mmv(pe, ti=ti, bi=bi, p=p):
                        for kc in range(KC):
                            ins = pe.matmul(p.ap[:, 0:128], lhsT=HT[ti].ap[:, kc, bi * 128:(bi + 1) * 128], rhs=whd.ap[:, kc, 1, :],
                                            start=(kc == 0), stop=(kc == KC - 1))
                        return ins
                    kb.op("pe", mmv, [whd, HT[ti]], [p])
                    kb.op("act", lambda a, p=p, blk=blk: a.activation(out=Vtok.ap[:, blk, :], in_=p.ap[:, 0:128], func=AF.Copy), [p], [Vtok])
            for j in range(NB):
                for d in range(2):
                    blk = order[d][j]
                    i2 = d
                    c0 = blk * 128
                    tsl = slice(c0, c0 + 128)
                    midc = c0 + (63 if d == 0 else 64)
                    endc = c0 + (127 if d == 0 else 0)
                    epos = 127 if d == 0 else 0
                    kb.op("dve", lambda v, i2=i2, d=d, midc=midc: v.tensor_scalar(out=EB[i2].ap[:, 2:3], in0=Bc[d].ap[:, midc:midc + 1], scalar1=-1.0, scalar2=None,
                                                                                 op0=ALU.mult), [Bc[d]], [EB[i2]])
                    kb.op("act", lambda a, i2=i2, d=d, tsl=tsl: a.activation(out=E1[i2].ap, in_=Bc[d].ap[:, tsl], func=AF.Exp, bias=EB[i2].ap[:, 2:3]), [Bc[d], EB[i2]], [E1[i2]])
                    kb.op("act", lambda a, i2=i2, d=d, tsl=tsl, midc=midc: a.activation(out=E2[i2].ap, in_=Bc[d].ap[:, tsl], func=AF.Exp, scale=-1.0,
                                                                                       bias=Bc[d].ap[:, midc:midc + 1]), [Bc[d]], [E2[i2]])
                    kb.op("act", lambda a, i2=i2, d=d, endc=endc: a.activation(out=EB[i2].ap[:, 0:1], in_=Bc[d].ap[:, endc:endc + 1], func=AF.Exp), [Bc[d]], [EB[i2]])
                    kb.op("act", lambda a, i2=i2, d=d, midc=midc: a.activation(out=EB[i2].ap[:, 1:2], in_=Bc[d].ap[:, midc:midc + 1], func=AF.Exp), [Bc[d]], [EB[i2]])
                    kb.op("dve", lambda v, i2=i2, tsl=tsl: v.tensor_tensor(out=Qt[i2].ap, in0=QT.ap[:, tsl], in1=E1[i2].ap, op=ALU.mult), [QT, E1[i2]], [Qt[i2]])
                    kb.op("dve", lambda v, i2=i2, tsl=tsl, d=d: v.tensor_tensor(out=Kt[i2].ap, in0=KT[d].ap[:, tsl], in1=E2[i2].ap, op=ALU.mult), [KT[d], E2[i2]], [Kt[i2]])
                    kb.op("dve", lambda v, i2=i2, d=d: v.tensor_scalar(out=Sp[i2].ap, in0=S[d].ap, scalar1=EB[i2].ap[:, 1:2], scalar2=None, op0=ALU.mult), [S[d], EB[i2]], [Sp[i2]])
                    pK = kb.ps()
                    pKb = pK.ap.bitcast(BF16)
                    kb.op("pe", lambda pe, pKb=pKb, i2=i2: pe.transpose(pKb[:, 0:128], Kt[i2].ap, ident_b.ap), [Kt[i2], ident_b], [pK])
                    kb.op("act", lambda a, pKb=pKb, i2=i2: a.activation(out=KtT[i2].ap, in_=pKb[:, 0:128], func=AF.Copy), [pK], [KtT[i2]])
                    pA = kb.ps()
                    kb.op("pe", lambda pe, pA=pA, i2=i2: pe.matmul(pA.ap[:, 0:128], lhsT=Kt[i2].ap, rhs=Qt[i2].ap, start=True, stop=True), [Kt[i2], Qt[i2]], [pA])
                    kb.op("dve", lambda v, pA=pA, i2=i2: v.tensor_scalar(out=tS[i2].ap, in0=pA.ap[:, 0:128], scalar1=1e30, scalar2=-1e30, op0=ALU.min, op1=ALU.max), [pA], [tS[i2]])
                    kb.op("dve", lambda v, i2=i2, d=d: v.tensor_tensor(out=Am[i2].ap, in0=tS[i2].ap, in1=maskb.ap[:, d, :], op=ALU.mult), [tS[i2], maskb], [Am[i2]])
                    pO = kb.ps()

                    def mmo(pe, pO=pO, i2=i2, blk=blk):
                        pe.matmul(pO.ap[:, 0:128], lhsT=Sp[i2].ap, rhs=Qt[i2].ap, start=True, stop=False)
                        return pe.matmul(pO.ap[:, 0:128], lhsT=Vtok.ap[:, blk, :], rhs=Am[i2].ap, start=False, stop=True)
                    kb.op("pe", mmo, [Sp[i2], Qt[i2], Vtok, Am[i2]], [pO])
                    kb.op("dve", lambda v, pO=pO, tsl=tsl: v.tensor_tensor(out=O.ap[:, tsl], in0=pO.ap[:, 0:128], in1=O.ap[:, tsl], op=ALU.add), [pO, O], [O])
                    pS = kb.ps()
                    kb.op("pe", lambda pe, pS=pS, i2=i2, blk=blk: pe.matmul(pS.ap[:, 0:128], lhsT=KtT[i2].ap, rhs=Vtok.ap[:, blk, :], start=True, stop=True), [KtT[i2], Vtok], [pS])
                    kb.op("dve", lambda v, pS=pS, i2=i2, epos=epos: v.tensor_scalar(out=tS[i2].ap, in0=pS.ap[:, 0:128], scalar1=E1[i2].ap[:, epos:epos + 1], scalar2=None,
                                                                                   op0=ALU.mult), [pS, E1[i2]], [tS[i2]])
                    kb.op("dve", lambda v, i2=i2, d=d: v.scalar_tensor_tensor(out=S[d].ap, in0=S[d].ap, scalar=EB[i2].ap[:, 0:1], in1=tS[i2].ap,
                                                                             op0=ALU.mult, op1=ALU.add), [S[d], EB[i2], tS[i2]], [S[d]])
            for ti, (t0, n) in enumerate(TT):
                i2 = ti % 2
                p = kb.ps()

                def mmg(pe, ti=ti, n=n, p=p):
                    for kc in range(KC):
                        ins = pe.matmul(p.ap[:, 0:n], lhsT=whd.ap[:, kc, 2, :], rhs=HT[ti].ap[:, kc, 0:n], start=(kc == 0), stop=(kc == KC - 1))
                    return ins
                kb.op("pe", mmg, [whd, HT[ti]], [p])
                kb.op("act", lambda a, p=p, n=n, i2=i2: a.activation(out=sgt[i2].ap[:, 0:n], in_=p.ap[:, 0:n], func=AF.Silu), [p], [sgt[i2]])
                kb.op("act", lambda a, t0=t0, n=n, i2=i2: a.activation(out=sqo[i2].ap[:, 0:n], in_=O.ap[:, t0:t0 + n], func=AF.Square), [O], [sqo[i2]])
                p2 = kb.ps()
                kb.op("pe", lambda pe, p2=p2, n=n, i2=i2: pe.matmul(p2.ap[:, 0:n], lhsT=ones_b.ap, rhs=sqo[i2].ap[:, 0:n], start=True, stop=True), [ones_b, sqo[i2]], [p2])
                kb.op("act", lambda a, p2=p2, n=n, i2=i2: a.activation(out=rso[i2].ap[:, 0:n], in_=p2.ap[:, 0:n], func=AF.Sqrt, scale=1.0 / 128, bias=EPS_AP()), [p2, epsb], [rso[i2]])
                kb.op("dve", lambda v, n=n, i2=i2: v.reciprocal(out=rso[i2].ap[:, 0:n], in_=rso[i2].ap[:, 0:n]), [rso[i2]], [rso[i2]])
                kb.op("dve", lambda v, t0=t0, n=n, i2=i2: v.scalar_tensor_tensor(out=ogt[i2].ap[:, 0:n], in0=O.ap[:, t0:t0 + n], scalar=ng.ap[:, 0:1], in1=rso[i2].ap[:, 0:n],
                                                                                op0=ALU.mult, op1=ALU.mult), [O, ng, rso[i2]], [ogt[i2]])
                kb.op("dve", lambda v, n=n, i2=i2: v.tensor_tensor(out=ogb[i2].ap[:, 0:n], in0=ogt[i2].ap[:, 0:n], in1=sgt[i2].ap[:, 0:n], op=ALU.mult), [ogt[i2], sgt[i2]], [ogb[i2]])
                kb.dma("sp", MT[hd * 128:(hd + 1) * 128, t0:t0 + n], ogb[i2].ap[:, 0:n], reads=[ogb[i2]], writes=[MTb[hd]])
        out_proj(l, hg_w_o[slot], 8)

    def na_phase(l):
        slot = 0
        kb.reset_arena(base_off)
        HT = norm_all(l)
        NB = NTOK // 128
        kb.ps_lo, kb.ps_hi = 0, 4
        ncf = kb.alloc("ncf", [3, 128], F32)
        kb.dma("sp", ncf.ap, nac, writes=[ncf])
        ncb = kb.alloc("ncb", [3, 128], BF16)
        kb.op("dve", lambda v: v.tensor_copy(out=ncb.ap, in_=ncf.ap), [ncf], [ncb])
        cmask = kb.alloc("cmask", [15, 64], F32)
        kb.dma("sp", cmask.ap, na_cmask, writes=[cmask])
        qkg = kb.alloc("qkg", [2], F32)
        kb.dma("sp", qkg.ap, na_qkg, writes=[qkg])
        wq = kb.alloc("wqkv", [KC, 3, 128], BF16)
        QK = [kb.alloc(f"QK{i}", [NTOK], BF16) for i in range(2)]
        VAB = [kb.alloc(f"VAB{i}", [NB, 128], BF16) for i in range(2)]
        for i in range(2):
            kb.op("pool", lambda g, i=i: g.memset(VAB[i].ap, 0.0), [], [VAB[i]])
        EBraw = [kb.alloc(f"EBraw{i}", [15, 64], F32) for i in range(2)]
        EBt = [kb.alloc(f"EBt{i}", [15, 64], BF16) for i in range(2)]
        NCH = (6, 4, 4)
        MK = [[kb.alloc(f"MK{h2}_{c}", [NCH[c] * 256], BF16) for c in range(3)] for h2 in range(2)]
        qf = [kb.alloc(f"qf{i}", [512], F32) for i in range(2)]
        sqn = [kb.alloc(f"sqn{i}", [512], BF16) for i in range(2)]
        rsn = [kb.alloc(f"rsn{i}", [512], F32) for i in range(2)]
        P2 = [kb.alloc(f"P2_{i}", [2, 256], BF16) for i in range(4)]
        rden = [kb.alloc(f"rden{i}", [256], F32) for i in range(2)]
        ob = [kb.alloc(f"ob{i}", [256], BF16) for i in range(2)]
        wv3 = na_w_qkv[slot].rearrange("(kc p) (v f) -> p kc v f", p=128, v=3)
        blocks = []
        for qb in range(16):
            r0 = 4 * qb
            if qb == 0:
                cls, kr0s = 1, [0, 2, 4, 6]
            elif qb == 15:
                cls, kr0s = 2, [56, 58, 60, 62]
            else:
                cls, kr0s = 0, [r0 - 4 + 2 * c for c in range(6)]
            keys = [(NCTX + kr0 * 64, ci) for ci, kr0 in enumerate(kr0s)] + [(0, None), (128, None)]
            blocks.append((NCTX + r0 * 64, cls, keys))
        blocks.append((0, None, [(0, None), (128, None)]))
        cls_def = {0: (8, [4, 6, 8, 10, 12, 14]), 1: (0, [0, 2, 4, 6]), 2: (60, [56, 58, 60, 62])}
        nP = 0
        nblk = 0
        for hp in range(8):
            for v_ in range(3):
                load_w(wq, wq.ap[:, :, v_, :], wv3[:, :, v_, hp * 128:(hp + 1) * 128])
            for ti, (t0, n) in enumerate(TT):
                for which in range(2):
                    p = kb.ps()

                    def mm(pe, p=p, which=which, ti=ti, n=n):
                        for kc in range(KC):
                            ins = pe.matmul(p.ap[:, 0:n], lhsT=wq.ap[:, kc, which, :], rhs=HT[ti].ap[:, kc, 0:n], start=(kc == 0), stop=(kc == KC - 1))
                        return ins
                    kb.op("pe", mm, [wq, HT[ti]], [p])
                    i2 = which
                    kb.op("act", lambda a, p=p, n=n, i2=i2: a.activation(out=qf[i2].ap[:, 0:n], in_=p.ap[:, 0:n], func=AF.Copy), [p], [qf[i2]])
                    kb.op("act", lambda a, n=n, i2=i2: a.activation(out=sqn[i2].ap[:, 0:n], in_=qf[i2].ap[:, 0:n], func=AF.Square), [qf[i2]], [sqn[i2]])
                    p2 = kb.ps()
                    kb.op("pe", lambda pe, p2=p2, n=n, i2=i2: pe.matmul(p2.ap[:, 0:n], lhsT=ncb.ap[:, 0, :], rhs=sqn[i2].ap[:, 0:n], start=True, stop=True), [ncb, sqn[i2]], [p2])
                    kb.op("act", lambda a, p2=p2, n=n, i2=i2: a.activation(out=rsn[i2].ap[:, 0:n], in_=p2.ap[:, 0:n], func=AF.Sqrt, scale=1.0 / 64, bias=EPS_AP()), [p2, epsb], [rsn[i2]])
                    kb.op("dve", lambda v, n=n, i2=i2: v.reciprocal(out=rsn[i2].ap[:, 0:n], in_=rsn[i2].ap[:, 0:n]), [rsn[i2]], [rsn[i2]])
                    kb.op("dve", lambda v, n=n, i2=i2, which=which, t0=t0: v.scalar_tensor_tensor(
                        out=QK[which].ap[:, t0:t0 + n], in0=qf[i2].ap[:, 0:n], scalar=qkg.ap[:, which:which + 1], in1=rsn[i2].ap[:, 0:n],
                        op0=ALU.mult, op1=ALU.mult), [qf[i2], qkg, rsn[i2]], [QK[which]])
                for bi in range(n // 128):
                    blk = t0 // 128 + bi
                    p = kb.ps()

                    def mmv(pe, ti=ti, bi=bi, p=p):
                        for kc in range(KC):
                            ins = pe.matmul(p.ap[:, 0:128], lhsT=HT[ti].ap[:, kc, bi * 128:(bi + 1) * 128], rhs=wq.ap[:, kc, 2, :],
                                            start=(kc == 0), stop=(kc == KC - 1))
                        return ins
                    kb.op("pe", mmv, [wq, HT[ti]], [p])
                    kb.op("act", lambda a, p=p, blk=blk: a.activation(out=VAB[0].ap[:, blk, 0:64], in_=p.ap[:, 0:64], func=AF.Copy), [p], [VAB[0]])
                    kb.op("dve", lambda v, p=p, blk=blk: v.tensor_copy(out=VAB[1].ap[:, blk, 64:128], in_=p.ap[:, 64:128]), [p], [VAB[1]])
            for h2 in range(2):
                h = 2 * hp + h2
                for half in range(2):
                    kb.dma("sp", EBraw[h2].ap[half * 64:(half + 1) * 64], na_rp[h], writes=[EBraw[h2]])
                kb.op("act", lambda a, h2=h2: a.activation(out=EBraw[h2].ap, in_=EBraw[h2].ap, func=AF.Exp), [EBraw[h2]], [EBraw[h2]])
                kb.op("dve", lambda v, h2=h2: v.tensor_tensor(out=EBt[h2].ap, in0=EBraw[h2].ap, in1=cmask.ap, op=ALU.mult), [EBraw[h2], cmask], [EBt[h2]])
                for cls in range(3):
                    r0, kr0s = cls_def[cls]
                    mk = MK[h2][cls]
                    kb.op("pool", lambda g, mk=mk: g.memset(mk.ap, 0.0), [], [mk])

                    def copies(g, mk=mk, r0=r0, kr0s=kr0s, cls=cls, h2=h2):
                        ins = None
                        for ci, kr0 in enumerate(kr0s):
                            for a_ in range(2):
                                for bq in range(4):
                                    dr = kr0 + a_ - r0 - bq
                                    if cls == 0 and not (-4 <= dr <= 3):
                                        continue
                                    assert -7 <= dr <= 7
                                    ins = g.tensor_copy(out=mk.ap[a_ * 64:(a_ + 1) * 64, ci * 256 + bq * 64:ci * 256 + (bq + 1) * 64],
                                                        in_=EBt[h2].ap[a_ * 64:(a_ + 1) * 64, dr + 7, :])
                        return ins
                    kb.op("pool", copies, [EBt[h2]], [mk])
            for (q0, cls, keys) in blocks:
                num = kb.psum[4 + 2 * (nblk % 2)]
                den = kb.psum[5 + 2 * (nblk % 2)]
                i2 = nblk % 2
                nblk += 1
                npairs = len(keys) // 2
                total = 2 * npairs
                cnt = 0
                for h2 in range(2):
                    pb = 64 * h2
                    for pi in range(npairs):
                        (k0, c0), (k1, c1) = keys[2 * pi], keys[2 * pi + 1]
                        pS = kb.ps()

                        def mms(pe, pS=pS, pb=pb, k0=k0, k1=k1, q0=q0):
                            pe.matmul(pS.ap[:, 0:256], lhsT=QK[1].ap[pb:pb + 64, k0:k0 + 128], rhs=QK[0].ap[pb:pb + 64, q0:q0 + 256], start=True, stop=True)
                            return pe.matmul(pS.ap[:, 256:512], lhsT=QK[1].ap[pb:pb + 64, k1:k1 + 128], rhs=QK[0].ap[pb:pb + 64, q0:q0 + 256], start=True, stop=True)
                        kb.op("pe", mms, [QK[0], QK[1]], [pS])
                        P = P2[nP % 4]
                        nP += 1
                        kb.op("act", lambda a, pS=pS, P=P: a.activation(out=P.ap.rearrange("p a b -> p (a b)"), in_=pS.ap, func=AF.Exp, scale=0.125), [pS], [P])
                        if c0 is not None:
                            mk = MK[h2][cls]
                            kb.op("pool", lambda g, P=P, mk=mk, c0=c0: g.tensor_tensor(out=P.ap.rearrange("p a b -> p (a b)"), in0=P.ap.rearrange("p a b -> p (a b)"),
                                                                                       in1=mk.ap[:, c0 * 256:c0 * 256 + 512], op=ALU.mult), [P, mk], [P])
                        first = (cnt == 0)
                        last = (cnt == total - 1)
                        cnt += 1

                        def mmpv(pe, P=P, h2=h2, k0=k0, k1=k1, first=first, last=last, num=num, den=den):
                            for c, kk in ((0, k0), (1, k1)):
                                st = first and c == 0
                                sp_ = last and c == 1
                                pe.matmul(num.ap[:, 0:256], lhsT=VAB[h2].ap[:, kk // 128, :], rhs=P.ap[:, c, :], start=st, stop=sp_)
                                ins = pe.matmul(den.ap[:, 0:256], lhsT=ncb.ap[:, 1 + h2, :], rhs=P.ap[:, c, :], start=st, stop=sp_)
                            return ins
                        kb.op("pe", mmpv, [P, VAB[h2], ncb], [num, den])
                kb.op("dve", lambda v, den=den, i2=i2: v.reciprocal(out=rden[i2].ap, in_=den.ap[:, 0:256]), [den], [rden[i2]])
                kb.op("dve", lambda v, num=num, i2=i2: v.tensor_tensor(out=ob[i2].ap, in0=num.ap[:, 0:256], in1=rden[i2].ap, op=ALU.mult), [num, rden[i2]], [ob[i2]])
                kb.dma("sp", MT[hp * 128:(hp + 1) * 128, q0:q0 + 256], ob[i2].ap, reads=[ob[i2]], writes=[MTb[hp]])
        kb.ps_lo, kb.ps_hi = 0, 8
        out_proj(l, na_w_o[slot], 8)

    for ph, l in phases:
        if ph == "ffn":
            ffn_phase(l)
        elif ph == "mix":
            kind = l % 3
            if kind == 0:
                rglru_phase(l)
            elif kind == 2:
                hgrn2_phase(l)
            else:
                na_phase(l)
    kb.barrier()
    for ti, (t0, n) in enumerate(TT):
        if ti == 0 and not cfg.get("dump_ctx"):
            continue
    if cfg.get("dump_all"):
        outA = nc.dram_tensor("outA", [D, NTOK], F32, kind="ExternalOutput").ap()
        for ti, (t0, n) in enumerate(TT):
            kb.dma("sp", outA[:, t0:t0 + n], XT[:, t0:t0 + n], reads=XTb[ti])
    for ti, (t0, n) in enumerate(TT):
        if ti == 0:
            continue
        kb.dma("sp", outT[:, t0 - NCTX:t0 - NCTX + n], XT[:, t0:t0 + n], reads=XTb[ti])
    kb.finish()
    return kb


def host_inputs(inp, b):
    f = np.float32
    m = {}
    m["xT"] = np.ascontiguousarray(np.concatenate([inp["ctx"][b], inp["x"][b]], axis=0).T.astype(f))
    cT = np.stack([inp["c"][b].reshape(KC, 128).T, inp["c_ctx"].reshape(KC, 128).T], axis=-1)
    m["cT"] = np.ascontiguousarray(cT.astype(f))
    m["mod_w"] = np.ascontiguousarray(inp["mod_w"])
    m["mod_b"] = np.ascontiguousarray(inp["mod_b"].reshape(DEPTH, 48, 128).transpose(2, 0, 1))
    ng = np.stack([inp["norm_mix_g"].reshape(DEPTH, KC, 128), inp["norm_ffn_g"].reshape(DEPTH, KC, 128)], axis=1)
    m["norm_g"] = np.ascontiguousarray(ng.transpose(3, 0, 1, 2))
    m["ffn_w_gu"] = np.ascontiguousarray(inp["ffn_w_gu"])
    m["ffn_w_down"] = np.ascontiguousarray(inp["ffn_w_down"])
    m["lru_w_in"] = np.ascontiguousarray(inp["lru_w_in"])
    cw = inp["lru_conv_w"].reshape(2, 4, NRC, 128)
    cb = inp["lru_conv_b"].reshape(2, 1, NRC, 128)
    cv = np.concatenate([cw, cb], axis=1)
    m["lru_conv"] = np.ascontiguousarray(cv.transpose(3, 0, 2, 1))
    m["lru_gate_w"] = np.ascontiguousarray(inp["lru_gate_w"])
    gb = inp["lru_gate_b"].reshape(2, 2, 2, NRC, 128)
    lam = inp["lru_lambda"].reshape(2, 2, 1, NRC, 128)
    gvv = np.concatenate([gb, lam], axis=2)
    m["lru_gvec"] = np.ascontiguousarray(gvv.transpose(4, 0, 1, 3, 2))
    m["lru_w_out"] = np.ascontiguousarray(inp["lru_w_out"])
    m["hg_w_in"] = np.ascontiguousarray(inp["hg_w_in"])
    m["hg_lb"] = np.ascontiguousarray(inp["hg_lb_logits"].reshape(DEPTH, KC, 128).transpose(2, 1, 0))
    m["hg_ng"] = np.ascontiguousarray(inp["hg_norm_g"].reshape(128, 1))
    m["hg_w_o"] = np.ascontiguousarray(inp["hg_w_o"])
    m["hgc"] = hg_consts()
    m["na_w_qkv"] = np.ascontiguousarray(inp["na_w_qkv"])
    m["na_w_o"] = np.ascontiguousarray(inp["na_w_o"])
    m["na_qkg"] = np.ascontiguousarray(np.stack([np.tile(inp["na_q_norm_g"][0], 2), np.tile(inp["na_k_norm_g"][0], 2)], axis=1).astype(f))
    kc_ = np.arange(64)[:, None]
    qc_ = np.arange(64)[None, :]
    co = kc_ - qc_ + 15
    ok = (co >= 0) & (co <= 30)
    rp = inp["na_rpb"][0][:, :, np.clip(co, 0, 30)]
    rp = np.where(ok[None, None], rp, 0.0).astype(f)
    m["na_rp"] = np.ascontiguousarray(rp.transpose(0, 2, 1, 3))
    m["na_cmask"], m["nac"] = na_consts()
    c = np.zeros((128, 512), f)
    c[:, 0:128] = np.eye(128, dtype=f)
    c[:, 128:256] = 1.0
    m["consts"] = c
    return m


def na_consts():
    kc = np.arange(64)[:, None]
    qc = np.arange(64)[None, :]
    cs = np.clip(qc - 8, 0, 48)
    cm = ((kc >= cs) & (kc < cs + 16)).astype(np.float32)
    cmask = np.ascontiguousarray(np.broadcast_to(np.tile(cm, (2, 1))[:, None, :], (128, 15, 64))).astype(np.float32)
    nac = np.zeros((128, 3, 128), np.float32)
    nac[0:64, 0, 0:64] = 1.0
    nac[64:128, 0, 64:128] = 1.0
    nac[:, 1, 0:64] = 1.0
    nac[:, 2, 64:128] = 1.0
    return cmask, nac


def hg_consts():
    c = np.zeros((128, 2, 128), np.float32)
    s_ = np.arange(128)[:, None]
    t_ = np.arange(128)[None, :]
    c[:, 0, :] = (s_ <= t_)
    c[:, 1, :] = (s_ >= t_)
    return c


FULL_PHASES = [(p, l) for l in range(DEPTH) for p in ("mix", "ffn")]


def kernel(**inputs):
    inp = {k: np.asarray(v) for k, v in inputs.items()}
    kb = build({"phases": FULL_PHASES})
    in_maps = [host_inputs(inp, b) for b in range(8)]
    res = run_bass_kernel_spmd(kb.nc, in_maps, core_ids=list(range(8)))
    out = np.stack([np.ascontiguousarray(r["outT"].T) for r in res.results], axis=0)
    return out.astype(np.float32)
```

```python
import numpy as np
import concourse.bass as bass
import concourse.mybir as mybir
from concourse.bass_utils import run_bass_kernel_spmd

F32 = mybir.dt.float32
BF16 = mybir.dt.bfloat16
AF = mybir.ActivationFunctionType
ALU = mybir.AluOpType

D = 1024
KC = 8
SEQ = 4096
NCTX = 256
NTOK = SEQ + NCTX
DEPTH = 4
DFF = 2816
NFF = 22
DRNN = 1280
NRC = 10
EPS = 1e-6
EPOCH = 30000
TT = [(0, 256)] + [(256 + 512 * i, 512) for i in range(8)]
ARENA_WORDS = 52000


class Buf:
    __slots__ = ("ap", "name", "w", "r")

    def __init__(self, ap, name):
        self.ap = ap
        self.name = name
        self.w = None
        self.r = []


class _Dummy:
    def then_inc(self, *a, **k):
        return self


class _CostProxy:
    RATE = {"pe": 2.4, "act": 1.2, "dve": 0.96, "pool": 0.6, "sp": 1.0}

    def __init__(self, eng):
        self.eng = eng
        self.cost = 0.0

    def __getattr__(self, name):
        def f(*args, **kw):
            if name in ("matmul", "transpose"):
                rhs = kw.get("rhs")
                if rhs is None:
                    rhs = args[1] if len(args) > 1 else args[0]
                n = rhs.free_size()
                mult = 4.0 if rhs.dtype == F32 else 1.0
                self.cost += mult * max(n, 64) / 2.4 + 4.0
            else:
                out = kw.get("out", kw.get("ap", args[0] if args else None))
                n = out.free_size() if out is not None else 64
                if name == "tensor_tensor_scan":
                    n *= 2
                self.cost += 70.0 + max(n, 64) / self.RATE[self.eng]
            return _Dummy()
        return f


class Node:
    __slots__ = ("kind", "eng", "fn", "preds", "cost", "ev", "dma_args", "lat")

    def __init__(self, kind, eng, fn, preds, cost, dma_args=None, lat=0.0):
        self.kind = kind
        self.eng = eng
        self.fn = fn
        self.preds = preds
        self.cost = cost
        self.ev = None
        self.dma_args = dma_args
        self.lat = lat


class KB:
    WINDOW = 96

    def __init__(self):
        self.nc = bass.Bass("TRN2", target_bir_lowering=False)
        nc = self.nc
        self.E = {"pe": nc.tensor, "act": nc.scalar, "dve": nc.vector, "pool": nc.gpsimd, "sp": nc.sync}
        self.tick = dict.fromkeys(self.E, 0)
        self.known = {e: {} for e in self.E}
        self.sems = {}
        self.ndma = 24
        self.dma_cnt = [0] * self.ndma
        self.dma_rr = 0
        self.arena = nc.alloc_sbuf_tensor("arena", [128, ARENA_WORDS], F32)
        self.off = 0
        self.psum = [Buf(nc.alloc_psum_tensor(f"ps{i}", [128, 512], F32)[:], f"ps{i}") for i in range(8)]
        self.ps_rr = 0
        self.ps_lo = 0
        self.ps_hi = 8
        self.nops = 0
        self.nodes = []
        self.log = {e: [] for e in self.E}

    def alloc(self, name, free_shape, dtype=F32):
        n = int(np.prod(free_shape))
        words = (n + 1) // 2 if dtype == BF16 else n
        words = (words + 7) // 8 * 8
        assert self.off + words <= ARENA_WORDS, f"arena overflow at {name}: {self.off}+{words}"
        ap = self.arena[:, self.off:self.off + words]
        self.off += words
        if dtype == BF16:
            ap = ap.bitcast(BF16)
        ap = ap[:, 0:n]
        if len(free_shape) > 1:
            names = [f"d{i}" for i in range(len(free_shape))]
            kw = {nm: int(free_shape[i]) for i, nm in enumerate(names) if i > 0}
            ap = ap.rearrange(f"p ({' '.join(names)}) -> p {' '.join(names)}", **kw)
        return Buf(ap, name)

    def ps(self):
        if not (self.ps_lo <= self.ps_rr < self.ps_hi):
            self.ps_rr = self.ps_lo
        b = self.psum[self.ps_rr]
        self.ps_rr += 1
        if self.ps_rr >= self.ps_hi:
            self.ps_rr = self.ps_lo
        return b

    def _preds(self, reads, writes):
        preds = set()
        for b in reads:
            if b.w is not None:
                preds.add(b.w)
        for b in writes:
            if b.w is not None:
                preds.add(b.w)
            preds.update(b.r)
        return preds

    def _track(self, idx, reads, writes):
        for b in reads:
            b.r.append(idx)
        for b in writes:
            b.w = idx
            b.r = []

    def op(self, e, fn, reads=(), writes=()):
        px = _CostProxy(e)
        fn(px)
        idx = len(self.nodes)
        self.nodes.append(Node("op", e, fn, self._preds(reads, writes), px.cost))
        self._track(idx, reads, writes)
        self.nops += 1
        return idx

    def dma(self, q, out, in_, reads=(), writes=()):
        idx = len(self.nodes)
        nbytes = out.free_size() * out.partition_size() * (2 if out.dtype == BF16 else 4)
        issue = 1500.0 if q == "pool" else 150.0
        lat = 2500.0 + nbytes / 100.0
        self.nodes.append(Node("dma", q, None, self._preds(reads, writes), issue, dma_args=(out, in_), lat=lat))
        self._track(idx, reads, writes)
        return idx

    def barrier(self):
        self.nodes.append(Node("barrier", None, None, set(), 0.0))

    def reset_arena(self, off):
        self.barrier()
        self.off = off

    def _sem(self, key):
        s = self.sems.get(key)
        if s is None:
            s = self.nc.alloc_semaphore(name=f"s_{key[0]}_{key[1]}")
            self.sems[key] = s
        return s

    def _wait(self, e, deps):
        kn = self.known[e]
        for key, val in deps.items():
            if kn.get(key, 0) >= val:
                continue
            self.E[e].wait_ge(self._sem(key), val)
            self.log[e].append(("w", key, val))
            kn[key] = val

    def _dep_events(self, nd, skip_pe=False):
        deps = {}
        for p in nd.preds:
            ev = self.nodes[p].ev
            if skip_pe and ev[0][0] == "pe":
                continue
            if deps.get(ev[0], 0) < ev[1]:
                deps[ev[0]] = ev[1]
        return deps

    def _emit(self, nd):
        if nd.kind == "op":
            e = nd.eng
            self._wait(e, self._dep_events(nd, skip_pe=(e == "pe")))
            ins = nd.fn(self.E[e])
            self.tick[e] += 1
            t = self.tick[e]
            ep = (t - 1) // EPOCH
            key = (e, ep)
            nd.ev = (key, t - ep * EPOCH)
            ins.then_inc(self._sem(key), 1)
            self.log[e].append(("i", key, 1))
        elif nd.kind == "dma":
            q = nd.eng
            j = self.dma_rr
            self.dma_rr = (j + 1) % self.ndma
            deps = self._dep_events(nd)
            key = ("dma", j)
            if self.dma_cnt[j] > 0:
                deps[key] = max(deps.get(key, 0), 16 * self.dma_cnt[j])
            self._wait(q, deps)
            out, in_ = nd.dma_args
            ins = self.E[q].dma_start(out=out, in_=in_)
            self.dma_cnt[j] += 1
            nd.ev = (key, 16 * self.dma_cnt[j])
            ins.then_inc(self._sem(key), 16)
            self.log[q].append(("i", key, 16))
        else:
            deps = self.all_events()
            for e in self.E:
                self._wait(e, dict(deps))

    def all_events(self):
        deps = {}
        for e, t in self.tick.items():
            if t > 0:
                ep = (t - 1) // EPOCH
                deps[(e, ep)] = t - ep * EPOCH
        for j, c in enumerate(self.dma_cnt):
            if c > 0:
                deps[("dma", j)] = 16 * c
        return deps

    def finish(self):
        nodes = self.nodes
        N = len(nodes)
        emitted = [False] * N
        finish = [0.0] * N
        eng_free = dict.fromkeys(self.E, 0.0)
        HOP = 350.0
        pos = 0
        W = self.WINDOW
        tbar = 0.0
        while pos < N:
            best = None
            bkey = None
            hi = min(N, pos + W)
            for i in range(pos, hi):
                if emitted[i]:
                    continue
                nd = nodes[i]
                if nd.kind == "barrier":
                    if i == pos:
                        best = i
                    break
                ok = True
                tr = tbar
                for p in nd.preds:
                    if not emitted[p]:
                        ok = False
                        break
                    f = finish[p] + HOP
                    if f > tr:
                        tr = f
                if not ok:
                    continue
                st = eng_free[nd.eng]
                if tr > st:
                    st = tr
                key = (st, i)
                if bkey is None or key < bkey:
                    bkey = key
                    best = i
                if st <= eng_free[nd.eng] and i == pos:
                    break
            nd = nodes[best]
            self._emit(nd)
            emitted[best] = True
            if nd.kind == "barrier":
                tbar = max(max(eng_free.values()), max(finish[max(0, best - 4000):best + 1] or [0.0]))
                for e in eng_free:
                    eng_free[e] = tbar
                finish[best] = tbar
            else:
                st = bkey[0]
                eng_free[nd.eng] = st + nd.cost
                finish[best] = st + nd.cost + nd.lat
            while pos < N and emitted[pos]:
                pos += 1
        self._wait("sp", self.all_events())


def build(cfg):
    kb = KB()
    nc = kb.nc
    phases = cfg["phases"]
    IN = {}

    def din(name, shape, dt=F32):
        IN[name] = nc.dram_tensor(name, list(shape), dt, kind="ExternalInput").ap()
        return IN[name]

    xT_in = din("xT", [D, NTOK])
    cT = din("cT", [128, KC, 2])
    mod_w = din("mod_w", [DEPTH, D, 6 * D])
    mod_b = din("mod_b", [128, DEPTH, 48])
    norm_g = din("norm_g", [128, DEPTH, 2, KC])
    ffn_w_gu = din("ffn_w_gu", [DEPTH, D, 2 * DFF])
    ffn_w_down = din("ffn_w_down", [DEPTH, DFF, D])
    lru_w_in = din("lru_w_in", [2, D, 2 * DRNN])
    lru_conv = din("lru_conv", [128, 2, NRC, 5])
    lru_gate_w = din("lru_gate_w", [2, 2, 2, NRC, 128, 128])
    lru_gvec = din("lru_gvec", [128, 2, 2, NRC, 3])
    lru_w_out = din("lru_w_out", [2, DRNN, D])
    hg_w_in = din("hg_w_in", [1, D, 5 * D])
    hg_lb = din("hg_lb", [128, KC, DEPTH])
    hg_ng = din("hg_ng", [128, 1])
    hg_w_o = din("hg_w_o", [1, D, D])
    hgc = din("hgc", [128, 2, 128])
    na_w_qkv = din("na_w_qkv", [1, D, 3 * D])
    na_w_o = din("na_w_o", [1, D, D])
    na_qkg = din("na_qkg", [128, 2])
    na_rp = din("na_rp", [16, 64, 15, 64])
    na_cmask = din("na_cmask", [128, 15, 64])
    nac = din("nac", [128, 3, 128])
    consts = din("consts", [128, 512])
    outT = nc.dram_tensor("outT", [D, SEQ], F32, kind="ExternalOutput").ap()
    XT = nc.dram_tensor("XTs", [D, NTOK], F32, kind="Internal").ap()
    MT = nc.dram_tensor("MTs", [DRNN, NTOK], BF16, kind="Internal").ap()

    XTv = XT.rearrange("(kc p) t -> p kc t", p=128)
    XTb = [[Buf(None, f"XT{i}_{k}") for k in range(KC)] for i in range(len(TT))]
    MTb = [Buf(None, f"MT{i}") for i in range(NRC)]
    last_layer = max(l for _, l in phases)

    ident_f = kb.alloc("ident_f", [128], F32)
    ones_f = kb.alloc("ones_f", [128], F32)
    ident_b = kb.alloc("ident_b", [128], BF16)
    ones_b = kb.alloc("ones_b", [128], BF16)
    MOD = kb.alloc("MOD", [DEPTH, 48, 2], F32)
    NG = kb.alloc("NG", [DEPTH, 2, KC], F32)
    GS = kb.alloc("GS", [DEPTH, 2, KC, 2], F32)
    kb.dma("sp", ident_f.ap, consts[:, 0:128], writes=[ident_f])
    kb.dma("sp", ones_f.ap, consts[:, 128:256], writes=[ones_f])
    kb.dma("sp", NG.ap, norm_g, writes=[NG])
    kb.op("dve", lambda v: v.tensor_copy(out=ident_b.ap, in_=ident_f.ap), [ident_f], [ident_b])
    kb.op("dve", lambda v: v.tensor_copy(out=ones_b.ap, in_=ones_f.ap), [ones_f], [ones_b])
    epsb = kb.alloc("epsb", [1], F32)
    kb.op("pool", lambda g: g.memset(epsb.ap, EPS), [], [epsb])
    base_off = kb.off

    for ti, (t0, n) in enumerate(TT):
        kb.dma("sp", XT[:, t0:t0 + n], xT_in[:, t0:t0 + n], writes=XTb[ti])
    c_raw = kb.alloc("c_raw", [KC, 2], F32)
    c_act = kb.alloc("c_act", [KC, 2], F32)
    mb = kb.alloc("mb", [DEPTH, 48], F32)
    kb.dma("sp", c_raw.ap, cT, writes=[c_raw])
    kb.dma("sp", mb.ap, mod_b, writes=[mb])
    kb.op("act", lambda a: a.activation(out=c_act.ap, in_=c_raw.ap, func=AF.Silu), [c_raw], [c_act])
    layers_needed = sorted(set(l for _, l in phases))
    mw = [kb.alloc(f"mw{i}", [KC, 1024], F32) for i in range(2)]
    cnt = 0
    for l in layers_needed:
        for v in range(6):
            w = mw[cnt % 2]
            cnt += 1
            kb.dma("sp", w.ap, mod_w[l].rearrange("(kc p) f -> p kc f", p=128)[:, :, v * 1024:(v + 1) * 1024],
                   writes=[w])
            for mc in range(8):
                p = kb.ps()

                def mm(pe, w=w, mc=mc, p=p):
                    for kc in range(KC):
                        ins = pe.matmul(p.ap[:, 0:2], lhsT=w.ap[:, kc, mc * 128:(mc + 1) * 128],
                                        rhs=c_act.ap[:, kc, :], start=(kc == 0), stop=(kc == KC - 1))
                    return ins
                kb.op("pe", mm, [w, c_act], [p])
                ch = v * 8 + mc
                kb.op("dve", lambda d, p=p, l=l, ch=ch: d.tensor_scalar(
                    out=MOD.ap[:, l, ch, :], in0=p.ap[:, 0:2], scalar1=mb.ap[:, l, ch:ch + 1], scalar2=None,
                    op0=ALU.add), [p, mb], [MOD])
        for which, vs in ((0, 1), (1, 4)):
            for kc in range(KC):
                kb.op("dve", lambda d, l=l, which=which, vs=vs, kc=kc: d.tensor_scalar(
                    out=GS.ap[:, l, which, kc, :], in0=MOD.ap[:, l, vs * 8 + kc, :], scalar1=1.0,
                    scalar2=NG.ap[:, l, which, kc:kc + 1], op0=ALU.add, op1=ALU.mult), [MOD, NG], [GS])

    def modv(l, v, kc, ty):
        return MOD.ap[:, l, v * 8 + kc, ty:ty + 1]

    def norm_tile(l, which, ti, xs, sq, rs, out_fn, tmp):
        t0, n = TT[ti]
        ty = 1 if ti == 0 else 0
        kb.dma("sp", xs.ap[:, :, 0:n], XTv[:, :, t0:t0 + n], reads=XTb[ti], writes=[xs])
        kb.op("act", lambda a: a.activation(out=sq.ap[:, :, 0:n], in_=xs.ap[:, :, 0:n], func=AF.Square), [xs], [sq])
        p = kb.ps()

        def mm(pe):
            for kc in range(KC):
                ins = pe.matmul(p.ap[:, 0:n], lhsT=ones_b.ap, rhs=sq.ap[:, kc, 0:n], start=(kc == 0), stop=(kc == KC - 1))
            return ins
        kb.op("pe", mm, [ones_b, sq], [p])
        kb.op("act", lambda a: a.activation(out=rs.ap[:, 0:n], in_=p.ap[:, 0:n], func=AF.Sqrt, scale=1.0 / D, bias=EPS_AP()),
              [p, epsb], [rs])
        kb.op("dve", lambda d: d.reciprocal(out=rs.ap[:, 0:n], in_=rs.ap[:, 0:n]), [rs], [rs])
        sh_v = 0 if which == 0 else 3
        for kc in range(KC):
            tb = tmp[kc % len(tmp)]
            kb.op("dve", lambda d, kc=kc, tb=tb: d.tensor_tensor(out=tb.ap[:, 0:n], in0=xs.ap[:, kc, 0:n], in1=rs.ap[:, 0:n],
                                                                op=ALU.mult), [xs, rs], [tb])
            oap, ob = out_fn(kc)
            kb.op("act", lambda a, kc=kc, tb=tb, oap=oap: a.activation(
                out=oap, in_=tb.ap[:, 0:n], func=AF.Identity, scale=GS.ap[:, l, which, kc, ty:ty + 1],
                bias=modv(l, sh_v, kc, ty)), [tb, GS, MOD], [ob])

    def EPS_AP():
        return epsb.ap[:, 0:1]

    def load_w(dst, dst_ap, src_ap):
        kb.dma("pool", dst_ap, src_ap, writes=[dst])

    def resid(l, gate_v, ti, mc, ypsum, xres):
        t0, n = TT[ti]
        ty = 1 if ti == 0 else 0
        kb.dma("sp", xres.ap[:, 0:n], XT[mc * 128:(mc + 1) * 128, t0:t0 + n], reads=[XTb[ti][mc]], writes=[xres])
        kb.op("dve", lambda d: d.scalar_tensor_tensor(out=xres.ap[:, 0:n], in0=ypsum.ap[:, 0:n], scalar=modv(l, gate_v, mc, ty),
                                                      in1=xres.ap[:, 0:n], op0=ALU.mult, op1=ALU.add), [ypsum, xres, MOD], [xres])
        kb.dma("sp", XT[mc * 128:(mc + 1) * 128, t0:t0 + n], xres.ap[:, 0:n], reads=[xres], writes=[XTb[ti][mc]])

    def ffn_phase(l):
        kb.reset_arena(base_off)
        TMAX = 1536
        H2 = kb.alloc("H2", [KC, TMAX], BF16)
        G = kb.alloc("G", [NFF, TMAX], BF16)
        xs = kb.alloc("xs", [KC, 512], F32)
        sq = kb.alloc("sq", [KC, 512], BF16)
        rs = kb.alloc("rs", [512], F32)
        tmp = [kb.alloc(f"tmp{i}", [512], F32) for i in range(2)]
        wgu = [kb.alloc(f"wgu{i}", [KC, 2, 256], BF16) for i in range(2)]
        wdn = [kb.alloc(f"wdn{i}", [NFF, 256], BF16) for i in range(2)]
        xres = [kb.alloc(f"xres{i}", [512], F32) for i in range(3)]
        sil = [kb.alloc(f"sil{i}", [512], F32) for i in range(2)]
        wguv = ffn_w_gu[l].rearrange("(kc p) (two f) -> p kc two f", p=128, two=2)
        wdnv = ffn_w_down[l].rearrange("(kc p) f -> p kc f", p=128)
        sts = [[0, 1, 2], [3, 4, 5], [6, 7, 8]]
        ngu = 0
        ndn = 0
        nxr = 0
        for st in sts:
            if l == DEPTH - 1 and False:
                pass
            offs = {}
            o = 0
            for ti in st:
                offs[ti] = o
                o += TT[ti][1]
            for ti in st:
                n = TT[ti][1]
                norm_tile(l, 1, ti, xs, sq, rs, lambda kc, ti=ti, n=n: (H2.ap[:, kc, offs[ti]:offs[ti] + n], H2), tmp)
            for jg in range(NFF // 2):
                w = wgu[ngu % 2]
                ngu += 1
                for two in range(2):
                    load_w(w, w.ap[:, :, two, :], wguv[:, :, two, jg * 256:(jg + 1) * 256])
                for jj in range(2):
                    j = jg * 2 + jj
                    for ti in st:
                        n = TT[ti][1]
                        pa = kb.ps()
                        pb = kb.ps()

                        def mm(pe, w=w, jj=jj, ti=ti, n=n, pa=pa, pb=pb, offs=offs):
                            for two, p in ((0, pa), (1, pb)):
                                for kc in range(KC):
                                    ins = pe.matmul(p.ap[:, 0:n], lhsT=w.ap[:, kc, two, jj * 128:(jj + 1) * 128],
                                                    rhs=H2.ap[:, kc, offs[ti]:offs[ti] + n], start=(kc == 0), stop=(kc == KC - 1))
                            return ins
                        kb.op("pe", mm, [w, H2], [pa, pb])
                        s = sil[(j + ti) % 2]
                        kb.op("act", lambda a, pa=pa, s=s, n=n: a.activation(out=s.ap[:, 0:n], in_=pa.ap[:, 0:n], func=AF.Silu), [pa], [s])
                        kb.op("dve", lambda d, pb=pb, s=s, n=n, j=j, ti=ti, offs=offs: d.tensor_tensor(
                            out=G.ap[:, j, offs[ti]:offs[ti] + n], in0=s.ap[:, 0:n], in1=pb.ap[:, 0:n], op=ALU.mult), [s, pb], [G])
            for mg in range(4):
                w = wdn[ndn % 2]
                ndn += 1
                load_w(w, w.ap, wdnv[:, :, mg * 256:(mg + 1) * 256])
                for mm_ in range(2):
                    mc = mg * 2 + mm_
                    for ti in st:
                        n = TT[ti][1]
                        p = kb.ps()

                        def mm(pe, w=w, mm_=mm_, ti=ti, n=n, p=p, offs=offs):
                            for k in range(NFF):
                                ins = pe.matmul(p.ap[:, 0:n], lhsT=w.ap[:, k, mm_ * 128:(mm_ + 1) * 128],
                                                rhs=G.ap[:, k, offs[ti]:offs[ti] + n], start=(k == 0), stop=(k == NFF - 1))
                            return ins
                        kb.op("pe", mm, [w, G], [p])
                        xr = xres[nxr % 3]
                        nxr += 1
                        resid(l, 5, ti, mc, p, xr)

    def rglru_phase(l):
        slot = l // 3
        kb.reset_arena(base_off)
        HT = [kb.alloc(f"HT{ti}", [KC, TT[ti][1]], BF16) for ti in range(len(TT))]
        o1 = kb.off
        xs = kb.alloc("xs", [KC, 512], F32)
        sq = kb.alloc("sq", [KC, 512], BF16)
        rs = kb.alloc("rs", [512], F32)
        tmp = [kb.alloc(f"tmp{i}", [512], F32) for i in range(2)]
        for ti in range(len(TT)):
            n = TT[ti][1]
            norm_tile(l, 0, ti, xs, sq, rs, lambda kc, ti=ti, n=n: (HT[ti].ap[:, kc, 0:n], HT[ti]), tmp)
        kb.reset_arena(o1)
        U = kb.alloc("U", [NTOK], F32)
        A = kb.alloc("A", [NTOK], F32)
        T1 = kb.alloc("T1", [NTOK], F32)
        T2 = kb.alloc("T2", [NTOK], F32)
        H0 = kb.alloc("H0", [NTOK], F32)
        Ub = kb.alloc("Ub", [NTOK], BF16)
        win = [kb.alloc(f"win{i}", [KC, 2, 128], BF16) for i in range(2)]
        gw = [kb.alloc(f"gw{i}", [2, 2, 128], BF16) for i in range(2)]
        cv = kb.alloc("cv", [2, NRC, 5], F32)
        gv = kb.alloc("gv", [2, 2, NRC, 3], F32)
        cexp = kb.alloc("cexp", [2, NRC], F32)
        gtmp = [kb.alloc(f"gtmp{i}", [512], F32) for i in range(2)]
        mt = [kb.alloc(f"mt{i}", [512], BF16) for i in range(2)]
        kb.dma("sp", cv.ap, lru_conv, writes=[cv])
        kb.dma("sp", gv.ap, lru_gvec, writes=[gv])
        for d_ in range(2):
            kb.op("act", lambda a, d_=d_: a.activation(out=cexp.ap[:, d_, :], in_=gv.ap[:, slot, d_, :, 2], func=AF.Exp, scale=-1.0), [gv], [cexp])
            kb.op("act", lambda a, d_=d_: a.activation(out=cexp.ap[:, d_, :], in_=cexp.ap[:, d_, :], func=AF.Ln, bias=1.0), [cexp], [cexp])
            kb.op("dve", lambda v, d_=d_: v.tensor_scalar(out=cexp.ap[:, d_, :], in0=cexp.ap[:, d_, :], scalar1=-8.0, scalar2=None, op0=ALU.mult), [cexp], [cexp])
        winv = lru_w_in[slot].rearrange("(kc p) (two f) -> p kc two f", p=128, two=2)
        segs = [(0, NCTX), (NCTX, NTOK)]
        for n_ in range(NRC):
            w = win[n_ % 2]
            g = gw[n_ % 2]
            for two in range(2):
                load_w(w, w.ap[:, :, two, :], winv[:, :, two, n_ * 128:(n_ + 1) * 128])
                load_w(g, g.ap[:, two, :, :], lru_gate_w[slot, two, :, n_].rearrange("g j k -> j g k"))
            for ti, (t0, n) in enumerate(TT):
                p = kb.ps()

                def mm(pe, w=w, ti=ti, n=n, p=p):
                    for kc in range(KC):
                        ins = pe.matmul(p.ap[:, 0:n], lhsT=w.ap[:, kc, 1, :], rhs=HT[ti].ap[:, kc, 0:n], start=(kc == 0), stop=(kc == KC - 1))
                    return ins
                kb.op("pe", mm, [w, HT[ti]], [p])
                kb.op("act", lambda a, p=p, t0=t0, n=n: a.activation(out=A.ap[:, t0:t0 + n], in_=p.ap[:, 0:n], func=AF.Copy), [p], [A])
            kb.op("dve", lambda v, n_=n_: v.tensor_scalar(out=U.ap, in0=A.ap, scalar1=cv.ap[:, slot, n_, 2:3], scalar2=cv.ap[:, slot, n_, 4:5],
                                                          op0=ALU.mult, op1=ALU.add), [A, cv], [U])
            for j in (0, 1, 3):
                s = j - 2
                for (a0, a1) in segs:
                    lo = max(a0, a0 - s)
                    hi = min(a1, a1 - s)
                    kb.op("dve", lambda v, n_=n_, j=j, s=s, lo=lo, hi=hi: v.scalar_tensor_tensor(
                        out=U.ap[:, lo:hi], in0=A.ap[:, lo + s:hi + s], scalar=cv.ap[:, slot, n_, j:j + 1], in1=U.ap[:, lo:hi],
                        op0=ALU.mult, op1=ALU.add), [A, U, cv], [U])
            kb.op("act", lambda a: a.activation(out=Ub.ap, in_=U.ap, func=AF.Copy), [U], [Ub])
            for d_ in range(2):
                for ti, (t0, n) in enumerate(TT):
                    pr = kb.ps()
                    pi = kb.ps()

                    def mm(pe, g=g, d_=d_, t0=t0, n=n, pr=pr, pi=pi):
                        pe.matmul(pr.ap[:, 0:n], lhsT=g.ap[:, d_, 0, :], rhs=Ub.ap[:, t0:t0 + n], start=True, stop=True)
                        return pe.matmul(pi.ap[:, 0:n], lhsT=g.ap[:, d_, 1, :], rhs=Ub.ap[:, t0:t0 + n], start=True, stop=True)
                    kb.op("pe", mm, [g, Ub], [pr, pi])
                    kb.op("act", lambda a, pr=pr, t0=t0, n=n, d_=d_, n_=n_: a.activation(
                        out=A.ap[:, t0:t0 + n], in_=pr.ap[:, 0:n], func=AF.Sigmoid, bias=gv.ap[:, slot, d_, n_, 0:1]), [pr, gv], [A])
                    kb.op("act", lambda a, pi=pi, t0=t0, n=n, d_=d_, n_=n_: a.activation(
                        out=T2.ap[:, t0:t0 + n], in_=pi.ap[:, 0:n], func=AF.Sigmoid, bias=gv.ap[:, slot, d_, n_, 1:2]), [pi, gv], [T2])
                kb.op("act", lambda a, d_=d_, n_=n_: a.activation(out=A.ap, in_=A.ap, func=AF.Exp, scale=cexp.ap[:, d_, n_:n_ + 1]), [A, cexp], [A])
                kb.op("dve", lambda v: v.tensor_tensor(out=T1.ap, in0=A.ap, in1=A.ap, op=ALU.mult), [A], [T1])
                kb.op("act", lambda a: a.activation(out=T1.ap, in_=T1.ap, func=AF.Sqrt, scale=-1.0, bias=1.0), [T1], [T1])
                kb.op("pool", lambda v: v.tensor_tensor(out=T2.ap, in0=T2.ap, in1=U.ap, op=ALU.mult), [T2, U], [T2])
                kb.op("dve", lambda v: v.tensor_tensor(out=T2.ap, in0=T2.ap, in1=T1.ap, op=ALU.mult), [T2, T1], [T2])
                if d_ == 0:
                    kb.op("dve", lambda v: v.tensor_tensor_scan(out=H0.ap, data0=A.ap, data1=T2.ap, initial=0.0, op0=ALU.mult, op1=ALU.add),
                          [A, T2], [H0])
                else:
                    kb.op("dve", lambda v: v.tensor_tensor_scan(out=T1.ap[:, NCTX - 1::-1] if False else T1.ap[:, 0:NCTX][:, ::-1],
                                                                data0=A.ap[:, 0:NCTX][:, ::-1], data1=T2.ap[:, 0:NCTX][:, ::-1],
                                                                initial=0.0, op0=ALU.mult, op1=ALU.add), [A, T2], [T1])
                    kb.op("dve", lambda v: v.tensor_tensor_scan(out=T1.ap[:, NCTX:NTOK][:, ::-1], data0=A.ap[:, NCTX:NTOK][:, ::-1],
                                                                data1=T2.ap[:, NCTX:NTOK][:, ::-1], initial=T1.ap[:, 0:1],
                                                                op0=ALU.mult, op1=ALU.add), [A, T2, T1], [T1])
            kb.op("pool", lambda v: v.tensor_tensor(out=H0.ap, in0=H0.ap, in1=T1.ap, op=ALU.add), [H0, T1], [H0])
            for ti, (t0, n) in enumerate(TT):
                m = mt[ti % 2]
                p = kb.ps()

                def mm(pe, w=w, ti=ti, n=n, p=p):
                    for kc in range(KC):
                        ins = pe.matmul(p.ap[:, 0:n], lhsT=w.ap[:, kc, 0, :], rhs=HT[ti].ap[:, kc, 0:n], start=(kc == 0), stop=(kc == KC - 1))
                    return ins
                kb.op("pe", mm, [w, HT[ti]], [p])
                gt = gtmp[ti % 2]
                kb.op("act", lambda a, p=p, n=n, gt=gt: a.activation(out=gt.ap[:, 0:n], in_=p.ap[:, 0:n], func=AF.Gelu_apprx_tanh), [p], [gt])
                kb.op("dve", lambda v, gt=gt, t0=t0, n=n, m=m: v.tensor_tensor(out=m.ap[:, 0:n], in0=gt.ap[:, 0:n], in1=H0.ap[:, t0:t0 + n],
                                                                              op=ALU.mult), [gt, H0], [m])
                kb.dma("sp", MT[n_ * 128:(n_ + 1) * 128, t0:t0 + n], m.ap[:, 0:n], reads=[m], writes=[MTb[n_]])
        out_proj(l, lru_w_out[slot], NRC)

    def out_proj(l, w_dram, nk):
        kb.reset_arena(base_off)
        wo = kb.alloc("wo", [nk, D], BF16)
        load_w(wo, wo.ap, w_dram.rearrange("(n p) f -> p n f", p=128))
        mts = [kb.alloc(f"mts{i}", [nk, 512], BF16) for i in range(2)]
        xres = [kb.alloc(f"xres{i}", [512], F32) for i in range(3)]
        MTv = MT.rearrange("(n p) t -> p n t", p=128)
        nxr = 0
        for ti, (t0, n) in enumerate(TT):
            ms = mts[ti % 2]
            kb.dma("sp", ms.ap[:, :, 0:n], MTv[:, 0:nk, t0:t0 + n], reads=MTb[0:nk], writes=[ms])
            for mc in range(KC):
                p = kb.ps()

                def mm(pe, ms=ms, mc=mc, n=n, p=p):
                    for k in range(nk):
                        ins = pe.matmul(p.ap[:, 0:n], lhsT=wo.ap[:, k, mc * 128:(mc + 1) * 128], rhs=ms.ap[:, k, 0:n],
                                        start=(k == 0), stop=(k == nk - 1))
                    return ins
                kb.op("pe", mm, [wo, ms], [p])
                xr = xres[nxr % 3]
                nxr += 1
                resid(l, 2, ti, mc, p, xr)

    def norm_all(l):
        HT = [kb.alloc(f"HT{ti}", [KC, TT[ti][1]], BF16) for ti in range(len(TT))]
        o1 = kb.off
        xs = kb.alloc("xs", [KC, 512], F32)
        sq = kb.alloc("sq", [KC, 512], BF16)
        rs = kb.alloc("rs", [512], F32)
        tmp = [kb.alloc(f"tmp{i}", [512], F32) for i in range(2)]
        for ti in range(len(TT)):
            n = TT[ti][1]
            norm_tile(l, 0, ti, xs, sq, rs, lambda kc, ti=ti, n=n: (HT[ti].ap[:, kc, 0:n], HT[ti]), tmp)
        kb.reset_arena(o1)
        return HT

    def hgrn2_phase(l):
        slot = 0
        kb.reset_arena(base_off)
        HT = norm_all(l)
        NB = NTOK // 128
        HC = kb.alloc("HC", [2, 128], F32)
        kb.dma("sp", HC.ap, hgc, writes=[HC])
        maskb = kb.alloc("maskb", [2, 128], BF16)
        kb.op("dve", lambda v: v.tensor_copy(out=maskb.ap, in_=HC.ap), [HC], [maskb])
        ng = kb.alloc("hng", [1], F32)
        kb.dma("sp", ng.ap, hg_ng, writes=[ng])
        lbl = kb.alloc("lbl", [KC, DEPTH], F32)
        lbv = kb.alloc("lbv", [KC, 4], F32)
        kb.dma("sp", lbl.ap, hg_lb, writes=[lbl])
        kb.op("act", lambda a: a.activation(out=lbl.ap, in_=lbl.ap, func=AF.Exp), [lbl], [lbl])
        kb.op("dve", lambda v: v.tensor_tensor(out=lbv.ap[:, :, 2], in0=lbl.ap[:, :, 0], in1=lbl.ap[:, :, 1], op=ALU.add), [lbl], [lbv])
        kb.op("dve", lambda v: v.tensor_tensor(out=lbv.ap[:, :, 2], in0=lbv.ap[:, :, 2], in1=lbl.ap[:, :, 2], op=ALU.add), [lbl, lbv], [lbv])
        kb.op("dve", lambda v: v.tensor_tensor(out=lbv.ap[:, :, 2], in0=lbv.ap[:, :, 2], in1=lbl.ap[:, :, 3], op=ALU.add), [lbl, lbv], [lbv])
        kb.op("dve", lambda v: v.memset(lbv.ap[:, :, 3], 0.0), [lbv], [lbv])
        for j in range(1, l + 1):
            kb.op("dve", lambda v, j=j: v.tensor_tensor(out=lbv.ap[:, :, 3], in0=lbv.ap[:, :, 3], in1=lbl.ap[:, :, j], op=ALU.add), [lbl, lbv], [lbv])
        kb.op("dve", lambda v: v.reciprocal(out=lbv.ap[:, :, 2], in_=lbv.ap[:, :, 2]), [lbv], [lbv])
        kb.op("dve", lambda v: v.tensor_tensor(out=lbv.ap[:, :, 0], in0=lbv.ap[:, :, 3], in1=lbv.ap[:, :, 2], op=ALU.mult), [lbv], [lbv])
        kb.op("dve", lambda v: v.tensor_scalar(out=lbv.ap[:, :, 1], in0=lbv.ap[:, :, 0], scalar1=-1.0, scalar2=1.0, op0=ALU.mult, op1=ALU.add), [lbv], [lbv])

        whd = kb.alloc("whd", [KC, 5, 128], BF16)
        QT = kb.alloc("QT", [NTOK], BF16)
        Vtok = kb.alloc("Vtok", [NB, 128], BF16)
        O = kb.alloc("O", [NTOK], F32)
        KT = [kb.alloc(f"KT{d}", [NTOK], BF16) for d in range(2)]
        Bc = [kb.alloc(f"Bc{d}", [NTOK], F32) for d in range(2)]
        S = [kb.alloc(f"S{d}", [128], F32) for d in range(2)]
        ft = [kb.alloc(f"ft{i}", [512], F32) for i in range(2)]
        lt = [kb.alloc(f"lt{i}", [512], F32) for i in range(2)]
        E1 = [kb.alloc(f"E1_{i}", [128], F32) for i in range(2)]
        E2 = [kb.alloc(f"E2_{i}", [128], F32) for i in range(2)]
        EB = [kb.alloc(f"EB_{i}", [4], F32) for i in range(2)]
        Qt = [kb.alloc(f"Qt{i}", [128], BF16) for i in range(2)]
        Kt = [kb.alloc(f"Kt{i}", [128], BF16) for i in range(2)]
        KtT = [kb.alloc(f"KtT{i}", [128], BF16) for i in range(2)]
        Sp = [kb.alloc(f"Sp{i}", [128], BF16) for i in range(2)]
        Am = [kb.alloc(f"Am{i}", [128], BF16) for i in range(2)]
        tS = [kb.alloc(f"tS{i}", [128], F32) for i in range(2)]
        sgt = ft
        ogt = lt
        sqo = [kb.alloc(f"sqo{i}", [512], BF16) for i in range(2)]
        rso = [kb.alloc(f"rso{i}", [512], F32) for i in range(2)]
        ogb = [kb.alloc(f"ogb{i}", [512], BF16) for i in range(2)]
        wv = hg_w_in[slot].rearrange("(kc p) (v f) -> p kc v f", p=128, v=5)
        order = [list(range(NB)), [1, 0] + list(range(NB - 1, 1, -1))]
        for hd in range(8):
            for v_ in range(5):
                load_w(whd, whd.ap[:, :, v_, :], wv[:, :, v_, hd * 128:(hd + 1) * 128])
            kb.op("pool", lambda g: g.memset(O.ap, 0.0), [], [O])
            for d in range(2):
                kb.op("pool", lambda g, d=d: g.memset(S[d].ap, 0.0), [], [S[d]])
            for ti, (t0, n) in enumerate(TT):
                def proj(v_, ti=ti, n=n):
                    p = kb.ps()

                    def mm(pe):
                        for kc in range(KC):
                            ins = pe.matmul(p.ap[:, 0:n], lhsT=whd.ap[:, kc, v_, :], rhs=HT[ti].ap[:, kc, 0:n], start=(kc == 0), stop=(kc == KC - 1))
                        return ins
                    kb.op("pe", mm, [whd, HT[ti]], [p])
                    return p
                p = proj(0)
                kb.op("act", lambda a, p=p, t0=t0, n=n: a.activation(out=QT.ap[:, t0:t0 + n], in_=p.ap[:, 0:n], func=AF.Silu), [p], [QT])
                for d in range(2):
                    p = proj(3 + d)
                    f_ = ft[d]
                    l_ = lt[d]
                    kb.op("act", lambda a, p=p, n=n, f_=f_: a.activation(out=f_.ap[:, 0:n], in_=p.ap[:, 0:n], func=AF.Sigmoid), [p], [f_])
                    kb.op("dve", lambda v, n=n, f_=f_, hd=hd: v.tensor_scalar(out=f_.ap[:, 0:n], in0=f_.ap[:, 0:n], scalar1=lbv.ap[:, hd, 1:2],
                                                                            scalar2=lbv.ap[:, hd, 0:1], op0=ALU.mult, op1=ALU.add), [f_, lbv], [f_])
                    kb.op("act", lambda a, n=n, f_=f_, l_=l_: a.activation(out=l_.ap[:, 0:n], in_=f_.ap[:, 0:n], func=AF.Ln), [f_], [l_])
                    kb.op("dve", lambda v, n=n, f_=f_, d=d, t0=t0: v.tensor_scalar(out=KT[d].ap[:, t0:t0 + n], in0=f_.ap[:, 0:n], scalar1=-1.0, scalar2=1.0,
                                                                                  op0=ALU.mult, op1=ALU.add), [f_], [KT[d]])
                    for bi in range(n // 128):
                        c0 = t0 + bi * 128
                        if d == 0:
                            kb.op("dve", lambda v, l_=l_, bi=bi, c0=c0: v.tensor_tensor_scan(
                                out=Bc[0].ap[:, c0:c0 + 128], data0=ones_f.ap, data1=l_.ap[:, bi * 128:(bi + 1) * 128], initial=0.0,
                                op0=ALU.mult, op1=ALU.add), [ones_f, l_], [Bc[0]])
                        else:
                            kb.op("dve", lambda v, l_=l_, bi=bi, c0=c0: v.tensor_tensor_scan(
                                out=Bc[1].ap[:, c0:c0 + 128][:, ::-1], data0=ones_f.ap, data1=l_.ap[:, bi * 128:(bi + 1) * 128][:, ::-1], initial=0.0,
                                op0=ALU.mult, op1=ALU.add), [ones_f, l_], [Bc[1]])
                for bi in range(n // 128):
                    blk = t0 // 128 + bi
                    p = kb.ps()

                    def mmv(pe, ti=ti, bi=bi, p=p):
                        for kc in range(KC):
                            ins = pe.matmul(p.ap[:, 0:128], lhsT=HT[ti].ap[:, kc, bi * 128:(bi + 1) * 128], rhs=whd.ap[:, kc, 1, :],
                                            start=(kc == 0), stop=(kc == KC - 1))
                        return ins
                    kb.op("pe", mmv, [whd, HT[ti]], [p])
                    kb.op("act", lambda a, p=p, blk=blk: a.activation(out=Vtok.ap[:, blk, :], in_=p.ap[:, 0:128], func=AF.Copy), [p], [Vtok])
            for j in range(NB):
                for d in range(2):
                    blk = order[d][j]
                    i2 = d
                    c0 = blk * 128
                    tsl = slice(c0, c0 + 128)
                    midc = c0 + (63 if d == 0 else 64)
                    endc = c0 + (127 if d == 0 else 0)
                    epos = 127 if d == 0 else 0
                    kb.op("dve", lambda v, i2=i2, d=d, midc=midc: v.tensor_scalar(out=EB[i2].ap[:, 2:3], in0=Bc[d].ap[:, midc:midc + 1], scalar1=-1.0, scalar2=None,
                                                                                 op0=ALU.mult), [Bc[d]], [EB[i2]])
                    kb.op("act", lambda a, i2=i2, d=d, tsl=tsl: a.activation(out=E1[i2].ap, in_=Bc[d].ap[:, tsl], func=AF.Exp, bias=EB[i2].ap[:, 2:3]), [Bc[d], EB[i2]], [E1[i2]])
                    kb.op("act", lambda a, i2=i2, d=d, tsl=tsl, midc=midc: a.activation(out=E2[i2].ap, in_=Bc[d].ap[:, tsl], func=AF.Exp, scale=-1.0,
                                                                                       bias=Bc[d].ap[:, midc:midc + 1]), [Bc[d]], [E2[i2]])
                    kb.op("act", lambda a, i2=i2, d=d, endc=endc: a.activation(out=EB[i2].ap[:, 0:1], in_=Bc[d].ap[:, endc:endc + 1], func=AF.Exp), [Bc[d]], [EB[i2]])
                    kb.op("act", lambda a, i2=i2, d=d, midc=midc: a.activation(out=EB[i2].ap[:, 1:2], in_=Bc[d].ap[:, midc:midc + 1], func=AF.Exp), [Bc[d]], [EB[i2]])
                    kb.op("dve", lambda v, i2=i2, tsl=tsl: v.tensor_tensor(out=Qt[i2].ap, in0=QT.ap[:, tsl], in1=E1[i2].ap, op=ALU.mult), [QT, E1[i2]], [Qt[i2]])
                    kb.op("dve", lambda v, i2=i2, tsl=tsl, d=d: v.tensor_tensor(out=Kt[i2].ap, in0=KT[d].ap[:, tsl], in1=E2[i2].ap, op=ALU.mult), [KT[d], E2[i2]], [Kt[i2]])
                    kb.op("dve", lambda v, i2=i2, d=d: v.tensor_scalar(out=Sp[i2].ap, in0=S[d].ap, scalar1=EB[i2].ap[:, 1:2], scalar2=None, op0=ALU.mult), [S[d], EB[i2]], [Sp[i2]])
                    pK = kb.ps()
                    pKb = pK.ap.bitcast(BF16)
                    kb.op("pe", lambda pe, pKb=pKb, i2=i2: pe.transpose(pKb[:, 0:128], Kt[i2].ap, ident_b.ap), [Kt[i2], ident_b], [pK])
                    kb.op("act", lambda a, pKb=pKb, i2=i2: a.activation(out=KtT[i2].ap, in_=pKb[:, 0:128], func=AF.Copy), [pK], [KtT[i2]])
                    pA = kb.ps()
                    kb.op("pe", lambda pe, pA=pA, i2=i2: pe.matmul(pA.ap[:, 0:128], lhsT=Kt[i2].ap, rhs=Qt[i2].ap, start=True, stop=True), [Kt[i2], Qt[i2]], [pA])
                    kb.op("dve", lambda v, pA=pA, i2=i2: v.tensor_scalar(out=tS[i2].ap, in0=pA.ap[:, 0:128], scalar1=1e30, scalar2=-1e30, op0=ALU.min, op1=ALU.max), [pA], [tS[i2]])
                    kb.op("dve", lambda v, i2=i2, d=d: v.tensor_tensor(out=Am[i2].ap, in0=tS[i2].ap, in1=maskb.ap[:, d, :], op=ALU.mult), [tS[i2], maskb], [Am[i2]])
                    pO = kb.ps()

                    def mmo(pe, pO=pO, i2=i2, blk=blk):
                        pe.matmul(pO.ap[:, 0:128], lhsT=Sp[i2].ap, rhs=Qt[i2].ap, start=True, stop=False)
                        return pe.matmul(pO.ap[:, 0:128], lhsT=Vtok.ap[:, blk, :], rhs=Am[i2].ap, start=False, stop=True)
                    kb.op("pe", mmo, [Sp[i2], Qt[i2], Vtok, Am[i2]], [pO])
                    kb.op("dve", lambda v, pO=pO, tsl=tsl: v.tensor_tensor(out=O.ap[:, tsl], in0=pO.ap[:, 0:128], in1=O.ap[:, tsl], op=ALU.add), [pO, O], [O])
                    pS = kb.ps()
                    kb.op("pe", lambda pe, pS=pS, i2=i2, blk=blk: pe.matmul(pS.ap[:, 0:128], lhsT=KtT[i2].ap, rhs=Vtok.ap[:, blk, :], start=True, stop=True), [KtT[i2], Vtok], [pS])
                    kb.op("dve", lambda v, pS=pS, i2=i2, epos=epos: v.tensor_scalar(out=tS[i2].ap, in0=pS.ap[:, 0:128], scalar1=E1[i2].ap[:, epos:epos + 1], scalar2=None,
                                                                                   op0=ALU.mult), [pS, E1[i2]], [tS[i2]])
                    kb.op("dve", lambda v, i2=i2, d=d: v.scalar_tensor_tensor(out=S[d].ap, in0=S[d].ap, scalar=EB[i2].ap[:, 0:1], in1=tS[i2].ap,
                                                                             op0=ALU.mult, op1=ALU.add), [S[d], EB[i2], tS[i2]], [S[d]])
            for ti, (t0, n) in enumerate(TT):
                i2 = ti % 2
                p = kb.ps()

                def mmg(pe, ti=ti, n=n, p=p):
                    for kc in range(KC):
                        ins = pe.matmul(p.ap[:, 0:n], lhsT=whd.ap[:, kc, 2, :], rhs=HT[ti].ap[:, kc, 0:n], start=(kc == 0), stop=(kc == KC - 1))
                    return ins
                kb.op("pe", mmg, [whd, HT[ti]], [p])
                kb.op("act", lambda a, p=p, n=n, i2=i2: a.activation(out=sgt[i2].ap[:, 0:n], in_=p.ap[:, 0:n], func=AF.Silu), [p], [sgt[i2]])
                kb.op("act", lambda a, t0=t0, n=n, i2=i2: a.activation(out=sqo[i2].ap[:, 0:n], in_=O.ap[:, t0:t0 + n], func=AF.Square), [O], [sqo[i2]])
                p2 = kb.ps()
                kb.op("pe", lambda pe, p2=p2, n=n, i2=i2: pe.matmul(p2.ap[:, 0:n], lhsT=ones_b.ap, rhs=sqo[i2].ap[:, 0:n], start=True, stop=True), [ones_b, sqo[i2]], [p2])
                kb.op("act", lambda a, p2=p2, n=n, i2=i2: a.activation(out=rso[i2].ap[:, 0:n], in_=p2.ap[:, 0:n], func=AF.Sqrt, scale=1.0 / 128, bias=EPS_AP()), [p2, epsb], [rso[i2]])
                kb.op("dve", lambda v, n=n, i2=i2: v.reciprocal(out=rso[i2].ap[:, 0:n], in_=rso[i2].ap[:, 0:n]), [rso[i2]], [rso[i2]])
                kb.op("dve", lambda v, t0=t0, n=n, i2=i2: v.scalar_tensor_tensor(out=ogt[i2].ap[:, 0:n], in0=O.ap[:, t0:t0 + n], scalar=ng.ap[:, 0:1], in1=rso[i2].ap[:, 0:n],
                                                                                op0=ALU.mult, op1=ALU.mult), [O, ng, rso[i2]], [ogt[i2]])
                kb.op("dve", lambda v, n=n, i2=i2: v.tensor_tensor(out=ogb[i2].ap[:, 0:n], in0=ogt[i2].ap[:, 0:n], in1=sgt[i2].ap[:, 0:n], op=ALU.mult), [ogt[i2], sgt[i2]], [ogb[i2]])
                kb.dma("sp", MT[hd * 128:(hd + 1) * 128, t0:t0 + n], ogb[i2].ap[:, 0:n], reads=[ogb[i2]], writes=[MTb[hd]])
        out_proj(l, hg_w_o[slot], 8)

    def na_phase(l):
        slot = 0
        kb.reset_arena(base_off)
        HT = norm_all(l)
        NB = NTOK // 128
        kb.ps_lo, kb.ps_hi = 0, 4
        ncf = kb.alloc("ncf", [3, 128], F32)
        kb.dma("sp", ncf.ap, nac, writes=[ncf])
        ncb = kb.alloc("ncb", [3, 128], BF16)
        kb.op("dve", lambda v: v.tensor_copy(out=ncb.ap, in_=ncf.ap), [ncf], [ncb])
        cmask = kb.alloc("cmask", [15, 64], F32)
        kb.dma("sp", cmask.ap, na_cmask, writes=[cmask])
        qkg = kb.alloc("qkg", [2], F32)
        kb.dma("sp", qkg.ap, na_qkg, writes=[qkg])
        wq = kb.alloc("wqkv", [KC, 3, 128], BF16)
        QK = [kb.alloc(f"QK{i}", [NTOK], BF16) for i in range(2)]
        VAB = [kb.alloc(f"VAB{i}", [NB, 128], BF16) for i in range(2)]
        for i in range(2):
            kb.op("pool", lambda g, i=i: g.memset(VAB[i].ap, 0.0), [], [VAB[i]])
        EBraw = [kb.alloc(f"EBraw{i}", [15, 64], F32) for i in range(2)]
        EBt = [kb.alloc(f"EBt{i}", [15, 64], BF16) for i in range(2)]
        NCH = (6, 4, 4)
        MK = [[kb.alloc(f"MK{h2}_{c}", [NCH[c] * 256], BF16) for c in range(3)] for h2 in range(2)]
        qf = [kb.alloc(f"qf{i}", [512], F32) for i in range(2)]
        sqn = [kb.alloc(f"sqn{i}", [512], BF16) for i in range(2)]
        rsn = [kb.alloc(f"rsn{i}", [512], F32) for i in range(2)]
        P2 = [kb.alloc(f"P2_{i}", [2, 256], BF16) for i in range(4)]
        rden = [kb.alloc(f"rden{i}", [256], F32) for i in range(2)]
        ob = [kb.alloc(f"ob{i}", [256], BF16) for i in range(2)]
        wv3 = na_w_qkv[slot].rearrange("(kc p) (v f) -> p kc v f", p=128, v=3)
        blocks = []
        for qb in range(16):
            r0 = 4 * qb
            if qb == 0:
                cls, kr0s = 1, [0, 2, 4, 6]
            elif qb == 15:
                cls, kr0s = 2, [56, 58, 60, 62]
            else:
                cls, kr0s = 0, [r0 - 4 + 2 * c for c in range(6)]
            keys = [(NCTX + kr0 * 64, ci) for ci, kr0 in enumerate(kr0s)] + [(0, None), (128, None)]
            blocks.append((NCTX + r0 * 64, cls, keys))
        blocks.append((0, None, [(0, None), (128, None)]))
        cls_def = {0: (8, [4, 6, 8, 10, 12, 14]), 1: (0, [0, 2, 4, 6]), 2: (60, [56, 58, 60, 62])}
        nP = 0
        nblk = 0
        for hp in range(8):
            for v_ in range(3):
                load_w(wq, wq.ap[:, :, v_, :], wv3[:, :, v_, hp * 128:(hp + 1) * 128])
            for ti, (t0, n) in enumerate(TT):
                for which in range(2):
                    p = kb.ps()

                    def mm(pe, p=p, which=which, ti=ti, n=n):
                        for kc in range(KC):
                            ins = pe.matmul(p.ap[:, 0:n], lhsT=wq.ap[:, kc, which, :], rhs=HT[ti].ap[:, kc, 0:n], start=(kc == 0), stop=(kc == KC - 1))
                        return ins
                    kb.op("pe", mm, [wq, HT[ti]], [p])
                    i2 = which
                    kb.op("act", lambda a, p=p, n=n, i2=i2: a.activation(out=qf[i2].ap[:, 0:n], in_=p.ap[:, 0:n], func=AF.Copy), [p], [qf[i2]])
                    kb.op("act", lambda a, n=n, i2=i2: a.activation(out=sqn[i2].ap[:, 0:n], in_=qf[i2].ap[:, 0:n], func=AF.Square), [qf[i2]], [sqn[i2]])
                    p2 = kb.ps()
                    kb.op("pe", lambda pe, p2=p2, n=n, i2=i2: pe.matmul(p2.ap[:, 0:n], lhsT=ncb.ap[:, 0, :], rhs=sqn[i2].ap[:, 0:n], start=True, stop=True), [ncb, sqn[i2]], [p2])
                    kb.op("act", lambda a, p2=p2, n=n, i2=i2: a.activation(out=rsn[i2].ap[:, 0:n], in_=p2.ap[:, 0:n], func=AF.Sqrt, scale=1.0 / 64, bias=EPS_AP()), [p2, epsb], [rsn[i2]])
                    kb.op("dve", lambda v, n=n, i2=i2: v.reciprocal(out=rsn[i2].ap[:, 0:n], in_=rsn[i2].ap[:, 0:n]), [rsn[i2]], [rsn[i2]])
                    kb.op("dve", lambda v, n=n, i2=i2, which=which, t0=t0: v.scalar_tensor_tensor(
                        out=QK[which].ap[:, t0:t0 + n], in0=qf[i2].ap[:, 0:n], scalar=qkg.ap[:, which:which + 1], in1=rsn[i2].ap[:, 0:n],
                        op0=ALU.mult, op1=ALU.mult), [qf[i2], qkg, rsn[i2]], [QK[which]])
                for bi in range(n // 128):
                    blk = t0 // 128 + bi
                    p = kb.ps()

                    def mmv(pe, ti=ti, bi=bi, p=p):
                        for kc in range(KC):
                            ins = pe.matmul(p.ap[:, 0:128], lhsT=HT[ti].ap[:, kc, bi * 128:(bi + 1) * 128], rhs=wq.ap[:, kc, 2, :],
                                            start=(kc == 0), stop=(kc == KC - 1))
                        return ins
                    kb.op("pe", mmv, [wq, HT[ti]], [p])
                    kb.op("act", lambda a, p=p, blk=blk: a.activation(out=VAB[0].ap[:, blk, 0:64], in_=p.ap[:, 0:64], func=AF.Copy), [p], [VAB[0]])
                    kb.op("dve", lambda v, p=p, blk=blk: v.tensor_copy(out=VAB[1].ap[:, blk, 64:128], in_=p.ap[:, 64:128]), [p], [VAB[1]])
            for h2 in range(2):
                h = 2 * hp + h2
                for half in range(2):
                    kb.dma("sp", EBraw[h2].ap[half * 64:(half + 1) * 64], na_rp[h], writes=[EBraw[h2]])
                kb.op("act", lambda a, h2=h2: a.activation(out=EBraw[h2].ap, in_=EBraw[h2].ap, func=AF.Exp), [EBraw[h2]], [EBraw[h2]])
                kb.op("dve", lambda v, h2=h2: v.tensor_tensor(out=EBt[h2].ap, in0=EBraw[h2].ap, in1=cmask.ap, op=ALU.mult), [EBraw[h2], cmask], [EBt[h2]])
                for cls in range(3):
                    r0, kr0s = cls_def[cls]
                    mk = MK[h2][cls]
                    kb.op("pool", lambda g, mk=mk: g.memset(mk.ap, 0.0), [], [mk])

                    def copies(g, mk=mk, r0=r0, kr0s=kr0s, cls=cls, h2=h2):
                        ins = None
                        for ci, kr0 in enumerate(kr0s):
                            for a_ in range(2):
                                for bq in range(4):
                                    dr = kr0 + a_ - r0 - bq
                                    if cls == 0 and not (-4 <= dr <= 3):
                                        continue
                                    assert -7 <= dr <= 7
                                    ins = g.tensor_copy(out=mk.ap[a_ * 64:(a_ + 1) * 64, ci * 256 + bq * 64:ci * 256 + (bq + 1) * 64],
                                                        in_=EBt[h2].ap[a_ * 64:(a_ + 1) * 64, dr + 7, :])
                        return ins
                    kb.op("pool", copies, [EBt[h2]], [mk])
            for (q0, cls, keys) in blocks:
                num = kb.psum[4 + 2 * (nblk % 2)]
                den = kb.psum[5 + 2 * (nblk % 2)]
                i2 = nblk % 2
                nblk += 1
                npairs = len(keys) // 2
                total = 2 * npairs
                cnt = 0
                for h2 in range(2):
                    pb = 64 * h2
                    for pi in range(npairs):
                        (k0, c0), (k1, c1) = keys[2 * pi], keys[2 * pi + 1]
                        pS = kb.ps()

                        def mms(pe, pS=pS, pb=pb, k0=k0, k1=k1, q0=q0):
                            pe.matmul(pS.ap[:, 0:256], lhsT=QK[1].ap[pb:pb + 64, k0:k0 + 128], rhs=QK[0].ap[pb:pb + 64, q0:q0 + 256], start=True, stop=True)
                            return pe.matmul(pS.ap[:, 256:512], lhsT=QK[1].ap[pb:pb + 64, k1:k1 + 128], rhs=QK[0].ap[pb:pb + 64, q0:q0 + 256], start=True, stop=True)
                        kb.op("pe", mms, [QK[0], QK[1]], [pS])
                        P = P2[nP % 4]
                        nP += 1
                        kb.op("act", lambda a, pS=pS, P=P: a.activation(out=P.ap.rearrange("p a b -> p (a b)"), in_=pS.ap, func=AF.Exp, scale=0.125), [pS], [P])
                        if c0 is not None:
                            mk = MK[h2][cls]
                            kb.op("pool", lambda g, P=P, mk=mk, c0=c0: g.tensor_tensor(out=P.ap.rearrange("p a b -> p (a b)"), in0=P.ap.rearrange("p a b -> p (a b)"),
                                                                                       in1=mk.ap[:, c0 * 256:c0 * 256 + 512], op=ALU.mult), [P, mk], [P])
                        first = (cnt == 0)
                        last = (cnt == total - 1)
                        cnt += 1

                        def mmpv(pe, P=P, h2=h2, k0=k0, k1=k1, first=first, last=last, num=num, den=den):
                            for c, kk in ((0, k0), (1, k1)):
                                st = first and c == 0
                                sp_ = last and c == 1
                                pe.matmul(num.ap[:, 0:256], lhsT=VAB[h2].ap[:, kk // 128, :], rhs=P.ap[:, c, :], start=st, stop=sp_)
                                ins = pe.matmul(den.ap[:, 0:256], lhsT=ncb.ap[:, 1 + h2, :], rhs=P.ap[:, c, :], start=st, stop=sp_)
                            return ins
                        kb.op("pe", mmpv, [P, VAB[h2], ncb], [num, den])
                kb.op("dve", lambda v, den=den, i2=i2: v.reciprocal(out=rden[i2].ap, in_=den.ap[:, 0:256]), [den], [rden[i2]])
                kb.op("dve", lambda v, num=num, i2=i2: v.tensor_tensor(out=ob[i2].ap, in0=num.ap[:, 0:256], in1=rden[i2].ap, op=ALU.mult), [num, rden[i2]], [ob[i2]])
                kb.dma("sp", MT[hp * 128:(hp + 1) * 128, q0:q0 + 256], ob[i2].ap, reads=[ob[i2]], writes=[MTb[hp]])
        kb.ps_lo, kb.ps_hi = 0, 8
        out_proj(l, na_w_o[slot], 8)

    for ph, l in phases:
        if ph == "ffn":
            ffn_phase(l)
        elif ph == "mix":
            kind = l % 3
            if kind == 0:
                rglru_phase(l)
            elif kind == 2:
                hgrn2_phase(l)
            else:
                na_phase(l)
    kb.barrier()
    for ti, (t0, n) in enumerate(TT):
        if ti == 0 and not cfg.get("dump_ctx"):
            continue
    if cfg.get("dump_all"):
        outA = nc.dram_tensor("outA", [D, NTOK], F32, kind="ExternalOutput").ap()
        for ti, (t0, n) in enumerate(TT):
            kb.dma("sp", outA[:, t0:t0 + n], XT[:, t0:t0 + n], reads=XTb[ti])
    for ti, (t0, n) in enumerate(TT):
        if ti == 0:
            continue
        kb.dma("sp", outT[:, t0 - NCTX:t0 - NCTX + n], XT[:, t0:t0 + n], reads=XTb[ti])
    kb.finish()
    return kb


def host_inputs(inp, b):
    f = np.float32
    m = {}
    m["xT"] = np.ascontiguousarray(np.concatenate([inp["ctx"][b], inp["x"][b]], axis=0).T.astype(f))
    cT = np.stack([inp["c"][b].reshape(KC, 128).T, inp["c_ctx"].reshape(KC, 128).T], axis=-1)
    m["cT"] = np.ascontiguousarray(cT.astype(f))
    m["mod_w"] = np.ascontiguousarray(inp["mod_w"])
    m["mod_b"] = np.ascontiguousarray(inp["mod_b"].reshape(DEPTH, 48, 128).transpose(2, 0, 1))
    ng = np.stack([inp["norm_mix_g"].reshape(DEPTH, KC, 128), inp["norm_ffn_g"].reshape(DEPTH, KC, 128)], axis=1)
    m["norm_g"] = np.ascontiguousarray(ng.transpose(3, 0, 1, 2))
    m["ffn_w_gu"] = np.ascontiguousarray(inp["ffn_w_gu"])
    m["ffn_w_down"] = np.ascontiguousarray(inp["ffn_w_down"])
    m["lru_w_in"] = np.ascontiguousarray(inp["lru_w_in"])
    cw = inp["lru_conv_w"].reshape(2, 4, NRC, 128)
    cb = inp["lru_conv_b"].reshape(2, 1, NRC, 128)
    cv = np.concatenate([cw, cb], axis=1)
    m["lru_conv"] = np.ascontiguousarray(cv.transpose(3, 0, 2, 1))
    m["lru_gate_w"] = np.ascontiguousarray(inp["lru_gate_w"])
    gb = inp["lru_gate_b"].reshape(2, 2, 2, NRC, 128)
    lam = inp["lru_lambda"].reshape(2, 2, 1, NRC, 128)
    gvv = np.concatenate([gb, lam], axis=2)
    m["lru_gvec"] = np.ascontiguousarray(gvv.transpose(4, 0, 1, 3, 2))
    m["lru_w_out"] = np.ascontiguousarray(inp["lru_w_out"])
    m["hg_w_in"] = np.ascontiguousarray(inp["hg_w_in"])
    m["hg_lb"] = np.ascontiguousarray(inp["hg_lb_logits"].reshape(DEPTH, KC, 128).transpose(2, 1, 0))
    m["hg_ng"] = np.ascontiguousarray(inp["hg_norm_g"].reshape(128, 1))
    m["hg_w_o"] = np.ascontiguousarray(inp["hg_w_o"])
    m["hgc"] = hg_consts()
    m["na_w_qkv"] = np.ascontiguousarray(inp["na_w_qkv"])
    m["na_w_o"] = np.ascontiguousarray(inp["na_w_o"])
    m["na_qkg"] = np.ascontiguousarray(np.stack([np.tile(inp["na_q_norm_g"][0], 2), np.tile(inp["na_k_norm_g"][0], 2)], axis=1).astype(f))
    kc_ = np.arange(64)[:, None]
    qc_ = np.arange(64)[None, :]
    co = kc_ - qc_ + 15
    ok = (co >= 0) & (co <= 30)
    rp = inp["na_rpb"][0][:, :, np.clip(co, 0, 30)]
    rp = np.where(ok[None, None], rp, 0.0).astype(f)
    m["na_rp"] = np.ascontiguousarray(rp.transpose(0, 2, 1, 3))
    m["na_cmask"], m["nac"] = na_consts()
    c = np.zeros((128, 512), f)
    c[:, 0:128] = np.eye(128, dtype=f)
    c[:, 128:256] = 1.0
    m["consts"] = c
    return m


def na_consts():
    kc = np.arange(64)[:, None]
    qc = np.arange(64)[None, :]
    cs = np.clip(qc - 8, 0, 48)
    cm = ((kc >= cs) & (kc < cs + 16)).astype(np.float32)
    cmask = np.ascontiguousarray(np.broadcast_to(np.tile(cm, (2, 1))[:, None, :], (128, 15, 64))).astype(np.float32)
    nac = np.zeros((128, 3, 128), np.float32)
    nac[0:64, 0, 0:64] = 1.0
    nac[64:128, 0, 64:128] = 1.0
    nac[:, 1, 0:64] = 1.0
    nac[:, 2, 64:128] = 1.0
    return cmask, nac


def hg_consts():
    c = np.zeros((128, 2, 128), np.float32)
    s_ = np.arange(128)[:, None]
    t_ = np.arange(128)[None, :]
    c[:, 0, :] = (s_ <= t_)
    c[:, 1, :] = (s_ >= t_)
    return c


FULL_PHASES = [(p, l) for l in range(DEPTH) for p in ("mix", "ffn")]


def kernel(**inputs):
    inp = {k: np.asarray(v) for k, v in inputs.items()}
    kb = build({"phases": FULL_PHASES})
    in_maps = [host_inputs(inp, b) for b in range(8)]
    res = run_bass_kernel_spmd(kb.nc, in_maps, core_ids=list(range(8)))
    out = np.stack([np.ascontiguousarray(r["outT"].T) for r in res.results], axis=0)
    return out.astype(np.float32)
```

```python
import numpy as np
import concourse.bass as bass
import concourse.mybir as mybir
from concourse.bass_utils import run_bass_kernel_spmd

F32 = mybir.dt.float32
BF16 = mybir.dt.bfloat16
AF = mybir.ActivationFunctionType
ALU = mybir.AluOpType

D = 1024
KC = 8
SEQ = 4096
NCTX = 256
NTOK = SEQ + NCTX
DEPTH = 4
DFF = 2816
NFF = 22
DRNN = 1280
NRC = 10
EPS = 1e-6
EPOCH = 30000
TT = [(0, 256)] + [(256 + 512 * i, 512) for i in range(8)]
ARENA_WORDS = 52000


class Buf:
    __slots__ = ("ap", "name", "w", "r")

    def __init__(self, ap, name):
        self.ap = ap
        self.name = name
        self.w = None
        self.r = []


class _Dummy:
    def then_inc(self, *a, **k):
        return self


class _CostProxy:
    RATE = {"pe": 2.4, "act": 1.2, "dve": 0.96, "pool": 0.6, "sp": 1.0}

    def __init__(self, eng):
        self.eng = eng
        self.cost = 0.0

    def __getattr__(self, name):
        def f(*args, **kw):
            if name in ("matmul", "transpose"):
                rhs = kw.get("rhs")
                if rhs is None:
                    rhs = args[1] if len(args) > 1 else args[0]
                n = rhs.free_size()
                mult = 4.0 if rhs.dtype == F32 else 1.0
                self.cost += mult * max(n, 64) / 2.4 + 4.0
            else:
                out = kw.get("out", kw.get("ap", args[0] if args else None))
                n = out.free_size() if out is not None else 64
                if name == "tensor_tensor_scan":
                    n *= 2
                self.cost += 70.0 + max(n, 64) / self.RATE[self.eng]
            return _Dummy()
        return f


class Node:
    __slots__ = ("kind", "eng", "fn", "preds", "cost", "ev", "dma_args", "lat")

    def __init__(self, kind, eng, fn, preds, cost, dma_args=None, lat=0.0):
        self.kind = kind
        self.eng = eng
        self.fn = fn
        self.preds = preds
        self.cost = cost
        self.ev = None
        self.dma_args = dma_args
        self.lat = lat


class KB:
    WINDOW = 160

    def __init__(self):
        self.nc = bass.Bass("TRN2", target_bir_lowering=False)
        nc = self.nc
        self.E = {"pe": nc.tensor, "act": nc.scalar, "dve": nc.vector, "pool": nc.gpsimd, "sp": nc.sync}
        self.tick = dict.fromkeys(self.E, 0)
        self.known = {e: {} for e in self.E}
        self.sems = {}
        self.ndma = 24
        self.dma_cnt = [0] * self.ndma
        self.dma_rr = 0
        self.arena = nc.alloc_sbuf_tensor("arena", [128, ARENA_WORDS], F32)
        self.off = 0
        self.psum = [Buf(nc.alloc_psum_tensor(f"ps{i}", [128, 512], F32)[:], f"ps{i}") for i in range(8)]
        self.ps_rr = 0
        self.ps_lo = 0
        self.ps_hi = 8
        self.nops = 0
        self.nodes = []
        self.log = {e: [] for e in self.E}

    def alloc(self, name, free_shape, dtype=F32):
        n = int(np.prod(free_shape))
        words = (n + 1) // 2 if dtype == BF16 else n
        words = (words + 7) // 8 * 8
        assert self.off + words <= ARENA_WORDS, f"arena overflow at {name}: {self.off}+{words}"
        ap = self.arena[:, self.off:self.off + words]
        self.off += words
        if dtype == BF16:
            ap = ap.bitcast(BF16)
        ap = ap[:, 0:n]
        if len(free_shape) > 1:
            names = [f"d{i}" for i in range(len(free_shape))]
            kw = {nm: int(free_shape[i]) for i, nm in enumerate(names) if i > 0}
            ap = ap.rearrange(f"p ({' '.join(names)}) -> p {' '.join(names)}", **kw)
        return Buf(ap, name)

    def ps(self):
        if not (self.ps_lo <= self.ps_rr < self.ps_hi):
            self.ps_rr = self.ps_lo
        b = self.psum[self.ps_rr]
        self.ps_rr += 1
        if self.ps_rr >= self.ps_hi:
            self.ps_rr = self.ps_lo
        return b

    def _preds(self, reads, writes):
        preds = set()
        for b in reads:
            if b.w is not None:
                preds.add(b.w)
        for b in writes:
            if b.w is not None:
                preds.add(b.w)
            preds.update(b.r)
        return preds

    def _track(self, idx, reads, writes):
        for b in reads:
            b.r.append(idx)
        for b in writes:
            b.w = idx
            b.r = []

    def op(self, e, fn, reads=(), writes=()):
        px = _CostProxy(e)
        fn(px)
        idx = len(self.nodes)
        self.nodes.append(Node("op", e, fn, self._preds(reads, writes), px.cost))
        self._track(idx, reads, writes)
        self.nops += 1
        return idx

    def dma(self, q, out, in_, reads=(), writes=()):
        idx = len(self.nodes)
        nbytes = out.free_size() * out.partition_size() * (2 if out.dtype == BF16 else 4)
        issue = 1500.0 if q == "pool" else 150.0
        lat = 2500.0 + nbytes / 100.0
        self.nodes.append(Node("dma", q, None, self._preds(reads, writes), issue, dma_args=(out, in_), lat=lat))
        self._track(idx, reads, writes)
        return idx

    def barrier(self):
        self.nodes.append(Node("barrier", None, None, set(), 0.0))

    def reset_arena(self, off):
        self.barrier()
        self.off = off

    def _sem(self, key):
        s = self.sems.get(key)
        if s is None:
            s = self.nc.alloc_semaphore(name=f"s_{key[0]}_{key[1]}")
            self.sems[key] = s
        return s

    def _wait(self, e, deps):
        kn = self.known[e]
        for key, val in deps.items():
            if kn.get(key, 0) >= val:
                continue
            self.E[e].wait_ge(self._sem(key), val)
            self.log[e].append(("w", key, val))
            kn[key] = val

    def _dep_events(self, nd, skip_pe=False):
        deps = {}
        for p in nd.preds:
            ev = self.nodes[p].ev
            if skip_pe and ev[0][0] == "pe":
                continue
            if deps.get(ev[0], 0) < ev[1]:
                deps[ev[0]] = ev[1]
        return deps

    def _emit(self, nd):
        if nd.kind == "op":
            e = nd.eng
            self._wait(e, self._dep_events(nd, skip_pe=(e == "pe")))
            ins = nd.fn(self.E[e])
            self.tick[e] += 1
            t = self.tick[e]
            ep = (t - 1) // EPOCH
            key = (e, ep)
            nd.ev = (key, t - ep * EPOCH)
            ins.then_inc(self._sem(key), 1)
            self.log[e].append(("i", key, 1))
        elif nd.kind == "dma":
            q = nd.eng
            j = self.dma_rr
            self.dma_rr = (j + 1) % self.ndma
            deps = self._dep_events(nd)
            key = ("dma", j)
            if self.dma_cnt[j] > 0:
                deps[key] = max(deps.get(key, 0), 16 * self.dma_cnt[j])
            self._wait(q, deps)
            out, in_ = nd.dma_args
            ins = self.E[q].dma_start(out=out, in_=in_)
            self.dma_cnt[j] += 1
            nd.ev = (key, 16 * self.dma_cnt[j])
            ins.then_inc(self._sem(key), 16)
            self.log[q].append(("i", key, 16))
        else:
            deps = self.all_events()
            for e in self.E:
                self._wait(e, dict(deps))

    def all_events(self):
        deps = {}
        for e, t in self.tick.items():
            if t > 0:
                ep = (t - 1) // EPOCH
                deps[(e, ep)] = t - ep * EPOCH
        for j, c in enumerate(self.dma_cnt):
            if c > 0:
                deps[("dma", j)] = 16 * c
        return deps

    def finish(self):
        nodes = self.nodes
        N = len(nodes)
        emitted = [False] * N
        finish = [0.0] * N
        eng_free = dict.fromkeys(self.E, 0.0)
        HOP = 350.0
        pos = 0
        W = self.WINDOW
        tbar = 0.0
        while pos < N:
            best = None
            bkey = None
            hi = min(N, pos + W)
            for i in range(pos, hi):
                if emitted[i]:
                    continue
                nd = nodes[i]
                if nd.kind == "barrier":
                    if i == pos:
                        best = i
                    break
                ok = True
                tr = tbar
                for p in nd.preds:
                    if not emitted[p]:
                        ok = False
                        break
                    f = finish[p] + HOP
                    if f > tr:
                        tr = f
                if not ok:
                    continue
                st = eng_free[nd.eng]
                if tr > st:
                    st = tr
                key = (st, i)
                if bkey is None or key < bkey:
                    bkey = key
                    best = i
                if st <= eng_free[nd.eng] and i == pos:
                    break
            nd = nodes[best]
            self._emit(nd)
            emitted[best] = True
            if nd.kind == "barrier":
                tbar = max(max(eng_free.values()), max(finish[max(0, best - 4000):best + 1] or [0.0]))
                for e in eng_free:
                    eng_free[e] = tbar
                finish[best] = tbar
            else:
                st = bkey[0]
                eng_free[nd.eng] = st + nd.cost
                finish[best] = st + nd.cost + nd.lat
            while pos < N and emitted[pos]:
                pos += 1
        self._wait("sp", self.all_events())


def build(cfg):
    kb = KB()
    nc = kb.nc
    phases = cfg["phases"]
    IN = {}

    def din(name, shape, dt=F32):
        IN[name] = nc.dram_tensor(name, list(shape), dt, kind="ExternalInput").ap()
        return IN[name]

    xT_in = din("xT", [D, NTOK])
    cT = din("cT", [128, KC, 2])
    mod_w = din("mod_w", [DEPTH, D, 6 * D])
    mod_b = din("mod_b", [128, DEPTH, 48])
    norm_g = din("norm_g", [128, DEPTH, 2, KC])
    ffn_w_gu = din("ffn_w_gu", [DEPTH, D, 2 * DFF])
    ffn_w_down = din("ffn_w_down", [DEPTH, DFF, D])
    lru_w_in = din("lru_w_in", [2, D, 2 * DRNN])
    lru_conv = din("lru_conv", [128, 2, NRC, 5])
    lru_gate_w = din("lru_gate_w", [2, 2, 2, NRC, 128, 128])
    lru_gvec = din("lru_gvec", [128, 2, 2, NRC, 3])
    lru_w_out = din("lru_w_out", [2, DRNN, D])
    hg_w_in = din("hg_w_in", [1, D, 5 * D])
    hg_lb = din("hg_lb", [128, KC, DEPTH])
    hg_ng = din("hg_ng", [128, 1])
    hg_w_o = din("hg_w_o", [1, D, D])
    hgc = din("hgc", [128, 2, 128])
    na_w_qkv = din("na_w_qkv", [1, D, 3 * D])
    na_w_o = din("na_w_o", [1, D, D])
    na_qkg = din("na_qkg", [128, 2])
    na_rp = din("na_rp", [16, 64, 15, 64])
    na_cmask = din("na_cmask", [128, 15, 64])
    nac = din("nac", [128, 3, 128])
    consts = din("consts", [128, 512])
    outT = nc.dram_tensor("outT", [D, SEQ], F32, kind="ExternalOutput").ap()
    XT = nc.dram_tensor("XTs", [D, NTOK], F32, kind="Internal").ap()
    MT = nc.dram_tensor("MTs", [DRNN, NTOK], BF16, kind="Internal").ap()

    XTv = XT.rearrange("(kc p) t -> p kc t", p=128)
    XTb = [[Buf(None, f"XT{i}_{k}") for k in range(KC)] for i in range(len(TT))]
    MTb = [Buf(None, f"MT{i}") for i in range(NRC)]
    last_layer = max(l for _, l in phases)

    ident_f = kb.alloc("ident_f", [128], F32)
    ones_f = kb.alloc("ones_f", [128], F32)
    ident_b = kb.alloc("ident_b", [128], BF16)
    ones_b = kb.alloc("ones_b", [128], BF16)
    MOD = kb.alloc("MOD", [DEPTH, 48, 2], F32)
    NG = kb.alloc("NG", [DEPTH, 2, KC], F32)
    GS = kb.alloc("GS", [DEPTH, 2, KC, 2], F32)
    kb.dma("sp", ident_f.ap, consts[:, 0:128], writes=[ident_f])
    kb.dma("sp", ones_f.ap, consts[:, 128:256], writes=[ones_f])
    kb.dma("sp", NG.ap, norm_g, writes=[NG])
    kb.op("dve", lambda v: v.tensor_copy(out=ident_b.ap, in_=ident_f.ap), [ident_f], [ident_b])
    kb.op("dve", lambda v: v.tensor_copy(out=ones_b.ap, in_=ones_f.ap), [ones_f], [ones_b])
    epsb = kb.alloc("epsb", [1], F32)
    kb.op("pool", lambda g: g.memset(epsb.ap, EPS), [], [epsb])
    base_off = kb.off

    for ti, (t0, n) in enumerate(TT):
        kb.dma("sp", XT[:, t0:t0 + n], xT_in[:, t0:t0 + n], writes=XTb[ti])
    c_raw = kb.alloc("c_raw", [KC, 2], F32)
    c_act = kb.alloc("c_act", [KC, 2], F32)
    kb.dma("sp", c_raw.ap, cT, writes=[c_raw])
    kb.op("act", lambda a: a.activation(out=c_act.ap, in_=c_raw.ap, func=AF.Silu), [c_raw], [c_act])
    layers_needed = sorted(set(l for _, l in phases))
    mbrow = din("mod_brow", [DEPTH, 2, 6 * D])
    mw = [kb.alloc(f"mw{i}", [KC, 512], F32) for i in range(3)]
    mrow = kb.alloc("mrow", [6 * D], F32)
    brow = kb.alloc("brow", [6 * D], F32)
    cnt = 0
    for l in layers_needed:
        kb.dma("sp", brow.ap[0:2, :], mbrow[l], writes=[brow])
        for cg in range(12):
            w = mw[cnt % 3]
            cnt += 1
            kb.dma("sp", w.ap, mod_w[l].rearrange("(kc p) f -> p kc f", p=128)[:, :, cg * 512:(cg + 1) * 512], writes=[w])
            p = kb.ps()

            def mm(pe, w=w, p=p):
                for kc in range(KC):
                    ins = pe.matmul(p.ap[0:2, :], lhsT=c_act.ap[:, kc, :], rhs=w.ap[:, kc, :], start=(kc == 0), stop=(kc == KC - 1))
                return ins
            kb.op("pe", mm, [w, c_act], [p])
            kb.op("dve", lambda d, p=p, cg=cg: d.tensor_tensor(out=mrow.ap[0:2, cg * 512:(cg + 1) * 512], in0=p.ap[0:2, :],
                                                             in1=brow.ap[0:2, cg * 512:(cg + 1) * 512], op=ALU.add), [p, brow], [mrow])
        pt = kb.ps()

        def tr(pe, pt=pt):
            for ch in range(48):
                ins = pe.transpose(pt.ap[:, 2 * ch:2 * ch + 2], mrow.ap[0:2, ch * 128:(ch + 1) * 128], ident_f.ap[0:2, 0:2])
            return ins
        kb.op("pe", tr, [mrow, ident_f], [pt])
        kb.op("dve", lambda d, pt=pt, l=l: d.tensor_copy(out=MOD.ap[:, l, :, :].rearrange("p a b -> p (a b)"), in_=pt.ap[:, 0:96]), [pt], [MOD])
        for which, vs in ((0, 1), (1, 4)):
            for kc in range(KC):
                kb.op("dve", lambda d, l=l, which=which, vs=vs, kc=kc: d.tensor_scalar(
                    out=GS.ap[:, l, which, kc, :], in0=MOD.ap[:, l, vs * 8 + kc, :], scalar1=1.0,
                    scalar2=NG.ap[:, l, which, kc:kc + 1], op0=ALU.add, op1=ALU.mult), [MOD, NG], [GS])

    def modv(l, v, kc, ty):
        return MOD.ap[:, l, v * 8 + kc, ty:ty + 1]

    def norm_tile(l, which, ti, xs, sq, rs, out_fn, tmp):
        t0, n = TT[ti]
        ty = 1 if ti == 0 else 0
        kb.dma("sp", xs.ap[:, :, 0:n], XTv[:, :, t0:t0 + n], reads=XTb[ti], writes=[xs])
        kb.op("act", lambda a: a.activation(out=sq.ap[:, :, 0:n], in_=xs.ap[:, :, 0:n], func=AF.Square), [xs], [sq])
        p = kb.ps()

        def mm(pe):
            for kc in range(KC):
                ins = pe.matmul(p.ap[:, 0:n], lhsT=ones_b.ap, rhs=sq.ap[:, kc, 0:n], start=(kc == 0), stop=(kc == KC - 1))
            return ins
        kb.op("pe", mm, [ones_b, sq], [p])
        kb.op("act", lambda a: a.activation(out=rs.ap[:, 0:n], in_=p.ap[:, 0:n], func=AF.Sqrt, scale=1.0 / D, bias=EPS_AP()),
              [p, epsb], [rs])
        kb.op("dve", lambda d: d.reciprocal(out=rs.ap[:, 0:n], in_=rs.ap[:, 0:n]), [rs], [rs])
        sh_v = 0 if which == 0 else 3
        for kc in range(KC):
            tb = tmp[kc % len(tmp)]
            kb.op("dve", lambda d, kc=kc, tb=tb: d.tensor_tensor(out=tb.ap[:, 0:n], in0=xs.ap[:, kc, 0:n], in1=rs.ap[:, 0:n],
                                                                op=ALU.mult), [xs, rs], [tb])
            oap, ob = out_fn(kc)
            kb.op("act", lambda a, kc=kc, tb=tb, oap=oap: a.activation(
                out=oap, in_=tb.ap[:, 0:n], func=AF.Identity, scale=GS.ap[:, l, which, kc, ty:ty + 1],
                bias=modv(l, sh_v, kc, ty)), [tb, GS, MOD], [ob])

    def EPS_AP():
        return epsb.ap[:, 0:1]

    def load_w(dst, dst_ap, src_ap):
        kb.dma("pool", dst_ap, src_ap, writes=[dst])

    def resid(l, gate_v, ti, mc, ypsum, xres):
        t0, n = TT[ti]
        ty = 1 if ti == 0 else 0
        kb.dma("sp", xres.ap[:, 0:n], XT[mc * 128:(mc + 1) * 128, t0:t0 + n], reads=[XTb[ti][mc]], writes=[xres])
        kb.op("dve", lambda d: d.scalar_tensor_tensor(out=xres.ap[:, 0:n], in0=ypsum.ap[:, 0:n], scalar=modv(l, gate_v, mc, ty),
                                                      in1=xres.ap[:, 0:n], op0=ALU.mult, op1=ALU.add), [ypsum, xres, MOD], [xres])
        kb.dma("sp", XT[mc * 128:(mc + 1) * 128, t0:t0 + n], xres.ap[:, 0:n], reads=[xres], writes=[XTb[ti][mc]])

    def ffn_phase(l):
        kb.reset_arena(base_off)
        TMAX = 1536
        H2s = [kb.alloc(f"H2_{i}", [KC, TMAX], BF16) for i in range(2)]
        G = kb.alloc("G", [NFF, TMAX], BF16)
        xs = kb.alloc("xs", [KC, 512], F32)
        sq = kb.alloc("sq", [KC, 512], BF16)
        rs = kb.alloc("rs", [512], F32)
        tmp = [kb.alloc(f"tmp{i}", [512], F32) for i in range(2)]
        wgu = [kb.alloc(f"wgu{i}", [KC, 2, 256], BF16) for i in range(2)]
        wdn = [kb.alloc(f"wdn{i}", [NFF, 256], BF16) for i in range(2)]
        xres = [kb.alloc(f"xres{i}", [512], F32) for i in range(3)]
        sil = [kb.alloc(f"sil{i}", [512], F32) for i in range(2)]
        wguv = ffn_w_gu[l].rearrange("(kc p) (two f) -> p kc two f", p=128, two=2)
        wdnv = ffn_w_down[l].rearrange("(kc p) f -> p kc f", p=128)
        sts = [[0, 1, 2], [3, 4, 5], [6, 7, 8]]
        cnt = {"gu": 0, "dn": 0, "xr": 0}

        def offsets(st):
            offs = {}
            o = 0
            for ti in st:
                offs[ti] = o
                o += TT[ti][1]
            return offs

        def do_norm(si):
            st = sts[si]
            offs = offsets(st)
            H2 = H2s[si % 2]
            for ti in st:
                n = TT[ti][1]
                norm_tile(l, 1, ti, xs, sq, rs, lambda kc, ti=ti, n=n, offs=offs, H2=H2: (H2.ap[:, kc, offs[ti]:offs[ti] + n], H2), tmp)

        def do_gu(si):
            st = sts[si]
            offs = offsets(st)
            H2 = H2s[si % 2]
            for jg in range(NFF // 2):
                w = wgu[cnt["gu"] % 2]
                cnt["gu"] += 1
                for two in range(2):
                    load_w(w, w.ap[:, :, two, :], wguv[:, :, two, jg * 256:(jg + 1) * 256])
                for jj in range(2):
                    j = jg * 2 + jj
                    for ti in st:
                        n = TT[ti][1]
                        pa = kb.ps()
                        pb = kb.ps()

                        def mm(pe, w=w, jj=jj, ti=ti, n=n, pa=pa, pb=pb, offs=offs, H2=H2):
                            for two, p in ((0, pa), (1, pb)):
                                for kc in range(KC):
                                    ins = pe.matmul(p.ap[:, 0:n], lhsT=w.ap[:, kc, two, jj * 128:(jj + 1) * 128],
                                                    rhs=H2.ap[:, kc, offs[ti]:offs[ti] + n], start=(kc == 0), stop=(kc == KC - 1))
                            return ins
                        kb.op("pe", mm, [w, H2], [pa, pb])
                        s_ = sil[(j + ti) % 2]
                        kb.op("act", lambda a, pa=pa, s_=s_, n=n: a.activation(out=s_.ap[:, 0:n], in_=pa.ap[:, 0:n], func=AF.Silu), [pa], [s_])
                        kb.op("dve", lambda d, pb=pb, s_=s_, n=n, j=j, ti=ti, offs=offs: d.tensor_tensor(
                            out=G.ap[:, j, offs[ti]:offs[ti] + n], in0=s_.ap[:, 0:n], in1=pb.ap[:, 0:n], op=ALU.mult), [s_, pb], [G])

        def do_down(si):
            st = sts[si]
            offs = offsets(st)
            for mg in range(4):
                w = wdn[cnt["dn"] % 2]
                cnt["dn"] += 1
                load_w(w, w.ap, wdnv[:, :, mg * 256:(mg + 1) * 256])
                for mm_ in range(2):
                    mc = mg * 2 + mm_
                    for ti in st:
                        n = TT[ti][1]
                        p = kb.ps()

                        def mm(pe, w=w, mm_=mm_, ti=ti, n=n, p=p, offs=offs):
                            for k in range(NFF):
                                ins = pe.matmul(p.ap[:, 0:n], lhsT=w.ap[:, k, mm_ * 128:(mm_ + 1) * 128],
                                                rhs=G.ap[:, k, offs[ti]:offs[ti] + n], start=(k == 0), stop=(k == NFF - 1))
                            return ins
                        kb.op("pe", mm, [w, G], [p])
                        xr = xres[cnt["xr"] % 3]
                        cnt["xr"] += 1
                        resid(l, 5, ti, mc, p, xr)

        do_norm(0)
        for si in range(len(sts)):
            do_gu(si)
            if si + 1 < len(sts):
                do_norm(si + 1)
            do_down(si)

    def rglru_phase(l):
        slot = l // 3
        kb.reset_arena(base_off)
        HT = [kb.alloc(f"HT{ti}", [KC, TT[ti][1]], BF16) for ti in range(len(TT))]
        o1 = kb.off
        xs = kb.alloc("xs", [KC, 512], F32)
        sq = kb.alloc("sq", [KC, 512], BF16)
        rs = kb.alloc("rs", [512], F32)
        tmp = [kb.alloc(f"tmp{i}", [512], F32) for i in range(2)]
        for ti in range(len(TT)):
            n = TT[ti][1]
            norm_tile(l, 0, ti, xs, sq, rs, lambda kc, ti=ti, n=n: (HT[ti].ap[:, kc, 0:n], HT[ti]), tmp)
        kb.reset_arena(o1)
        U = kb.alloc("U", [NTOK], F32)
        A = kb.alloc("A", [NTOK], F32)
        T1 = kb.alloc("T1", [NTOK], F32)
        T2 = kb.alloc("T2", [NTOK], F32)
        H0 = kb.alloc("H0", [NTOK], F32)
        Ub = kb.alloc("Ub", [NTOK], BF16)
        win = [kb.alloc(f"win{i}", [KC, 2, 128], BF16) for i in range(2)]
        gw = [kb.alloc(f"gw{i}", [2, 2, 128], BF16) for i in range(2)]
        cv = kb.alloc("cv", [2, NRC, 5], F32)
        gv = kb.alloc("gv", [2, 2, NRC, 3], F32)
        cexp = kb.alloc("cexp", [2, NRC], F32)
        gtmp = [kb.alloc(f"gtmp{i}", [512], F32) for i in range(2)]
        mt = [kb.alloc(f"mt{i}", [512], BF16) for i in range(2)]
        kb.dma("sp", cv.ap, lru_conv, writes=[cv])
        kb.dma("sp", gv.ap, lru_gvec, writes=[gv])
        for d_ in range(2):
            kb.op("act", lambda a, d_=d_: a.activation(out=cexp.ap[:, d_, :], in_=gv.ap[:, slot, d_, :, 2], func=AF.Exp, scale=-1.0), [gv], [cexp])
            kb.op("act", lambda a, d_=d_: a.activation(out=cexp.ap[:, d_, :], in_=cexp.ap[:, d_, :], func=AF.Ln, bias=1.0), [cexp], [cexp])
            kb.op("dve", lambda v, d_=d_: v.tensor_scalar(out=cexp.ap[:, d_, :], in0=cexp.ap[:, d_, :], scalar1=-8.0, scalar2=None, op0=ALU.mult), [cexp], [cexp])
        winv = lru_w_in[slot].rearrange("(kc p) (two f) -> p kc two f", p=128, two=2)
        segs = [(0, NCTX), (NCTX, NTOK)]
        for n_ in range(NRC):
            w = win[n_ % 2]
            g = gw[n_ % 2]
            for two in range(2):
                load_w(w, w.ap[:, :, two, :], winv[:, :, two, n_ * 128:(n_ + 1) * 128])
                load_w(g, g.ap[:, two, :, :], lru_gate_w[slot, two, :, n_].rearrange("g j k -> j g k"))
            for ti, (t0, n) in enumerate(TT):
                p = kb.ps()

                def mm(pe, w=w, ti=ti, n=n, p=p):
                    for kc in range(KC):
                        ins = pe.matmul(p.ap[:, 0:n], lhsT=w.ap[:, kc, 1, :], rhs=HT[ti].ap[:, kc, 0:n], start=(kc == 0), stop=(kc == KC - 1))
                    return ins
                kb.op("pe", mm, [w, HT[ti]], [p])
                kb.op("act", lambda a, p=p, t0=t0, n=n: a.activation(out=A.ap[:, t0:t0 + n], in_=p.ap[:, 0:n], func=AF.Copy), [p], [A])
            kb.op("dve", lambda v, n_=n_: v.tensor_scalar(out=U.ap, in0=A.ap, scalar1=cv.ap[:, slot, n_, 2:3], scalar2=cv.ap[:, slot, n_, 4:5],
                                                          op0=ALU.mult, op1=ALU.add), [A, cv], [U])
            for j in (0, 1, 3):
                s = j - 2
                for (a0, a1) in segs:
                    lo = max(a0, a0 - s)
                    hi = min(a1, a1 - s)
                    kb.op("dve", lambda v, n_=n_, j=j, s=s, lo=lo, hi=hi: v.scalar_tensor_tensor(
                        out=U.ap[:, lo:hi], in0=A.ap[:, lo + s:hi + s], scalar=cv.ap[:, slot, n_, j:j + 1], in1=U.ap[:, lo:hi],
                        op0=ALU.mult, op1=ALU.add), [A, U, cv], [U])
            kb.op("act", lambda a: a.activation(out=Ub.ap, in_=U.ap, func=AF.Copy), [U], [Ub])
            for d_ in range(2):
                for ti, (t0, n) in enumerate(TT):
                    pr = kb.ps()
                    pi = kb.ps()

                    def mm(pe, g=g, d_=d_, t0=t0, n=n, pr=pr, pi=pi):
                        pe.matmul(pr.ap[:, 0:n], lhsT=g.ap[:, d_, 0, :], rhs=Ub.ap[:, t0:t0 + n], start=True, stop=True)
                        return pe.matmul(pi.ap[:, 0:n], lhsT=g.ap[:, d_, 1, :], rhs=Ub.ap[:, t0:t0 + n], start=True, stop=True)
                    kb.op("pe", mm, [g, Ub], [pr, pi])
                    kb.op("act", lambda a, pr=pr, t0=t0, n=n, d_=d_, n_=n_: a.activation(
                        out=A.ap[:, t0:t0 + n], in_=pr.ap[:, 0:n], func=AF.Sigmoid, bias=gv.ap[:, slot, d_, n_, 0:1]), [pr, gv], [A])
                    kb.op("act", lambda a, pi=pi, t0=t0, n=n, d_=d_, n_=n_: a.activation(
                        out=T2.ap[:, t0:t0 + n], in_=pi.ap[:, 0:n], func=AF.Sigmoid, bias=gv.ap[:, slot, d_, n_, 1:2]), [pi, gv], [T2])
                kb.op("act", lambda a, d_=d_, n_=n_: a.activation(out=A.ap, in_=A.ap, func=AF.Exp, scale=cexp.ap[:, d_, n_:n_ + 1]), [A, cexp], [A])
                kb.op("dve", lambda v: v.tensor_tensor(out=T1.ap, in0=A.ap, in1=A.ap, op=ALU.mult), [A], [T1])
                kb.op("act", lambda a: a.activation(out=T1.ap, in_=T1.ap, func=AF.Sqrt, scale=-1.0, bias=1.0), [T1], [T1])
                kb.op("pool", lambda v: v.tensor_tensor(out=T2.ap, in0=T2.ap, in1=U.ap, op=ALU.mult), [T2, U], [T2])
                kb.op("dve", lambda v: v.tensor_tensor(out=T2.ap, in0=T2.ap, in1=T1.ap, op=ALU.mult), [T2, T1], [T2])
                if d_ == 0:
                    kb.op("dve", lambda v: v.tensor_tensor_scan(out=H0.ap, data0=A.ap, data1=T2.ap, initial=0.0, op0=ALU.mult, op1=ALU.add),
                          [A, T2], [H0])
                else:
                    kb.op("dve", lambda v: v.tensor_tensor_scan(out=T1.ap[:, NCTX - 1::-1] if False else T1.ap[:, 0:NCTX][:, ::-1],
                                                                data0=A.ap[:, 0:NCTX][:, ::-1], data1=T2.ap[:, 0:NCTX][:, ::-1],
                                                                initial=0.0, op0=ALU.mult, op1=ALU.add), [A, T2], [T1])
                    kb.op("dve", lambda v: v.tensor_tensor_scan(out=T1.ap[:, NCTX:NTOK][:, ::-1], data0=A.ap[:, NCTX:NTOK][:, ::-1],
                                                                data1=T2.ap[:, NCTX:NTOK][:, ::-1], initial=T1.ap[:, 0:1],
                                                                op0=ALU.mult, op1=ALU.add), [A, T2, T1], [T1])
            kb.op("pool", lambda v: v.tensor_tensor(out=H0.ap, in0=H0.ap, in1=T1.ap, op=ALU.add), [H0, T1], [H0])
            for ti, (t0, n) in enumerate(TT):
                m = mt[ti % 2]
                p = kb.ps()

                def mm(pe, w=w, ti=ti, n=n, p=p):
                    for kc in range(KC):
                        ins = pe.matmul(p.ap[:, 0:n], lhsT=w.ap[:, kc, 0, :], rhs=HT[ti].ap[:, kc, 0:n], start=(kc == 0), stop=(kc == KC - 1))
                    return ins
                kb.op("pe", mm, [w, HT[ti]], [p])
                gt = gtmp[ti % 2]
                kb.op("act", lambda a, p=p, n=n, gt=gt: a.activation(out=gt.ap[:, 0:n], in_=p.ap[:, 0:n], func=AF.Gelu_apprx_tanh), [p], [gt])
                kb.op("dve", lambda v, gt=gt, t0=t0, n=n, m=m: v.tensor_tensor(out=m.ap[:, 0:n], in0=gt.ap[:, 0:n], in1=H0.ap[:, t0:t0 + n],
                                                                              op=ALU.mult), [gt, H0], [m])
                kb.dma("sp", MT[n_ * 128:(n_ + 1) * 128, t0:t0 + n], m.ap[:, 0:n], reads=[m], writes=[MTb[n_]])
        out_proj(l, lru_w_out[slot], NRC)

    def out_proj(l, w_dram, nk):
        kb.reset_arena(base_off)
        wo = kb.alloc("wo", [nk, D], BF16)
        load_w(wo, wo.ap, w_dram.rearrange("(n p) f -> p n f", p=128))
        mts = [kb.alloc(f"mts{i}", [nk, 512], BF16) for i in range(2)]
        xres = [kb.alloc(f"xres{i}", [512], F32) for i in range(3)]
        MTv = MT.rearrange("(n p) t -> p n t", p=128)
        nxr = 0
        for ti, (t0, n) in enumerate(TT):
            ms = mts[ti % 2]
            kb.dma("sp", ms.ap[:, :, 0:n], MTv[:, 0:nk, t0:t0 + n], reads=MTb[0:nk], writes=[ms])
            for mc in range(KC):
                p = kb.ps()

                def mm(pe, ms=ms, mc=mc, n=n, p=p):
                    for k in range(nk):
                        ins = pe.matmul(p.ap[:, 0:n], lhsT=wo.ap[:, k, mc * 128:(mc + 1) * 128], rhs=ms.ap[:, k, 0:n],
                                        start=(k == 0), stop=(k == nk - 1))
                    return ins
                kb.op("pe", mm, [wo, ms], [p])
                xr = xres[nxr % 3]
                nxr += 1
                resid(l, 2, ti, mc, p, xr)

    def norm_all(l):
        HT = [kb.alloc(f"HT{ti}", [KC, TT[ti][1]], BF16) for ti in range(len(TT))]
        o1 = kb.off
        xs = kb.alloc("xs", [KC, 512], F32)
        sq = kb.alloc("sq", [KC, 512], BF16)
        rs = kb.alloc("rs", [512], F32)
        tmp = [kb.alloc(f"tmp{i}", [512], F32) for i in range(2)]
        for ti in range(len(TT)):
            n = TT[ti][1]
            norm_tile(l, 0, ti, xs, sq, rs, lambda kc, ti=ti, n=n: (HT[ti].ap[:, kc, 0:n], HT[ti]), tmp)
        kb.reset_arena(o1)
        return HT

    def hgrn2_phase(l):
        slot = 0
        kb.reset_arena(base_off)
        HT = norm_all(l)
        NB = NTOK // 128
        HC = kb.alloc("HC", [2, 128], F32)
        kb.dma("sp", HC.ap, hgc, writes=[HC])
        maskb = kb.alloc("maskb", [2, 128], BF16)
        kb.op("dve", lambda v: v.tensor_copy(out=maskb.ap, in_=HC.ap), [HC], [maskb])
        ng = kb.alloc("hng", [1], F32)
        kb.dma("sp", ng.ap, hg_ng, writes=[ng])
        lbl = kb.alloc("lbl", [KC, DEPTH], F32)
        lbv = kb.alloc("lbv", [KC, 4], F32)
        kb.dma("sp", lbl.ap, hg_lb, writes=[lbl])
        kb.op("act", lambda a: a.activation(out=lbl.ap, in_=lbl.ap, func=AF.Exp), [lbl], [lbl])
        kb.op("dve", lambda v: v.tensor_tensor(out=lbv.ap[:, :, 2], in0=lbl.ap[:, :, 0], in1=lbl.ap[:, :, 1], op=ALU.add), [lbl], [lbv])
        kb.op("dve", lambda v: v.tensor_tensor(out=lbv.ap[:, :, 2], in0=lbv.ap[:, :, 2], in1=lbl.ap[:, :, 2], op=ALU.add), [lbl, lbv], [lbv])
        kb.op("dve", lambda v: v.tensor_tensor(out=lbv.ap[:, :, 2], in0=lbv.ap[:, :, 2], in1=lbl.ap[:, :, 3], op=ALU.add), [lbl, lbv], [lbv])
        kb.op("dve", lambda v: v.memset(lbv.ap[:, :, 3], 0.0), [lbv], [lbv])
        for j in range(1, l + 1):
            kb.op("dve", lambda v, j=j: v.tensor_tensor(out=lbv.ap[:, :, 3], in0=lbv.ap[:, :, 3], in1=lbl.ap[:, :, j], op=ALU.add), [lbl, lbv], [lbv])
        kb.op("dve", lambda v: v.reciprocal(out=lbv.ap[:, :, 2], in_=lbv.ap[:, :, 2]), [lbv], [lbv])
        kb.op("dve", lambda v: v.tensor_tensor(out=lbv.ap[:, :, 0], in0=lbv.ap[:, :, 3], in1=lbv.ap[:, :, 2], op=ALU.mult), [lbv], [lbv])
        kb.op("dve", lambda v: v.tensor_scalar(out=lbv.ap[:, :, 1], in0=lbv.ap[:, :, 0], scalar1=-1.0, scalar2=1.0, op0=ALU.mult, op1=ALU.add), [lbv], [lbv])

        whd = kb.alloc("whd", [KC, 5, 128], BF16)
        QT = kb.alloc("QT", [NTOK], BF16)
        Vtok = kb.alloc("Vtok", [NB, 128], BF16)
        O = kb.alloc("O", [NTOK], F32)
        KT = [kb.alloc(f"KT{d}", [NTOK], BF16) for d in range(2)]
        Bc = [kb.alloc(f"Bc{d}", [NTOK], F32) for d in range(2)]
        S = [kb.alloc(f"S{d}", [128], F32) for d in range(2)]
        ft = [kb.alloc(f"ft{i}", [512], F32) for i in range(2)]
        lt = [kb.alloc(f"lt{i}", [512], F32) for i in range(2)]
        E1 = [kb.alloc(f"E1_{i}", [128], F32) for i in range(2)]
        E2 = [kb.alloc(f"E2_{i}", [128], F32) for i in range(2)]
        EB = [kb.alloc(f"EB_{i}", [4], F32) for i in range(2)]
        Qt = [kb.alloc(f"Qt{i}", [128], BF16) for i in range(2)]
        Kt = [kb.alloc(f"Kt{i}", [128], BF16) for i in range(2)]
        KtT = [kb.alloc(f"KtT{i}", [128], BF16) for i in range(2)]
        Sp = [kb.alloc(f"Sp{i}", [128], BF16) for i in range(2)]
        Am = [kb.alloc(f"Am{i}", [128], BF16) for i in range(2)]
        tS = [kb.alloc(f"tS{i}", [128], F32) for i in range(2)]
        sgt = ft
        ogt = lt
        sqo = [kb.alloc(f"sqo{i}", [512], BF16) for i in range(2)]
        rso = [kb.alloc(f"rso{i}", [512], F32) for i in range(2)]
        ogb = [kb.alloc(f"ogb{i}", [512], BF16) for i in range(2)]
        wv = hg_w_in[slot].rearrange("(kc p) (v f) -> p kc v f", p=128, v=5)
        order = [list(range(NB)), [1, 0] + list(range(NB - 1, 1, -1))]
        for hd in range(8):
            for v_ in range(5):
                load_w(whd, whd.ap[:, :, v_, :], wv[:, :, v_, hd * 128:(hd + 1) * 128])
            kb.op("pool", lambda g: g.memset(O.ap, 0.0), [], [O])
            for d in range(2):
                kb.op("pool", lambda g, d=d: g.memset(S[d].ap, 0.0), [], [S[d]])
            for ti, (t0, n) in enumerate(TT):
                def proj(v_, ti=ti, n=n):
                    p = kb.ps()

                    def mm(pe):
                        for kc in range(KC):
                            ins = pe.matmul(p.ap[:, 0:n], lhsT=whd.ap[:, kc, v_, :], rhs=HT[ti].ap[:, kc, 0:n], start=(kc == 0), stop=(kc == KC - 1))
                        return ins
                    kb.op("pe", mm, [whd, HT[ti]], [p])
                    return p
                p = proj(0)
                kb.op("act", lambda a, p=p, t0=t0, n=n: a.activation(out=QT.ap[:, t0:t0 + n], in_=p.ap[:, 0:n], func=AF.Silu), [p], [QT])
                for d in range(2):
                    p = proj(3 + d)
                    f_ = ft[d]
                    l_ = lt[d]
                    kb.op("act", lambda a, p=p, n=n, f_=f_: a.activation(out=f_.ap[:, 0:n], in_=p.ap[:, 0:n], func=AF.Sigmoid), [p], [f_])
                    kb.op("dve", lambda v, n=n, f_=f_, hd=hd: v.tensor_scalar(out=f_.ap[:, 0:n], in0=f_.ap[:, 0:n], scalar1=lbv.ap[:, hd, 1:2],
                                                                            scalar2=lbv.ap[:, hd, 0:1], op0=ALU.mult, op1=ALU.add), [f_, lbv], [f_])
                    kb.op("act", lambda a, n=n, f_=f_, l_=l_: a.activation(out=l_.ap[:, 0:n], in_=f_.ap[:, 0:n], func=AF.Ln), [f_], [l_])
                    kb.op("dve", lambda v, n=n, f_=f_, d=d, t0=t0: v.tensor_scalar(out=KT[d].ap[:, t0:t0 + n], in0=f_.ap[:, 0:n], scalar1=-1.0, scalar2=1.0,
                                                                                  op0=ALU.mult, op1=ALU.add), [f_], [KT[d]])
                    for bi in range(n // 128):
                        c0 = t0 + bi * 128
                        if d == 0:
                            kb.op("dve", lambda v, l_=l_, bi=bi, c0=c0: v.tensor_tensor_scan(
                                out=Bc[0].ap[:, c0:c0 + 128], data0=ones_f.ap, data1=l_.ap[:, bi * 128:(bi + 1) * 128], initial=0.0,
                                op0=ALU.mult, op1=ALU.add), [ones_f, l_], [Bc[0]])
                        else:
                            kb.op("dve", lambda v, l_=l_, bi=bi, c0=c0: v.tensor_tensor_scan(
                                out=Bc[1].ap[:, c0:c0 + 128][:, ::-1], data0=ones_f.ap, data1=l_.ap[:, bi * 128:(bi + 1) * 128][:, ::-1], initial=0.0,
                                op0=ALU.mult, op1=ALU.add), [ones_f, l_], [Bc[1]])
                for bi in range(n // 128):
                    blk = t0 // 128 + bi
                    p = kb.ps()

                    def mmv(pe, ti=ti, bi=bi, p=p):
                        for kc in range(KC):
                            ins = pe.matmul(p.ap[:, 0:128], lhsT=HT[ti].ap[:, kc, bi * 128:(bi + 1) * 128], rhs=whd.ap[:, kc, 1, :],
                                            start=(kc == 0), stop=(kc == KC - 1))
                        return ins
                    kb.op("pe", mmv, [whd, HT[ti]], [p])
                    kb.op("act", lambda a, p=p, blk=blk: a.activation(out=Vtok.ap[:, blk, :], in_=p.ap[:, 0:128], func=AF.Copy), [p], [Vtok])
            for j in range(NB):
                for d in range(2):
                    blk = order[d][j]
                    i2 = d
                    c0 = blk * 128
                    tsl = slice(c0, c0 + 128)
                    midc = c0 + (63 if d == 0 else 64)
                    endc = c0 + (127 if d == 0 else 0)
                    epos = 127 if d == 0 else 0
                    kb.op("dve", lambda v, i2=i2, d=d, midc=midc: v.tensor_scalar(out=EB[i2].ap[:, 2:3], in0=Bc[d].ap[:, midc:midc + 1], scalar1=-1.0, scalar2=None,
                                                                                 op0=ALU.mult), [Bc[d]], [EB[i2]])
                    kb.op("act", lambda a, i2=i2, d=d, tsl=tsl: a.activation(out=E1[i2].ap, in_=Bc[d].ap[:, tsl], func=AF.Exp, bias=EB[i2].ap[:, 2:3]), [Bc[d], EB[i2]], [E1[i2]])
                    kb.op("act", lambda a, i2=i2, d=d, tsl=tsl, midc=midc: a.activation(out=E2[i2].ap, in_=Bc[d].ap[:, tsl], func=AF.Exp, scale=-1.0,
                                                                                       bias=Bc[d].ap[:, midc:midc + 1]), [Bc[d]], [E2[i2]])
                    kb.op("act", lambda a, i2=i2, d=d, endc=endc: a.activation(out=EB[i2].ap[:, 0:1], in_=Bc[d].ap[:, endc:endc + 1], func=AF.Exp), [Bc[d]], [EB[i2]])
                    kb.op("act", lambda a, i2=i2, d=d, midc=midc: a.activation(out=EB[i2].ap[:, 1:2], in_=Bc[d].ap[:, midc:midc + 1], func=AF.Exp), [Bc[d]], [EB[i2]])
                    kb.op("dve", lambda v, i2=i2, tsl=tsl: v.tensor_tensor(out=Qt[i2].ap, in0=QT.ap[:, tsl], in1=E1[i2].ap, op=ALU.mult), [QT, E1[i2]], [Qt[i2]])
                    kb.op("dve", lambda v, i2=i2, tsl=tsl, d=d: v.tensor_tensor(out=Kt[i2].ap, in0=KT[d].ap[:, tsl], in1=E2[i2].ap, op=ALU.mult), [KT[d], E2[i2]], [Kt[i2]])
                    kb.op("dve", lambda v, i2=i2, d=d: v.tensor_scalar(out=Sp[i2].ap, in0=S[d].ap, scalar1=EB[i2].ap[:, 1:2], scalar2=None, op0=ALU.mult), [S[d], EB[i2]], [Sp[i2]])
                    pK = kb.ps()
                    pKb = pK.ap.bitcast(BF16)
                    kb.op("pe", lambda pe, pKb=pKb, i2=i2: pe.transpose(pKb[:, 0:128], Kt[i2].ap, ident_b.ap), [Kt[i2], ident_b], [pK])
                    kb.op("act", lambda a, pKb=pKb, i2=i2: a.activation(out=KtT[i2].ap, in_=pKb[:, 0:128], func=AF.Copy), [pK], [KtT[i2]])
                    pA = kb.ps()
                    kb.op("pe", lambda pe, pA=pA, i2=i2: pe.matmul(pA.ap[:, 0:128], lhsT=Kt[i2].ap, rhs=Qt[i2].ap, start=True, stop=True), [Kt[i2], Qt[i2]], [pA])
                    kb.op("dve", lambda v, pA=pA, i2=i2: v.tensor_scalar(out=tS[i2].ap, in0=pA.ap[:, 0:128], scalar1=1e30, scalar2=-1e30, op0=ALU.min, op1=ALU.max), [pA], [tS[i2]])
                    kb.op("dve", lambda v, i2=i2, d=d: v.tensor_tensor(out=Am[i2].ap, in0=tS[i2].ap, in1=maskb.ap[:, d, :], op=ALU.mult), [tS[i2], maskb], [Am[i2]])
                    pO = kb.ps()

                    def mmo(pe, pO=pO, i2=i2, blk=blk):
                        pe.matmul(pO.ap[:, 0:128], lhsT=Sp[i2].ap, rhs=Qt[i2].ap, start=True, stop=False)
                        return pe.matmul(pO.ap[:, 0:128], lhsT=Vtok.ap[:, blk, :], rhs=Am[i2].ap, start=False, stop=True)
                    kb.op("pe", mmo, [Sp[i2], Qt[i2], Vtok, Am[i2]], [pO])
                    kb.op("dve", lambda v, pO=pO, tsl=tsl: v.tensor_tensor(out=O.ap[:, tsl], in0=pO.ap[:, 0:128], in1=O.ap[:, tsl], op=ALU.add), [pO, O], [O])
                    pS = kb.ps()
                    kb.op("pe", lambda pe, pS=pS, i2=i2, blk=blk: pe.matmul(pS.ap[:, 0:128], lhsT=KtT[i2].ap, rhs=Vtok.ap[:, blk, :], start=True, stop=True), [KtT[i2], Vtok], [pS])
                    kb.op("dve", lambda v, pS=pS, i2=i2, epos=epos: v.tensor_scalar(out=tS[i2].ap, in0=pS.ap[:, 0:128], scalar1=E1[i2].ap[:, epos:epos + 1], scalar2=None,
                                                                                   op0=ALU.mult), [pS, E1[i2]], [tS[i2]])
                    kb.op("dve", lambda v, i2=i2, d=d: v.scalar_tensor_tensor(out=S[d].ap, in0=S[d].ap, scalar=EB[i2].ap[:, 0:1], in1=tS[i2].ap,
                                                                             op0=ALU.mult, op1=ALU.add), [S[d], EB[i2], tS[i2]], [S[d]])
            for ti, (t0, n) in enumerate(TT):
                i2 = ti % 2
                p = kb.ps()

                def mmg(pe, ti=ti, n=n, p=p):
                    for kc in range(KC):
                        ins = pe.matmul(p.ap[:, 0:n], lhsT=whd.ap[:, kc, 2, :], rhs=HT[ti].ap[:, kc, 0:n], start=(kc == 0), stop=(kc == KC - 1))
                    return ins
                kb.op("pe", mmg, [whd, HT[ti]], [p])
                kb.op("act", lambda a, p=p, n=n, i2=i2: a.activation(out=sgt[i2].ap[:, 0:n], in_=p.ap[:, 0:n], func=AF.Silu), [p], [sgt[i2]])
                kb.op("act", lambda a, t0=t0, n=n, i2=i2: a.activation(out=sqo[i2].ap[:, 0:n], in_=O.ap[:, t0:t0 + n], func=AF.Square), [O], [sqo[i2]])
                p2 = kb.ps()
                kb.op("pe", lambda pe, p2=p2, n=n, i2=i2: pe.matmul(p2.ap[:, 0:n], lhsT=ones_b.ap, rhs=sqo[i2].ap[:, 0:n], start=True, stop=True), [ones_b, sqo[i2]], [p2])
                kb.op("act", lambda a, p2=p2, n=n, i2=i2: a.activation(out=rso[i2].ap[:, 0:n], in_=p2.ap[:, 0:n], func=AF.Sqrt, scale=1.0 / 128, bias=EPS_AP()), [p2, epsb], [rso[i2]])
                kb.op("dve", lambda v, n=n, i2=i2: v.reciprocal(out=rso[i2].ap[:, 0:n], in_=rso[i2].ap[:, 0:n]), [rso[i2]], [rso[i2]])
                kb.op("dve", lambda v, t0=t0, n=n, i2=i2: v.scalar_tensor_tensor(out=ogt[i2].ap[:, 0:n], in0=O.ap[:, t0:t0 + n], scalar=ng.ap[:, 0:1], in1=rso[i2].ap[:, 0:n],
                                                                                op0=ALU.mult, op1=ALU.mult), [O, ng, rso[i2]], [ogt[i2]])
                kb.op("dve", lambda v, n=n, i2=i2: v.tensor_tensor(out=ogb[i2].ap[:, 0:n], in0=ogt[i2].ap[:, 0:n], in1=sgt[i2].ap[:, 0:n], op=ALU.mult), [ogt[i2], sgt[i2]], [ogb[i2]])
                kb.dma("sp", MT[hd * 128:(hd + 1) * 128, t0:t0 + n], ogb[i2].ap[:, 0:n], reads=[ogb[i2]], writes=[MTb[hd]])
        out_proj(l, hg_w_o[slot], 8)

    def na_phase(l):
        slot = 0
        kb.reset_arena(base_off)
        HT = norm_all(l)
        NB = NTOK // 128
        kb.ps_lo, kb.ps_hi = 0, 4
        ncf = kb.alloc("ncf", [3, 128], F32)
        kb.dma("sp", ncf.ap, nac, writes=[ncf])
        ncb = kb.alloc("ncb", [3, 128], BF16)
        kb.op("dve", lambda v: v.tensor_copy(out=ncb.ap, in_=ncf.ap), [ncf], [ncb])
        cmask = kb.alloc("cmask", [15, 64], F32)
        kb.dma("sp", cmask.ap, na_cmask, writes=[cmask])
        qkg = kb.alloc("qkg", [2], F32)
        kb.dma("sp", qkg.ap, na_qkg, writes=[qkg])
        wq = kb.alloc("wqkv", [KC, 3, 128], BF16)
        QK = [kb.alloc(f"QK{i}", [NTOK], BF16) for i in range(2)]
        VAB = [kb.alloc(f"VAB{i}", [NB, 128], BF16) for i in range(2)]
        for i in range(2):
            kb.op("pool", lambda g, i=i: g.memset(VAB[i].ap, 0.0), [], [VAB[i]])
        EBraw = [kb.alloc(f"EBraw{i}", [15, 64], F32) for i in range(2)]
        EBt = [kb.alloc(f"EBt{i}", [15, 64], BF16) for i in range(2)]
        NCH = (6, 4, 4)
        MK = [[kb.alloc(f"MK{h2}_{c}", [NCH[c] * 256], BF16) for c in range(3)] for h2 in range(2)]
        qf = [kb.alloc(f"qf{i}", [512], F32) for i in range(2)]
        sqn = [kb.alloc(f"sqn{i}", [512], BF16) for i in range(2)]
        rsn = [kb.alloc(f"rsn{i}", [512], F32) for i in range(2)]
        P2 = [kb.alloc(f"P2_{i}", [2, 256], BF16) for i in range(4)]
        rden = [kb.alloc(f"rden{i}", [256], F32) for i in range(2)]
        ob = [kb.alloc(f"ob{i}", [256], BF16) for i in range(2)]
        wv3 = na_w_qkv[slot].rearrange("(kc p) (v f) -> p kc v f", p=128, v=3)
        blocks = []
        for qb in range(16):
            r0 = 4 * qb
            if qb == 0:
                cls, kr0s = 1, [0, 2, 4, 6]
            elif qb == 15:
                cls, kr0s = 2, [56, 58, 60, 62]
            else:
                cls, kr0s = 0, [r0 - 4 + 2 * c for c in range(6)]
            keys = [(NCTX + kr0 * 64, ci) for ci, kr0 in enumerate(kr0s)] + [(0, None), (128, None)]
            blocks.append((NCTX + r0 * 64, cls, keys))
        blocks.append((0, None, [(0, None), (128, None)]))
        cls_def = {0: (8, [4, 6, 8, 10, 12, 14]), 1: (0, [0, 2, 4, 6]), 2: (60, [56, 58, 60, 62])}
        nP = 0
        nblk = 0
        for hp in range(8):
            for v_ in range(3):
                load_w(wq, wq.ap[:, :, v_, :], wv3[:, :, v_, hp * 128:(hp + 1) * 128])
            for ti, (t0, n) in enumerate(TT):
                for which in range(2):
                    p = kb.ps()

                    def mm(pe, p=p, which=which, ti=ti, n=n):
                        for kc in range(KC):
                            ins = pe.matmul(p.ap[:, 0:n], lhsT=wq.ap[:, kc, which, :], rhs=HT[ti].ap[:, kc, 0:n], start=(kc == 0), stop=(kc == KC - 1))
                        return ins
                    kb.op("pe", mm, [wq, HT[ti]], [p])
                    i2 = which
                    kb.op("act", lambda a, p=p, n=n, i2=i2: a.activation(out=qf[i2].ap[:, 0:n], in_=p.ap[:, 0:n], func=AF.Copy), [p], [qf[i2]])
                    kb.op("act", lambda a, n=n, i2=i2: a.activation(out=sqn[i2].ap[:, 0:n], in_=qf[i2].ap[:, 0:n], func=AF.Square), [qf[i2]], [sqn[i2]])
                    p2 = kb.ps()
                    kb.op("pe", lambda pe, p2=p2, n=n, i2=i2: pe.matmul(p2.ap[:, 0:n], lhsT=ncb.ap[:, 0, :], rhs=sqn[i2].ap[:, 0:n], start=True, stop=True), [ncb, sqn[i2]], [p2])
                    kb.op("act", lambda a, p2=p2, n=n, i2=i2: a.activation(out=rsn[i2].ap[:, 0:n], in_=p2.ap[:, 0:n], func=AF.Sqrt, scale=1.0 / 64, bias=EPS_AP()), [p2, epsb], [rsn[i2]])
                    kb.op("dve", lambda v, n=n, i2=i2: v.reciprocal(out=rsn[i2].ap[:, 0:n], in_=rsn[i2].ap[:, 0:n]), [rsn[i2]], [rsn[i2]])
                    kb.op("dve", lambda v, n=n, i2=i2, which=which, t0=t0: v.scalar_tensor_tensor(
                        out=QK[which].ap[:, t0:t0 + n], in0=qf[i2].ap[:, 0:n], scalar=qkg.ap[:, which:which + 1], in1=rsn[i2].ap[:, 0:n],
                        op0=ALU.mult, op1=ALU.mult), [qf[i2], qkg, rsn[i2]], [QK[which]])
                for bi in range(n // 128):
                    blk = t0 // 128 + bi
                    p = kb.ps()

                    def mmv(pe, ti=ti, bi=bi, p=p):
                        for kc in range(KC):
                            ins = pe.matmul(p.ap[:, 0:128], lhsT=HT[ti].ap[:, kc, bi * 128:(bi + 1) * 128], rhs=wq.ap[:, kc, 2, :],
                                            start=(kc == 0), stop=(kc == KC - 1))
                        return ins
                    kb.op("pe", mmv, [wq, HT[ti]], [p])
                    kb.op("act", lambda a, p=p, blk=blk: a.activation(out=VAB[0].ap[:, blk, 0:64], in_=p.ap[:, 0:64], func=AF.Copy), [p], [VAB[0]])
                    kb.op("dve", lambda v, p=p, blk=blk: v.tensor_copy(out=VAB[1].ap[:, blk, 64:128], in_=p.ap[:, 64:128]), [p], [VAB[1]])
            for h2 in range(2):
                h = 2 * hp + h2
                for half in range(2):
                    kb.dma("sp", EBraw[h2].ap[half * 64:(half + 1) * 64], na_rp[h], writes=[EBraw[h2]])
                kb.op("act", lambda a, h2=h2: a.activation(out=EBraw[h2].ap, in_=EBraw[h2].ap, func=AF.Exp), [EBraw[h2]], [EBraw[h2]])
                kb.op("dve", lambda v, h2=h2: v.tensor_tensor(out=EBt[h2].ap, in0=EBraw[h2].ap, in1=cmask.ap, op=ALU.mult), [EBraw[h2], cmask], [EBt[h2]])
                for cls in range(3):
                    r0, kr0s = cls_def[cls]
                    mk = MK[h2][cls]
                    kb.op("pool", lambda g, mk=mk: g.memset(mk.ap, 0.0), [], [mk])

                    def copies(g, mk=mk, r0=r0, kr0s=kr0s, cls=cls, h2=h2):
                        ins = None
                        for ci, kr0 in enumerate(kr0s):
                            for a_ in range(2):
                                for bq in range(4):
                                    dr = kr0 + a_ - r0 - bq
                                    if cls == 0 and not (-4 <= dr <= 3):
                                        continue
                                    assert -7 <= dr <= 7
                                    ins = g.tensor_copy(out=mk.ap[a_ * 64:(a_ + 1) * 64, ci * 256 + bq * 64:ci * 256 + (bq + 1) * 64],
                                                        in_=EBt[h2].ap[a_ * 64:(a_ + 1) * 64, dr + 7, :])
                        return ins
                    kb.op("pool", copies, [EBt[h2]], [mk])
            for (q0, cls, keys) in blocks:
                num = kb.psum[4 + 2 * (nblk % 2)]
                den = kb.psum[5 + 2 * (nblk % 2)]
                i2 = nblk % 2
                nblk += 1
                npairs = len(keys) // 2
                total = 2 * npairs
                cnt = 0
                for h2 in range(2):
                    pb = 64 * h2
                    for pi in range(npairs):
                        (k0, c0), (k1, c1) = keys[2 * pi], keys[2 * pi + 1]
                        pS = kb.ps()

                        def mms(pe, pS=pS, pb=pb, k0=k0, k1=k1, q0=q0):
                            pe.matmul(pS.ap[:, 0:256], lhsT=QK[1].ap[pb:pb + 64, k0:k0 + 128], rhs=QK[0].ap[pb:pb + 64, q0:q0 + 256], start=True, stop=True)
                            return pe.matmul(pS.ap[:, 256:512], lhsT=QK[1].ap[pb:pb + 64, k1:k1 + 128], rhs=QK[0].ap[pb:pb + 64, q0:q0 + 256], start=True, stop=True)
                        kb.op("pe", mms, [QK[0], QK[1]], [pS])
                        P = P2[nP % 4]
                        nP += 1
                        kb.op("act", lambda a, pS=pS, P=P: a.activation(out=P.ap.rearrange("p a b -> p (a b)"), in_=pS.ap, func=AF.Exp, scale=0.125), [pS], [P])
                        if c0 is not None:
                            mk = MK[h2][cls]
                            kb.op("pool", lambda g, P=P, mk=mk, c0=c0: g.tensor_tensor(out=P.ap.rearrange("p a b -> p (a b)"), in0=P.ap.rearrange("p a b -> p (a b)"),
                                                                                       in1=mk.ap[:, c0 * 256:c0 * 256 + 512], op=ALU.mult), [P, mk], [P])
                        first = (cnt == 0)
                        last = (cnt == total - 1)
                        cnt += 1

                        def mmpv(pe, P=P, h2=h2, k0=k0, k1=k1, first=first, last=last, num=num, den=den):
                            for c, kk in ((0, k0), (1, k1)):
                                st = first and c == 0
                                sp_ = last and c == 1
                                pe.matmul(num.ap[:, 0:256], lhsT=VAB[h2].ap[:, kk // 128, :], rhs=P.ap[:, c, :], start=st, stop=sp_)
                                ins = pe.matmul(den.ap[:, 0:256], lhsT=ncb.ap[:, 1 + h2, :], rhs=P.ap[:, c, :], start=st, stop=sp_)
                            return ins
                        kb.op("pe", mmpv, [P, VAB[h2], ncb], [num, den])
                kb.op("dve", lambda v, den=den, i2=i2: v.reciprocal(out=rden[i2].ap, in_=den.ap[:, 0:256]), [den], [rden[i2]])
                kb.op("dve", lambda v, num=num, i2=i2: v.tensor_tensor(out=ob[i2].ap, in0=num.ap[:, 0:256], in1=rden[i2].ap, op=ALU.mult), [num, rden[i2]], [ob[i2]])
                kb.dma("sp", MT[hp * 128:(hp + 1) * 128, q0:q0 + 256], ob[i2].ap, reads=[ob[i2]], writes=[MTb[hp]])
        kb.ps_lo, kb.ps_hi = 0, 8
        out_proj(l, na_w_o[slot], 8)

    for ph, l in phases:
        if ph == "ffn":
            ffn_phase(l)
        elif ph == "mix":
            kind = l % 3
            if kind == 0:
                rglru_phase(l)
            elif kind == 2:
                hgrn2_phase(l)
            else:
                na_phase(l)
    kb.barrier()
    for ti, (t0, n) in enumerate(TT):
        if ti == 0 and not cfg.get("dump_ctx"):
            continue
    if cfg.get("dump_all"):
        outA = nc.dram_tensor("outA", [D, NTOK], F32, kind="ExternalOutput").ap()
        for ti, (t0, n) in enumerate(TT):
            kb.dma("sp", outA[:, t0:t0 + n], XT[:, t0:t0 + n], reads=XTb[ti])
    for ti, (t0, n) in enumerate(TT):
        if ti == 0:
            continue
        kb.dma("sp", outT[:, t0 - NCTX:t0 - NCTX + n], XT[:, t0:t0 + n], reads=XTb[ti])
    kb.finish()
    return kb


def host_inputs(inp, b):
    f = np.float32
    m = {}
    m["xT"] = np.ascontiguousarray(np.concatenate([inp["ctx"][b], inp["x"][b]], axis=0).T.astype(f))
    cT = np.stack([inp["c"][b].reshape(KC, 128).T, inp["c_ctx"].reshape(KC, 128).T], axis=-1)
    m["cT"] = np.ascontiguousarray(cT.astype(f))
    m["mod_w"] = np.ascontiguousarray(inp["mod_w"])
    m["mod_b"] = np.ascontiguousarray(inp["mod_b"].reshape(DEPTH, 48, 128).transpose(2, 0, 1))
    m["mod_brow"] = np.ascontiguousarray(np.broadcast_to(inp["mod_b"][:, None, :], (DEPTH, 2, 6 * D))).astype(f)
    ng = np.stack([inp["norm_mix_g"].reshape(DEPTH, KC, 128), inp["norm_ffn_g"].reshape(DEPTH, KC, 128)], axis=1)
    m["norm_g"] = np.ascontiguousarray(ng.transpose(3, 0, 1, 2))
    m["ffn_w_gu"] = np.ascontiguousarray(inp["ffn_w_gu"])
    m["ffn_w_down"] = np.ascontiguousarray(inp["ffn_w_down"])
    m["lru_w_in"] = np.ascontiguousarray(inp["lru_w_in"])
    cw = inp["lru_conv_w"].reshape(2, 4, NRC, 128)
    cb = inp["lru_conv_b"].reshape(2, 1, NRC, 128)
    cv = np.concatenate([cw, cb], axis=1)
    m["lru_conv"] = np.ascontiguousarray(cv.transpose(3, 0, 2, 1))
    m["lru_gate_w"] = np.ascontiguousarray(inp["lru_gate_w"])
    gb = inp["lru_gate_b"].reshape(2, 2, 2, NRC, 128)
    lam = inp["lru_lambda"].reshape(2, 2, 1, NRC, 128)
    gvv = np.concatenate([gb, lam], axis=2)
    m["lru_gvec"] = np.ascontiguousarray(gvv.transpose(4, 0, 1, 3, 2))
    m["lru_w_out"] = np.ascontiguousarray(inp["lru_w_out"])
    m["hg_w_in"] = np.ascontiguousarray(inp["hg_w_in"])
    m["hg_lb"] = np.ascontiguousarray(inp["hg_lb_logits"].reshape(DEPTH, KC, 128).transpose(2, 1, 0))
    m["hg_ng"] = np.ascontiguousarray(inp["hg_norm_g"].reshape(128, 1))
    m["hg_w_o"] = np.ascontiguousarray(inp["hg_w_o"])
    m["hgc"] = hg_consts()
    m["na_w_qkv"] = np.ascontiguousarray(inp["na_w_qkv"])
    m["na_w_o"] = np.ascontiguousarray(inp["na_w_o"])
    m["na_qkg"] = np.ascontiguousarray(np.stack([np.tile(inp["na_q_norm_g"][0], 2), np.tile(inp["na_k_norm_g"][0], 2)], axis=1).astype(f))
    kc_ = np.arange(64)[:, None]
    qc_ = np.arange(64)[None, :]
    co = kc_ - qc_ + 15
    ok = (co >= 0) & (co <= 30)
    rp = inp["na_rpb"][0][:, :, np.clip(co, 0, 30)]
    rp = np.where(ok[None, None], rp, 0.0).astype(f)
    m["na_rp"] = np.ascontiguousarray(rp.transpose(0, 2, 1, 3))
    m["na_cmask"], m["nac"] = na_consts()
    c = np.zeros((128, 512), f)
    c[:, 0:128] = np.eye(128, dtype=f)
    c[:, 128:256] = 1.0
    m["consts"] = c
    return m


def na_consts():
    kc = np.arange(64)[:, None]
    qc = np.arange(64)[None, :]
    cs = np.clip(qc - 8, 0, 48)
    cm = ((kc >= cs) & (kc < cs + 16)).astype(np.float32)
    cmask = np.ascontiguousarray(np.broadcast_to(np.tile(cm, (2, 1))[:, None, :], (128, 15, 64))).astype(np.float32)
    nac = np.zeros((128, 3, 128), np.float32)
    nac[0:64, 0, 0:64] = 1.0
    nac[64:128, 0, 64:128] = 1.0
    nac[:, 1, 0:64] = 1.0
    nac[:, 2, 64:128] = 1.0
    return cmask, nac


def hg_consts():
    c = np.zeros((128, 2, 128), np.float32)
    s_ = np.arange(128)[:, None]
    t_ = np.arange(128)[None, :]
    c[:, 0, :] = (s_ <= t_)
    c[:, 1, :] = (s_ >= t_)
    return c


FULL_PHASES = [(p, l) for l in range(DEPTH) for p in ("mix", "ffn")]


def kernel(**inputs):
    inp = {k: np.asarray(v) for k, v in inputs.items()}
    kb = build({"phases": FULL_PHASES})
    in_maps = [host_inputs(inp, b) for b in range(8)]
    res = run_bass_kernel_spmd(kb.nc, in_maps, core_ids=list(range(8)))
    out = np.stack([np.ascontiguousarray(r["outT"].T) for r in res.results], axis=0)
    return out.astype(np.float32)
```

```python
import numpy as np
import concourse.bass as bass
import concourse.mybir as mybir
from concourse.bass_utils import run_bass_kernel_spmd

F32 = mybir.dt.float32
BF16 = mybir.dt.bfloat16
AF = mybir.ActivationFunctionType
ALU = mybir.AluOpType

D = 1024
KC = 8
SEQ = 4096
NCTX = 256
NTOK = SEQ + NCTX
DEPTH = 4
DFF = 2816
NFF = 22
DRNN = 1280
NRC = 10
EPS = 1e-6
EPOCH = 30000
TT = [(0, 256)] + [(256 + 512 * i, 512) for i in range(8)]
ARENA_WORDS = 52000


class Buf:
    __slots__ = ("ap", "name", "w", "r")

    def __init__(self, ap, name):
        self.ap = ap
        self.name = name
        self.w = None
        self.r = []


class _Dummy:
    def then_inc(self, *a, **k):
        return self


class _CostProxy:
    RATE = {"pe": 2.4, "act": 1.2, "dve": 0.96, "pool": 0.6, "sp": 1.0}

    def __init__(self, eng):
        self.eng = eng
        self.cost = 0.0

    def __getattr__(self, name):
        def f(*args, **kw):
            if name in ("matmul", "transpose"):
                rhs = kw.get("rhs")
                if rhs is None:
                    rhs = args[1] if len(args) > 1 else args[0]
                n = rhs.free_size()
                mult = 4.0 if rhs.dtype == F32 else 1.0
                self.cost += mult * max(n, 64) / 2.4 + 4.0
            else:
                out = kw.get("out", kw.get("ap", args[0] if args else None))
                n = out.free_size() if out is not None else 64
                if name == "tensor_tensor_scan":
                    n *= 2
                self.cost += 70.0 + max(n, 64) / self.RATE[self.eng]
            return _Dummy()
        return f


class Node:
    __slots__ = ("kind", "eng", "fn", "preds", "cost", "ev", "dma_args", "lat")

    def __init__(self, kind, eng, fn, preds, cost, dma_args=None, lat=0.0):
        self.kind = kind
        self.eng = eng
        self.fn = fn
        self.preds = preds
        self.cost = cost
        self.ev = None
        self.dma_args = dma_args
        self.lat = lat


class KB:
    WINDOW = 224

    def __init__(self):
        self.nc = bass.Bass("TRN2", target_bir_lowering=False)
        nc = self.nc
        self.E = {"pe": nc.tensor, "act": nc.scalar, "dve": nc.vector, "pool": nc.gpsimd, "sp": nc.sync}
        self.tick = dict.fromkeys(self.E, 0)
        self.known = {e: {} for e in self.E}
        self.sems = {}
        self.ndma = 24
        self.dma_cnt = [0] * self.ndma
        self.dma_rr = 0
        self.arena = nc.alloc_sbuf_tensor("arena", [128, ARENA_WORDS], F32)
        self.off = 0
        self.psum = [Buf(nc.alloc_psum_tensor(f"ps{i}", [128, 512], F32)[:], f"ps{i}") for i in range(8)]
        self.ps_rr = 0
        self.ps_lo = 0
        self.ps_hi = 8
        self.nops = 0
        self.nodes = []
        self.log = {e: [] for e in self.E}

    def alloc(self, name, free_shape, dtype=F32):
        n = int(np.prod(free_shape))
        words = (n + 1) // 2 if dtype == BF16 else n
        words = (words + 7) // 8 * 8
        assert self.off + words <= ARENA_WORDS, f"arena overflow at {name}: {self.off}+{words}"
        ap = self.arena[:, self.off:self.off + words]
        self.off += words
        if dtype == BF16:
            ap = ap.bitcast(BF16)
        ap = ap[:, 0:n]
        if len(free_shape) > 1:
            names = [f"d{i}" for i in range(len(free_shape))]
            kw = {nm: int(free_shape[i]) for i, nm in enumerate(names) if i > 0}
            ap = ap.rearrange(f"p ({' '.join(names)}) -> p {' '.join(names)}", **kw)
        return Buf(ap, name)

    def ps(self):
        if not (self.ps_lo <= self.ps_rr < self.ps_hi):
            self.ps_rr = self.ps_lo
        b = self.psum[self.ps_rr]
        self.ps_rr += 1
        if self.ps_rr >= self.ps_hi:
            self.ps_rr = self.ps_lo
        return b

    def _preds(self, reads, writes):
        preds = set()
        for b in reads:
            if b.w is not None:
                preds.add(b.w)
        for b in writes:
            if b.w is not None:
                preds.add(b.w)
            preds.update(b.r)
        return preds

    def _track(self, idx, reads, writes):
        for b in reads:
            b.r.append(idx)
        for b in writes:
            b.w = idx
            b.r = []

    def op(self, e, fn, reads=(), writes=()):
        px = _CostProxy(e)
        fn(px)
        idx = len(self.nodes)
        self.nodes.append(Node("op", e, fn, self._preds(reads, writes), px.cost))
        self._track(idx, reads, writes)
        self.nops += 1
        return idx

    def dma(self, q, out, in_, reads=(), writes=()):
        idx = len(self.nodes)
        nbytes = out.free_size() * out.partition_size() * (2 if out.dtype == BF16 else 4)
        issue = 1500.0 if q == "pool" else 150.0
        lat = 2500.0 + nbytes / 100.0
        self.nodes.append(Node("dma", q, None, self._preds(reads, writes), issue, dma_args=(out, in_), lat=lat))
        self._track(idx, reads, writes)
        return idx

    def barrier(self):
        self.nodes.append(Node("barrier", None, None, set(), 0.0))

    def reset_arena(self, off):
        self.barrier()
        self.off = off

    def _sem(self, key):
        s = self.sems.get(key)
        if s is None:
            s = self.nc.alloc_semaphore(name=f"s_{key[0]}_{key[1]}")
            self.sems[key] = s
        return s

    def _wait(self, e, deps):
        kn = self.known[e]
        for key, val in deps.items():
            if kn.get(key, 0) >= val:
                continue
            self.E[e].wait_ge(self._sem(key), val)
            self.log[e].append(("w", key, val))
            kn[key] = val

    def _dep_events(self, nd, skip_pe=False):
        deps = {}
        for p in nd.preds:
            ev = self.nodes[p].ev
            if skip_pe and ev[0][0] == "pe":
                continue
            if deps.get(ev[0], 0) < ev[1]:
                deps[ev[0]] = ev[1]
        return deps

    def _emit(self, nd):
        if nd.kind == "op":
            e = nd.eng
            self._wait(e, self._dep_events(nd, skip_pe=(e == "pe")))
            ins = nd.fn(self.E[e])
            self.tick[e] += 1
            t = self.tick[e]
            ep = (t - 1) // EPOCH
            key = (e, ep)
            nd.ev = (key, t - ep * EPOCH)
            ins.then_inc(self._sem(key), 1)
            self.log[e].append(("i", key, 1))
        elif nd.kind == "dma":
            q = nd.eng
            j = self.dma_rr
            self.dma_rr = (j + 1) % self.ndma
            deps = self._dep_events(nd)
            key = ("dma", j)
            if self.dma_cnt[j] > 0:
                deps[key] = max(deps.get(key, 0), 16 * self.dma_cnt[j])
            self._wait(q, deps)
            out, in_ = nd.dma_args
            ins = self.E[q].dma_start(out=out, in_=in_)
            self.dma_cnt[j] += 1
            nd.ev = (key, 16 * self.dma_cnt[j])
            ins.then_inc(self._sem(key), 16)
            self.log[q].append(("i", key, 16))
        else:
            deps = self.all_events()
            for e in self.E:
                self._wait(e, dict(deps))

    def all_events(self):
        deps = {}
        for e, t in self.tick.items():
            if t > 0:
                ep = (t - 1) // EPOCH
                deps[(e, ep)] = t - ep * EPOCH
        for j, c in enumerate(self.dma_cnt):
            if c > 0:
                deps[("dma", j)] = 16 * c
        return deps

    def finish(self):
        nodes = self.nodes
        N = len(nodes)
        emitted = [False] * N
        finish = [0.0] * N
        eng_free = dict.fromkeys(self.E, 0.0)
        HOP = 350.0
        pos = 0
        W = self.WINDOW
        tbar = 0.0
        while pos < N:
            best = None
            bkey = None
            hi = min(N, pos + W)
            for i in range(pos, hi):
                if emitted[i]:
                    continue
                nd = nodes[i]
                if nd.kind == "barrier":
                    if i == pos:
                        best = i
                    break
                ok = True
                tr = tbar
                for p in nd.preds:
                    if not emitted[p]:
                        ok = False
                        break
                    f = finish[p] + HOP
                    if f > tr:
                        tr = f
                if not ok:
                    continue
                st = eng_free[nd.eng]
                if tr > st:
                    st = tr
                key = (st, i)
                if bkey is None or key < bkey:
                    bkey = key
                    best = i
                if st <= eng_free[nd.eng] and i == pos:
                    break
            nd = nodes[best]
            self._emit(nd)
            emitted[best] = True
            if nd.kind == "barrier":
                tbar = max(max(eng_free.values()), max(finish[max(0, best - 4000):best + 1] or [0.0]))
                for e in eng_free:
                    eng_free[e] = tbar
                finish[best] = tbar
            else:
                st = bkey[0]
                eng_free[nd.eng] = st + nd.cost
                finish[best] = st + nd.cost + nd.lat
            while pos < N and emitted[pos]:
                pos += 1
        self._wait("sp", self.all_events())


def build(cfg):
    kb = KB()
    nc = kb.nc
    phases = cfg["phases"]
    IN = {}

    def din(name, shape, dt=F32):
        IN[name] = nc.dram_tensor(name, list(shape), dt, kind="ExternalInput").ap()
        return IN[name]

    xT_in = din("xT", [D, NTOK])
    cT = din("cT", [128, KC, 2])
    mod_w = din("mod_w", [DEPTH, D, 6 * D])
    mod_b = din("mod_b", [128, DEPTH, 48])
    norm_g = din("norm_g", [128, DEPTH, 2, KC])
    ffn_w_gu = din("ffn_w_gu", [DEPTH, D, 2 * DFF])
    ffn_w_down = din("ffn_w_down", [DEPTH, DFF, D])
    lru_w_in = din("lru_w_in", [2, D, 2 * DRNN])
    lru_conv = din("lru_conv", [128, 2, NRC, 5])
    lru_gate_w = din("lru_gate_w", [2, 2, 2, NRC, 128, 128])
    lru_gvec = din("lru_gvec", [128, 2, 2, NRC, 3])
    lru_w_out = din("lru_w_out", [2, DRNN, D])
    hg_w_in = din("hg_w_in", [1, D, 5 * D])
    hg_lb = din("hg_lb", [128, KC, DEPTH])
    hg_ng = din("hg_ng", [128, 1])
    hg_w_o = din("hg_w_o", [1, D, D])
    hgc = din("hgc", [128, 2, 128])
    na_w_qkv = din("na_w_qkv", [1, D, 3 * D])
    na_w_o = din("na_w_o", [1, D, D])
    na_qkg = din("na_qkg", [128, 2])
    na_rp = din("na_rp", [16, 64, 15, 64])
    na_cmask = din("na_cmask", [128, 15, 64])
    nac = din("nac", [128, 3, 128])
    consts = din("consts", [128, 512])
    outT = nc.dram_tensor("outT", [D, SEQ], F32, kind="ExternalOutput").ap()
    XT = nc.dram_tensor("XTs", [D, NTOK], F32, kind="Internal").ap()
    MT = nc.dram_tensor("MTs", [DRNN, NTOK], BF16, kind="Internal").ap()

    XTv = XT.rearrange("(kc p) t -> p kc t", p=128)
    XTb = [[Buf(None, f"XT{i}_{k}") for k in range(KC)] for i in range(len(TT))]
    MTb = [Buf(None, f"MT{i}") for i in range(NRC)]
    last_layer = max(l for _, l in phases)

    ident_f = kb.alloc("ident_f", [128], F32)
    ones_f = kb.alloc("ones_f", [128], F32)
    ident_b = kb.alloc("ident_b", [128], BF16)
    ones_b = kb.alloc("ones_b", [128], BF16)
    MOD = kb.alloc("MOD", [DEPTH, 48, 2], F32)
    NG = kb.alloc("NG", [DEPTH, 2, KC], F32)
    GS = kb.alloc("GS", [DEPTH, 2, KC, 2], F32)
    kb.dma("sp", ident_f.ap, consts[:, 0:128], writes=[ident_f])
    kb.dma("sp", ones_f.ap, consts[:, 128:256], writes=[ones_f])
    kb.dma("sp", NG.ap, norm_g, writes=[NG])
    kb.op("dve", lambda v: v.tensor_copy(out=ident_b.ap, in_=ident_f.ap), [ident_f], [ident_b])
    kb.op("dve", lambda v: v.tensor_copy(out=ones_b.ap, in_=ones_f.ap), [ones_f], [ones_b])
    epsb = kb.alloc("epsb", [1], F32)
    kb.op("pool", lambda g: g.memset(epsb.ap, EPS), [], [epsb])
    base_off = kb.off

    for ti, (t0, n) in enumerate(TT):
        kb.dma("sp", XT[:, t0:t0 + n], xT_in[:, t0:t0 + n], writes=XTb[ti])
    c_raw = kb.alloc("c_raw", [KC, 2], F32)
    c_act = kb.alloc("c_act", [KC, 2], F32)
    kb.dma("sp", c_raw.ap, cT, writes=[c_raw])
    kb.op("act", lambda a: a.activation(out=c_act.ap, in_=c_raw.ap, func=AF.Silu), [c_raw], [c_act])
    layers_needed = sorted(set(l for _, l in phases))
    mbrow = din("mod_brow", [DEPTH, 2, 6 * D])
    mw = [kb.alloc(f"mw{i}", [KC, 512], F32) for i in range(3)]
    mrow = kb.alloc("mrow", [6 * D], F32)
    brow = kb.alloc("brow", [6 * D], F32)
    cnt = 0
    for l in layers_needed:
        kb.dma("sp", brow.ap[0:2, :], mbrow[l], writes=[brow])
        for cg in range(12):
            w = mw[cnt % 3]
            cnt += 1
            kb.dma("sp", w.ap, mod_w[l].rearrange("(kc p) f -> p kc f", p=128)[:, :, cg * 512:(cg + 1) * 512], writes=[w])
            p = kb.ps()

            def mm(pe, w=w, p=p):
                for kc in range(KC):
                    ins = pe.matmul(p.ap[0:2, :], lhsT=c_act.ap[:, kc, :], rhs=w.ap[:, kc, :], start=(kc == 0), stop=(kc == KC - 1))
                return ins
            kb.op("pe", mm, [w, c_act], [p])
            kb.op("dve", lambda d, p=p, cg=cg: d.tensor_tensor(out=mrow.ap[0:2, cg * 512:(cg + 1) * 512], in0=p.ap[0:2, :],
                                                             in1=brow.ap[0:2, cg * 512:(cg + 1) * 512], op=ALU.add), [p, brow], [mrow])
        pt = kb.ps()

        def tr(pe, pt=pt):
            for ch in range(48):
                ins = pe.transpose(pt.ap[:, 2 * ch:2 * ch + 2], mrow.ap[0:2, ch * 128:(ch + 1) * 128], ident_f.ap[0:2, 0:2])
            return ins
        kb.op("pe", tr, [mrow, ident_f], [pt])
        kb.op("dve", lambda d, pt=pt, l=l: d.tensor_copy(out=MOD.ap[:, l, :, :].rearrange("p a b -> p (a b)"), in_=pt.ap[:, 0:96]), [pt], [MOD])
        for which, vs in ((0, 1), (1, 4)):
            for kc in range(KC):
                kb.op("dve", lambda d, l=l, which=which, vs=vs, kc=kc: d.tensor_scalar(
                    out=GS.ap[:, l, which, kc, :], in0=MOD.ap[:, l, vs * 8 + kc, :], scalar1=1.0,
                    scalar2=NG.ap[:, l, which, kc:kc + 1], op0=ALU.add, op1=ALU.mult), [MOD, NG], [GS])

    def modv(l, v, kc, ty):
        return MOD.ap[:, l, v * 8 + kc, ty:ty + 1]

    def norm_tile(l, which, ti, xs, sq, rs, out_fn, tmp):
        t0, n = TT[ti]
        ty = 1 if ti == 0 else 0
        kb.dma("sp", xs.ap[:, :, 0:n], XTv[:, :, t0:t0 + n], reads=XTb[ti], writes=[xs])
        kb.op("act", lambda a: a.activation(out=sq.ap[:, :, 0:n], in_=xs.ap[:, :, 0:n], func=AF.Square), [xs], [sq])
        p = kb.ps()

        def mm(pe):
            for kc in range(KC):
                ins = pe.matmul(p.ap[:, 0:n], lhsT=ones_b.ap, rhs=sq.ap[:, kc, 0:n], start=(kc == 0), stop=(kc == KC - 1))
            return ins
        kb.op("pe", mm, [ones_b, sq], [p])
        kb.op("act", lambda a: a.activation(out=rs.ap[:, 0:n], in_=p.ap[:, 0:n], func=AF.Sqrt, scale=1.0 / D, bias=EPS_AP()),
              [p, epsb], [rs])
        kb.op("dve", lambda d: d.reciprocal(out=rs.ap[:, 0:n], in_=rs.ap[:, 0:n]), [rs], [rs])
        sh_v = 0 if which == 0 else 3
        for kc in range(KC):
            tb = tmp[kc % len(tmp)]
            kb.op("dve", lambda d, kc=kc, tb=tb: d.tensor_tensor(out=tb.ap[:, 0:n], in0=xs.ap[:, kc, 0:n], in1=rs.ap[:, 0:n],
                                                                op=ALU.mult), [xs, rs], [tb])
            oap, ob = out_fn(kc)
            kb.op("act", lambda a, kc=kc, tb=tb, oap=oap: a.activation(
                out=oap, in_=tb.ap[:, 0:n], func=AF.Identity, scale=GS.ap[:, l, which, kc, ty:ty + 1],
                bias=modv(l, sh_v, kc, ty)), [tb, GS, MOD], [ob])

    def EPS_AP():
        return epsb.ap[:, 0:1]

    def load_w(dst, dst_ap, src_ap):
        kb.dma("pool", dst_ap, src_ap, writes=[dst])

    def resid(l, gate_v, ti, mc, ypsum, xres):
        t0, n = TT[ti]
        ty = 1 if ti == 0 else 0
        kb.dma("sp", xres.ap[:, 0:n], XT[mc * 128:(mc + 1) * 128, t0:t0 + n], reads=[XTb[ti][mc]], writes=[xres])
        kb.op("dve", lambda d: d.scalar_tensor_tensor(out=xres.ap[:, 0:n], in0=ypsum.ap[:, 0:n], scalar=modv(l, gate_v, mc, ty),
                                                      in1=xres.ap[:, 0:n], op0=ALU.mult, op1=ALU.add), [ypsum, xres, MOD], [xres])
        kb.dma("sp", XT[mc * 128:(mc + 1) * 128, t0:t0 + n], xres.ap[:, 0:n], reads=[xres], writes=[XTb[ti][mc]])

    def ffn_phase(l):
        kb.reset_arena(base_off)
        TMAX = 1536
        H2s = [kb.alloc(f"H2_{i}", [KC, TMAX], BF16) for i in range(2)]
        G = kb.alloc("G", [NFF, TMAX], BF16)
        xs = kb.alloc("xs", [KC, 512], F32)
        sq = kb.alloc("sq", [KC, 512], BF16)
        rs = kb.alloc("rs", [512], F32)
        tmp = [kb.alloc(f"tmp{i}", [512], F32) for i in range(2)]
        wgu = [kb.alloc(f"wgu{i}", [KC, 2, 256], BF16) for i in range(2)]
        wdn = [kb.alloc(f"wdn{i}", [NFF, 256], BF16) for i in range(2)]
        xres = [kb.alloc(f"xres{i}", [512], F32) for i in range(3)]
        sil = [kb.alloc(f"sil{i}", [512], F32) for i in range(2)]
        wguv = ffn_w_gu[l].rearrange("(kc p) (two f) -> p kc two f", p=128, two=2)
        wdnv = ffn_w_down[l].rearrange("(kc p) f -> p kc f", p=128)
        sts = [[0, 1, 2], [3, 4, 5], [6, 7, 8]]
        cnt = {"gu": 0, "dn": 0, "xr": 0}

        def offsets(st):
            offs = {}
            o = 0
            for ti in st:
                offs[ti] = o
                o += TT[ti][1]
            return offs

        def do_norm(si):
            st = sts[si]
            offs = offsets(st)
            H2 = H2s[si % 2]
            for ti in st:
                n = TT[ti][1]
                norm_tile(l, 1, ti, xs, sq, rs, lambda kc, ti=ti, n=n, offs=offs, H2=H2: (H2.ap[:, kc, offs[ti]:offs[ti] + n], H2), tmp)

        def do_gu(si):
            st = sts[si]
            offs = offsets(st)
            H2 = H2s[si % 2]
            for jg in range(NFF // 2):
                w = wgu[cnt["gu"] % 2]
                cnt["gu"] += 1
                for two in range(2):
                    load_w(w, w.ap[:, :, two, :], wguv[:, :, two, jg * 256:(jg + 1) * 256])
                for jj in range(2):
                    j = jg * 2 + jj
                    for ti in st:
                        n = TT[ti][1]
                        pa = kb.ps()
                        pb = kb.ps()

                        def mm(pe, w=w, jj=jj, ti=ti, n=n, pa=pa, pb=pb, offs=offs, H2=H2):
                            for two, p in ((0, pa), (1, pb)):
                                for kc in range(KC):
                                    ins = pe.matmul(p.ap[:, 0:n], lhsT=w.ap[:, kc, two, jj * 128:(jj + 1) * 128],
                                                    rhs=H2.ap[:, kc, offs[ti]:offs[ti] + n], start=(kc == 0), stop=(kc == KC - 1))
                            return ins
                        kb.op("pe", mm, [w, H2], [pa, pb])
                        s_ = sil[(j + ti) % 2]
                        kb.op("act", lambda a, pa=pa, s_=s_, n=n: a.activation(out=s_.ap[:, 0:n], in_=pa.ap[:, 0:n], func=AF.Silu), [pa], [s_])
                        kb.op("dve", lambda d, pb=pb, s_=s_, n=n, j=j, ti=ti, offs=offs: d.tensor_tensor(
                            out=G.ap[:, j, offs[ti]:offs[ti] + n], in0=s_.ap[:, 0:n], in1=pb.ap[:, 0:n], op=ALU.mult), [s_, pb], [G])

        def do_down(si):
            st = sts[si]
            offs = offsets(st)
            for mg in range(4):
                w = wdn[cnt["dn"] % 2]
                cnt["dn"] += 1
                load_w(w, w.ap, wdnv[:, :, mg * 256:(mg + 1) * 256])
                for mm_ in range(2):
                    mc = mg * 2 + mm_
                    for ti in st:
                        n = TT[ti][1]
                        p = kb.ps()

                        def mm(pe, w=w, mm_=mm_, ti=ti, n=n, p=p, offs=offs):
                            for k in range(NFF):
                                ins = pe.matmul(p.ap[:, 0:n], lhsT=w.ap[:, k, mm_ * 128:(mm_ + 1) * 128],
                                                rhs=G.ap[:, k, offs[ti]:offs[ti] + n], start=(k == 0), stop=(k == NFF - 1))
                            return ins
                        kb.op("pe", mm, [w, G], [p])
                        xr = xres[cnt["xr"] % 3]
                        cnt["xr"] += 1
                        resid(l, 5, ti, mc, p, xr)

        do_norm(0)
        for si in range(len(sts)):
            do_gu(si)
            if si + 1 < len(sts):
                do_norm(si + 1)
            do_down(si)

    def rglru_phase(l):
        slot = l // 3
        kb.reset_arena(base_off)
        HT = [kb.alloc(f"HT{ti}", [KC, TT[ti][1]], BF16) for ti in range(len(TT))]
        o1 = kb.off
        xs = kb.alloc("xs", [KC, 512], F32)
        sq = kb.alloc("sq", [KC, 512], BF16)
        rs = kb.alloc("rs", [512], F32)
        tmp = [kb.alloc(f"tmp{i}", [512], F32) for i in range(2)]
        for ti in range(len(TT)):
            n = TT[ti][1]
            norm_tile(l, 0, ti, xs, sq, rs, lambda kc, ti=ti, n=n: (HT[ti].ap[:, kc, 0:n], HT[ti]), tmp)
        kb.reset_arena(o1)
        U = kb.alloc("U", [NTOK], F32)
        A = kb.alloc("A", [NTOK], F32)
        T1 = kb.alloc("T1", [NTOK], F32)
        T2 = kb.alloc("T2", [NTOK], F32)
        H0 = kb.alloc("H0", [NTOK], F32)
        Ub = kb.alloc("Ub", [NTOK], BF16)
        win = [kb.alloc(f"win{i}", [KC, 2, 128], BF16) for i in range(2)]
        gw = [kb.alloc(f"gw{i}", [2, 2, 128], BF16) for i in range(2)]
        cv = kb.alloc("cv", [2, NRC, 5], F32)
        gv = kb.alloc("gv", [2, 2, NRC, 3], F32)
        cexp = kb.alloc("cexp", [2, NRC], F32)
        gtmp = [kb.alloc(f"gtmp{i}", [512], F32) for i in range(2)]
        mt = [kb.alloc(f"mt{i}", [512], BF16) for i in range(2)]
        kb.dma("sp", cv.ap, lru_conv, writes=[cv])
        kb.dma("sp", gv.ap, lru_gvec, writes=[gv])
        for d_ in range(2):
            kb.op("act", lambda a, d_=d_: a.activation(out=cexp.ap[:, d_, :], in_=gv.ap[:, slot, d_, :, 2], func=AF.Exp, scale=-1.0), [gv], [cexp])
            kb.op("act", lambda a, d_=d_: a.activation(out=cexp.ap[:, d_, :], in_=cexp.ap[:, d_, :], func=AF.Ln, bias=1.0), [cexp], [cexp])
            kb.op("dve", lambda v, d_=d_: v.tensor_scalar(out=cexp.ap[:, d_, :], in0=cexp.ap[:, d_, :], scalar1=-8.0, scalar2=None, op0=ALU.mult), [cexp], [cexp])
        winv = lru_w_in[slot].rearrange("(kc p) (two f) -> p kc two f", p=128, two=2)
        segs = [(0, NCTX), (NCTX, NTOK)]
        for n_ in range(NRC):
            w = win[n_ % 2]
            g = gw[n_ % 2]
            for two in range(2):
                load_w(w, w.ap[:, :, two, :], winv[:, :, two, n_ * 128:(n_ + 1) * 128])
                load_w(g, g.ap[:, two, :, :], lru_gate_w[slot, two, :, n_].rearrange("g j k -> j g k"))
            for ti, (t0, n) in enumerate(TT):
                p = kb.ps()

                def mm(pe, w=w, ti=ti, n=n, p=p):
                    for kc in range(KC):
                        ins = pe.matmul(p.ap[:, 0:n], lhsT=w.ap[:, kc, 1, :], rhs=HT[ti].ap[:, kc, 0:n], start=(kc == 0), stop=(kc == KC - 1))
                    return ins
                kb.op("pe", mm, [w, HT[ti]], [p])
                kb.op("act", lambda a, p=p, t0=t0, n=n: a.activation(out=A.ap[:, t0:t0 + n], in_=p.ap[:, 0:n], func=AF.Copy), [p], [A])
            kb.op("dve", lambda v, n_=n_: v.tensor_scalar(out=U.ap, in0=A.ap, scalar1=cv.ap[:, slot, n_, 2:3], scalar2=cv.ap[:, slot, n_, 4:5],
                                                          op0=ALU.mult, op1=ALU.add), [A, cv], [U])
            for j in (0, 1, 3):
                s = j - 2
                for (a0, a1) in segs:
                    lo = max(a0, a0 - s)
                    hi = min(a1, a1 - s)
                    kb.op("dve", lambda v, n_=n_, j=j, s=s, lo=lo, hi=hi: v.scalar_tensor_tensor(
                        out=U.ap[:, lo:hi], in0=A.ap[:, lo + s:hi + s], scalar=cv.ap[:, slot, n_, j:j + 1], in1=U.ap[:, lo:hi],
                        op0=ALU.mult, op1=ALU.add), [A, U, cv], [U])
            kb.op("act", lambda a: a.activation(out=Ub.ap, in_=U.ap, func=AF.Copy), [U], [Ub])
            for d_ in range(2):
                for ti, (t0, n) in enumerate(TT):
                    pr = kb.ps()
                    pi = kb.ps()

                    def mm(pe, g=g, d_=d_, t0=t0, n=n, pr=pr, pi=pi):
                        pe.matmul(pr.ap[:, 0:n], lhsT=g.ap[:, d_, 0, :], rhs=Ub.ap[:, t0:t0 + n], start=True, stop=True)
                        return pe.matmul(pi.ap[:, 0:n], lhsT=g.ap[:, d_, 1, :], rhs=Ub.ap[:, t0:t0 + n], start=True, stop=True)
                    kb.op("pe", mm, [g, Ub], [pr, pi])
                    kb.op("act", lambda a, pr=pr, t0=t0, n=n, d_=d_, n_=n_: a.activation(
                        out=A.ap[:, t0:t0 + n], in_=pr.ap[:, 0:n], func=AF.Sigmoid, bias=gv.ap[:, slot, d_, n_, 0:1]), [pr, gv], [A])
                    kb.op("act", lambda a, pi=pi, t0=t0, n=n, d_=d_, n_=n_: a.activation(
                        out=T2.ap[:, t0:t0 + n], in_=pi.ap[:, 0:n], func=AF.Sigmoid, bias=gv.ap[:, slot, d_, n_, 1:2]), [pi, gv], [T2])
                kb.op("act", lambda a, d_=d_, n_=n_: a.activation(out=A.ap, in_=A.ap, func=AF.Exp, scale=cexp.ap[:, d_, n_:n_ + 1]), [A, cexp], [A])
                kb.op("dve", lambda v: v.tensor_tensor(out=T1.ap, in0=A.ap, in1=A.ap, op=ALU.mult), [A], [T1])
                kb.op("act", lambda a: a.activation(out=T1.ap, in_=T1.ap, func=AF.Sqrt, scale=-1.0, bias=1.0), [T1], [T1])
                kb.op("dve", lambda v: v.tensor_tensor(out=T2.ap, in0=T2.ap, in1=U.ap, op=ALU.mult), [T2, U], [T2])
                kb.op("dve", lambda v: v.tensor_tensor(out=T2.ap, in0=T2.ap, in1=T1.ap, op=ALU.mult), [T2, T1], [T2])
                if d_ == 0:
                    kb.op("dve", lambda v: v.tensor_tensor_scan(out=H0.ap, data0=A.ap, data1=T2.ap, initial=0.0, op0=ALU.mult, op1=ALU.add),
                          [A, T2], [H0])
                else:
                    kb.op("dve", lambda v: v.tensor_tensor_scan(out=T1.ap[:, NCTX - 1::-1] if False else T1.ap[:, 0:NCTX][:, ::-1],
                                                                data0=A.ap[:, 0:NCTX][:, ::-1], data1=T2.ap[:, 0:NCTX][:, ::-1],
                                                                initial=0.0, op0=ALU.mult, op1=ALU.add), [A, T2], [T1])
                    kb.op("dve", lambda v: v.tensor_tensor_scan(out=T1.ap[:, NCTX:NTOK][:, ::-1], data0=A.ap[:, NCTX:NTOK][:, ::-1],
                                                                data1=T2.ap[:, NCTX:NTOK][:, ::-1], initial=T1.ap[:, 0:1],
                                                                op0=ALU.mult, op1=ALU.add), [A, T2, T1], [T1])
            kb.op("dve", lambda v: v.tensor_tensor(out=H0.ap, in0=H0.ap, in1=T1.ap, op=ALU.add), [H0, T1], [H0])
            for ti, (t0, n) in enumerate(TT):
                m = mt[ti % 2]
                p = kb.ps()

                def mm(pe, w=w, ti=ti, n=n, p=p):
                    for kc in range(KC):
                        ins = pe.matmul(p.ap[:, 0:n], lhsT=w.ap[:, kc, 0, :], rhs=HT[ti].ap[:, kc, 0:n], start=(kc == 0), stop=(kc == KC - 1))
                    return ins
                kb.op("pe", mm, [w, HT[ti]], [p])
                gt = gtmp[ti % 2]
                kb.op("act", lambda a, p=p, n=n, gt=gt: a.activation(out=gt.ap[:, 0:n], in_=p.ap[:, 0:n], func=AF.Gelu_apprx_tanh), [p], [gt])
                kb.op("dve", lambda v, gt=gt, t0=t0, n=n, m=m: v.tensor_tensor(out=m.ap[:, 0:n], in0=gt.ap[:, 0:n], in1=H0.ap[:, t0:t0 + n],
                                                                              op=ALU.mult), [gt, H0], [m])
                kb.dma("sp", MT[n_ * 128:(n_ + 1) * 128, t0:t0 + n], m.ap[:, 0:n], reads=[m], writes=[MTb[n_]])
        out_proj(l, lru_w_out[slot], NRC)

    def out_proj(l, w_dram, nk):
        kb.reset_arena(base_off)
        wo = kb.alloc("wo", [nk, D], BF16)
        load_w(wo, wo.ap, w_dram.rearrange("(n p) f -> p n f", p=128))
        mts = [kb.alloc(f"mts{i}", [nk, 512], BF16) for i in range(2)]
        xres = [kb.alloc(f"xres{i}", [512], F32) for i in range(3)]
        MTv = MT.rearrange("(n p) t -> p n t", p=128)
        nxr = 0
        for ti, (t0, n) in enumerate(TT):
            ms = mts[ti % 2]
            kb.dma("sp", ms.ap[:, :, 0:n], MTv[:, 0:nk, t0:t0 + n], reads=MTb[0:nk], writes=[ms])
            for mc in range(KC):
                p = kb.ps()

                def mm(pe, ms=ms, mc=mc, n=n, p=p):
                    for k in range(nk):
                        ins = pe.matmul(p.ap[:, 0:n], lhsT=wo.ap[:, k, mc * 128:(mc + 1) * 128], rhs=ms.ap[:, k, 0:n],
                                        start=(k == 0), stop=(k == nk - 1))
                    return ins
                kb.op("pe", mm, [wo, ms], [p])
                xr = xres[nxr % 3]
                nxr += 1
                resid(l, 2, ti, mc, p, xr)

    def norm_all(l):
        HT = [kb.alloc(f"HT{ti}", [KC, TT[ti][1]], BF16) for ti in range(len(TT))]
        o1 = kb.off
        xs = kb.alloc("xs", [KC, 512], F32)
        sq = kb.alloc("sq", [KC, 512], BF16)
        rs = kb.alloc("rs", [512], F32)
        tmp = [kb.alloc(f"tmp{i}", [512], F32) for i in range(2)]
        for ti in range(len(TT)):
            n = TT[ti][1]
            norm_tile(l, 0, ti, xs, sq, rs, lambda kc, ti=ti, n=n: (HT[ti].ap[:, kc, 0:n], HT[ti]), tmp)
        kb.reset_arena(o1)
        return HT

    def hgrn2_phase(l):
        slot = 0
        kb.reset_arena(base_off)
        HT = norm_all(l)
        NB = NTOK // 128
        HC = kb.alloc("HC", [2, 128], F32)
        kb.dma("sp", HC.ap, hgc, writes=[HC])
        maskb = kb.alloc("maskb", [2, 128], BF16)
        kb.op("dve", lambda v: v.tensor_copy(out=maskb.ap, in_=HC.ap), [HC], [maskb])
        ng = kb.alloc("hng", [1], F32)
        kb.dma("sp", ng.ap, hg_ng, writes=[ng])
        lbl = kb.alloc("lbl", [KC, DEPTH], F32)
        lbv = kb.alloc("lbv", [KC, 4], F32)
        kb.dma("sp", lbl.ap, hg_lb, writes=[lbl])
        kb.op("act", lambda a: a.activation(out=lbl.ap, in_=lbl.ap, func=AF.Exp), [lbl], [lbl])
        kb.op("dve", lambda v: v.tensor_tensor(out=lbv.ap[:, :, 2], in0=lbl.ap[:, :, 0], in1=lbl.ap[:, :, 1], op=ALU.add), [lbl], [lbv])
        kb.op("dve", lambda v: v.tensor_tensor(out=lbv.ap[:, :, 2], in0=lbv.ap[:, :, 2], in1=lbl.ap[:, :, 2], op=ALU.add), [lbl, lbv], [lbv])
        kb.op("dve", lambda v: v.tensor_tensor(out=lbv.ap[:, :, 2], in0=lbv.ap[:, :, 2], in1=lbl.ap[:, :, 3], op=ALU.add), [lbl, lbv], [lbv])
        kb.op("dve", lambda v: v.memset(lbv.ap[:, :, 3], 0.0), [lbv], [lbv])
        for j in range(1, l + 1):
            kb.op("dve", lambda v, j=j: v.tensor_tensor(out=lbv.ap[:, :, 3], in0=lbv.ap[:, :, 3], in1=lbl.ap[:, :, j], op=ALU.add), [lbl, lbv], [lbv])
        kb.op("dve", lambda v: v.reciprocal(out=lbv.ap[:, :, 2], in_=lbv.ap[:, :, 2]), [lbv], [lbv])
        kb.op("dve", lambda v: v.tensor_tensor(out=lbv.ap[:, :, 0], in0=lbv.ap[:, :, 3], in1=lbv.ap[:, :, 2], op=ALU.mult), [lbv], [lbv])
        kb.op("dve", lambda v: v.tensor_scalar(out=lbv.ap[:, :, 1], in0=lbv.ap[:, :, 0], scalar1=-1.0, scalar2=1.0, op0=ALU.mult, op1=ALU.add), [lbv], [lbv])

        whd = kb.alloc("whd", [KC, 5, 128], BF16)
        QT = kb.alloc("QT", [NTOK], BF16)
        Vtok = kb.alloc("Vtok", [NB, 128], BF16)
        O = kb.alloc("O", [NTOK], F32)
        KT = [kb.alloc(f"KT{d}", [NTOK], BF16) for d in range(2)]
        Bc = [kb.alloc(f"Bc{d}", [NTOK], F32) for d in range(2)]
        S = [kb.alloc(f"S{d}", [128], F32) for d in range(2)]
        ft = [kb.alloc(f"ft{i}", [512], F32) for i in range(2)]
        lt = [kb.alloc(f"lt{i}", [512], F32) for i in range(2)]
        E1 = [kb.alloc(f"E1_{i}", [128], F32) for i in range(2)]
        E2 = [kb.alloc(f"E2_{i}", [128], F32) for i in range(2)]
        EB = [kb.alloc(f"EB_{i}", [4], F32) for i in range(2)]
        Qt = [kb.alloc(f"Qt{i}", [128], BF16) for i in range(2)]
        Kt = [kb.alloc(f"Kt{i}", [128], BF16) for i in range(2)]
        KtT = [kb.alloc(f"KtT{i}", [128], BF16) for i in range(2)]
        Sp = [kb.alloc(f"Sp{i}", [128], BF16) for i in range(2)]
        Am = [kb.alloc(f"Am{i}", [128], BF16) for i in range(2)]
        tS = [kb.alloc(f"tS{i}", [128], F32) for i in range(2)]
        sgt = ft
        ogt = lt
        sqo = [kb.alloc(f"sqo{i}", [512], BF16) for i in range(2)]
        rso = [kb.alloc(f"rso{i}", [512], F32) for i in range(2)]
        ogb = [kb.alloc(f"ogb{i}", [512], BF16) for i in range(2)]
        wv = hg_w_in[slot].rearrange("(kc p) (v f) -> p kc v f", p=128, v=5)
        order = [list(range(NB)), [1, 0] + list(range(NB - 1, 1, -1))]
        for hd in range(8):
            for v_ in range(5):
                load_w(whd, whd.ap[:, :, v_, :], wv[:, :, v_, hd * 128:(hd + 1) * 128])
            kb.op("pool", lambda g: g.memset(O.ap, 0.0), [], [O])
            for d in range(2):
                kb.op("pool", lambda g, d=d: g.memset(S[d].ap, 0.0), [], [S[d]])
            for ti, (t0, n) in enumerate(TT):
                def proj(v_, ti=ti, n=n):
                    p = kb.ps()

                    def mm(pe):
                        for kc in range(KC):
                            ins = pe.matmul(p.ap[:, 0:n], lhsT=whd.ap[:, kc, v_, :], rhs=HT[ti].ap[:, kc, 0:n], start=(kc == 0), stop=(kc == KC - 1))
                        return ins
                    kb.op("pe", mm, [whd, HT[ti]], [p])
                    return p
                p = proj(0)
                kb.op("act", lambda a, p=p, t0=t0, n=n: a.activation(out=QT.ap[:, t0:t0 + n], in_=p.ap[:, 0:n], func=AF.Silu), [p], [QT])
                for d in range(2):
                    p = proj(3 + d)
                    f_ = ft[d]
                    l_ = lt[d]
                    kb.op("act", lambda a, p=p, n=n, f_=f_: a.activation(out=f_.ap[:, 0:n], in_=p.ap[:, 0:n], func=AF.Sigmoid), [p], [f_])
                    kb.op("dve", lambda v, n=n, f_=f_, hd=hd: v.tensor_scalar(out=f_.ap[:, 0:n], in0=f_.ap[:, 0:n], scalar1=lbv.ap[:, hd, 1:2],
                                                                            scalar2=lbv.ap[:, hd, 0:1], op0=ALU.mult, op1=ALU.add), [f_, lbv], [f_])
                    kb.op("act", lambda a, n=n, f_=f_, l_=l_: a.activation(out=l_.ap[:, 0:n], in_=f_.ap[:, 0:n], func=AF.Ln), [f_], [l_])
                    kb.op("dve", lambda v, n=n, f_=f_, d=d, t0=t0: v.tensor_scalar(out=KT[d].ap[:, t0:t0 + n], in0=f_.ap[:, 0:n], scalar1=-1.0, scalar2=1.0,
                                                                                  op0=ALU.mult, op1=ALU.add), [f_], [KT[d]])
                    for bi in range(n // 128):
                        c0 = t0 + bi * 128
                        if d == 0:
                            kb.op("dve", lambda v, l_=l_, bi=bi, c0=c0: v.tensor_tensor_scan(
                                out=Bc[0].ap[:, c0:c0 + 128], data0=ones_f.ap, data1=l_.ap[:, bi * 128:(bi + 1) * 128], initial=0.0,
                                op0=ALU.mult, op1=ALU.add), [ones_f, l_], [Bc[0]])
                        else:
                            kb.op("dve", lambda v, l_=l_, bi=bi, c0=c0: v.tensor_tensor_scan(
                                out=Bc[1].ap[:, c0:c0 + 128][:, ::-1], data0=ones_f.ap, data1=l_.ap[:, bi * 128:(bi + 1) * 128][:, ::-1], initial=0.0,
                                op0=ALU.mult, op1=ALU.add), [ones_f, l_], [Bc[1]])
                for bi in range(n // 128):
                    blk = t0 // 128 + bi
                    p = kb.ps()

                    def mmv(pe, ti=ti, bi=bi, p=p):
                        for kc in range(KC):
                            ins = pe.matmul(p.ap[:, 0:128], lhsT=HT[ti].ap[:, kc, bi * 128:(bi + 1) * 128], rhs=whd.ap[:, kc, 1, :],
                                            start=(kc == 0), stop=(kc == KC - 1))
                        return ins
                    kb.op("pe", mmv, [whd, HT[ti]], [p])
                    kb.op("act", lambda a, p=p, blk=blk: a.activation(out=Vtok.ap[:, blk, :], in_=p.ap[:, 0:128], func=AF.Copy), [p], [Vtok])
            for j in range(NB):
                for d in range(2):
                    blk = order[d][j]
                    i2 = d
                    c0 = blk * 128
                    tsl = slice(c0, c0 + 128)
                    midc = c0 + (63 if d == 0 else 64)
                    endc = c0 + (127 if d == 0 else 0)
                    epos = 127 if d == 0 else 0
                    kb.op("dve", lambda v, i2=i2, d=d, midc=midc: v.tensor_scalar(out=EB[i2].ap[:, 2:3], in0=Bc[d].ap[:, midc:midc + 1], scalar1=-1.0, scalar2=None,
                                                                                 op0=ALU.mult), [Bc[d]], [EB[i2]])
                    kb.op("act", lambda a, i2=i2, d=d, tsl=tsl: a.activation(out=E1[i2].ap, in_=Bc[d].ap[:, tsl], func=AF.Exp, bias=EB[i2].ap[:, 2:3]), [Bc[d], EB[i2]], [E1[i2]])
                    kb.op("act", lambda a, i2=i2, d=d, tsl=tsl, midc=midc: a.activation(out=E2[i2].ap, in_=Bc[d].ap[:, tsl], func=AF.Exp, scale=-1.0,
                                                                                       bias=Bc[d].ap[:, midc:midc + 1]), [Bc[d]], [E2[i2]])
                    kb.op("act", lambda a, i2=i2, d=d, endc=endc: a.activation(out=EB[i2].ap[:, 0:1], in_=Bc[d].ap[:, endc:endc + 1], func=AF.Exp), [Bc[d]], [EB[i2]])
                    kb.op("act", lambda a, i2=i2, d=d, midc=midc: a.activation(out=EB[i2].ap[:, 1:2], in_=Bc[d].ap[:, midc:midc + 1], func=AF.Exp), [Bc[d]], [EB[i2]])
                    kb.op("dve", lambda v, i2=i2, tsl=tsl: v.tensor_tensor(out=Qt[i2].ap, in0=QT.ap[:, tsl], in1=E1[i2].ap, op=ALU.mult), [QT, E1[i2]], [Qt[i2]])
                    kb.op("dve", lambda v, i2=i2, tsl=tsl, d=d: v.tensor_tensor(out=Kt[i2].ap, in0=KT[d].ap[:, tsl], in1=E2[i2].ap, op=ALU.mult), [KT[d], E2[i2]], [Kt[i2]])
                    kb.op("dve", lambda v, i2=i2, d=d: v.tensor_scalar(out=Sp[i2].ap, in0=S[d].ap, scalar1=EB[i2].ap[:, 1:2], scalar2=None, op0=ALU.mult), [S[d], EB[i2]], [Sp[i2]])
                    pK = kb.ps()
                    pKb = pK.ap.bitcast(BF16)
                    kb.op("pe", lambda pe, pKb=pKb, i2=i2: pe.transpose(pKb[:, 0:128], Kt[i2].ap, ident_b.ap), [Kt[i2], ident_b], [pK])
                    kb.op("act", lambda a, pKb=pKb, i2=i2: a.activation(out=KtT[i2].ap, in_=pKb[:, 0:128], func=AF.Copy), [pK], [KtT[i2]])
                    pA = kb.ps()
                    kb.op("pe", lambda pe, pA=pA, i2=i2: pe.matmul(pA.ap[:, 0:128], lhsT=Kt[i2].ap, rhs=Qt[i2].ap, start=True, stop=True), [Kt[i2], Qt[i2]], [pA])
                    kb.op("dve", lambda v, pA=pA, i2=i2: v.tensor_scalar(out=tS[i2].ap, in0=pA.ap[:, 0:128], scalar1=1e30, scalar2=-1e30, op0=ALU.min, op1=ALU.max), [pA], [tS[i2]])
                    kb.op("dve", lambda v, i2=i2, d=d: v.tensor_tensor(out=Am[i2].ap, in0=tS[i2].ap, in1=maskb.ap[:, d, :], op=ALU.mult), [tS[i2], maskb], [Am[i2]])
                    pO = kb.ps()

                    def mmo(pe, pO=pO, i2=i2, blk=blk):
                        pe.matmul(pO.ap[:, 0:128], lhsT=Sp[i2].ap, rhs=Qt[i2].ap, start=True, stop=False)
                        return pe.matmul(pO.ap[:, 0:128], lhsT=Vtok.ap[:, blk, :], rhs=Am[i2].ap, start=False, stop=True)
                    kb.op("pe", mmo, [Sp[i2], Qt[i2], Vtok, Am[i2]], [pO])
                    kb.op("dve", lambda v, pO=pO, tsl=tsl: v.tensor_tensor(out=O.ap[:, tsl], in0=pO.ap[:, 0:128], in1=O.ap[:, tsl], op=ALU.add), [pO, O], [O])
                    pS = kb.ps()
                    kb.op("pe", lambda pe, pS=pS, i2=i2, blk=blk: pe.matmul(pS.ap[:, 0:128], lhsT=KtT[i2].ap, rhs=Vtok.ap[:, blk, :], start=True, stop=True), [KtT[i2], Vtok], [pS])
                    kb.op("dve", lambda v, pS=pS, i2=i2, epos=epos: v.tensor_scalar(out=tS[i2].ap, in0=pS.ap[:, 0:128], scalar1=E1[i2].ap[:, epos:epos + 1], scalar2=None,
                                                                                   op0=ALU.mult), [pS, E1[i2]], [tS[i2]])
                    kb.op("dve", lambda v, i2=i2, d=d: v.scalar_tensor_tensor(out=S[d].ap, in0=S[d].ap, scalar=EB[i2].ap[:, 0:1], in1=tS[i2].ap,
                                                                             op0=ALU.mult, op1=ALU.add), [S[d], EB[i2], tS[i2]], [S[d]])
            for ti, (t0, n) in enumerate(TT):
                i2 = ti % 2
                p = kb.ps()

                def mmg(pe, ti=ti, n=n, p=p):
                    for kc in range(KC):
                        ins = pe.matmul(p.ap[:, 0:n], lhsT=whd.ap[:, kc, 2, :], rhs=HT[ti].ap[:, kc, 0:n], start=(kc == 0), stop=(kc == KC - 1))
                    return ins
                kb.op("pe", mmg, [whd, HT[ti]], [p])
                kb.op("act", lambda a, p=p, n=n, i2=i2: a.activation(out=sgt[i2].ap[:, 0:n], in_=p.ap[:, 0:n], func=AF.Silu), [p], [sgt[i2]])
                kb.op("act", lambda a, t0=t0, n=n, i2=i2: a.activation(out=sqo[i2].ap[:, 0:n], in_=O.ap[:, t0:t0 + n], func=AF.Square), [O], [sqo[i2]])
                p2 = kb.ps()
                kb.op("pe", lambda pe, p2=p2, n=n, i2=i2: pe.matmul(p2.ap[:, 0:n], lhsT=ones_b.ap, rhs=sqo[i2].ap[:, 0:n], start=True, stop=True), [ones_b, sqo[i2]], [p2])
                kb.op("act", lambda a, p2=p2, n=n, i2=i2: a.activation(out=rso[i2].ap[:, 0:n], in_=p2.ap[:, 0:n], func=AF.Sqrt, scale=1.0 / 128, bias=EPS_AP()), [p2, epsb], [rso[i2]])
                kb.op("dve", lambda v, n=n, i2=i2: v.reciprocal(out=rso[i2].ap[:, 0:n], in_=rso[i2].ap[:, 0:n]), [rso[i2]], [rso[i2]])
                kb.op("dve", lambda v, t0=t0, n=n, i2=i2: v.scalar_tensor_tensor(out=ogt[i2].ap[:, 0:n], in0=O.ap[:, t0:t0 + n], scalar=ng.ap[:, 0:1], in1=rso[i2].ap[:, 0:n],
                                                                                op0=ALU.mult, op1=ALU.mult), [O, ng, rso[i2]], [ogt[i2]])
                kb.op("dve", lambda v, n=n, i2=i2: v.tensor_tensor(out=ogb[i2].ap[:, 0:n], in0=ogt[i2].ap[:, 0:n], in1=sgt[i2].ap[:, 0:n], op=ALU.mult), [ogt[i2], sgt[i2]], [ogb[i2]])
                kb.dma("sp", MT[hd * 128:(hd + 1) * 128, t0:t0 + n], ogb[i2].ap[:, 0:n], reads=[ogb[i2]], writes=[MTb[hd]])
        out_proj(l, hg_w_o[slot], 8)

    def na_phase(l):
        slot = 0
        kb.reset_arena(base_off)
        HT = norm_all(l)
        NB = NTOK // 128
        kb.ps_lo, kb.ps_hi = 0, 4
        ncf = kb.alloc("ncf", [3, 128], F32)
        kb.dma("sp", ncf.ap, nac, writes=[ncf])
        ncb = kb.alloc("ncb", [3, 128], BF16)
        kb.op("dve", lambda v: v.tensor_copy(out=ncb.ap, in_=ncf.ap), [ncf], [ncb])
        cmask = kb.alloc("cmask", [15, 64], F32)
        kb.dma("sp", cmask.ap, na_cmask, writes=[cmask])
        qkg = kb.alloc("qkg", [2], F32)
        kb.dma("sp", qkg.ap, na_qkg, writes=[qkg])
        wq = kb.alloc("wqkv", [KC, 3, 128], BF16)
        QK = [kb.alloc(f"QK{i}", [NTOK], BF16) for i in range(2)]
        VAB = [kb.alloc(f"VAB{i}", [NB, 128], BF16) for i in range(2)]
        for i in range(2):
            kb.op("pool", lambda g, i=i: g.memset(VAB[i].ap, 0.0), [], [VAB[i]])
        EBraw = [kb.alloc(f"EBraw{i}", [15, 64], F32) for i in range(2)]
        EBt = [kb.alloc(f"EBt{i}", [15, 64], BF16) for i in range(2)]
        NCH = (6, 4, 4)
        MK = [[kb.alloc(f"MK{h2}_{c}", [NCH[c] * 256], BF16) for c in range(3)] for h2 in range(2)]
        qf = [kb.alloc(f"qf{i}", [512], F32) for i in range(2)]
        sqn = [kb.alloc(f"sqn{i}", [512], BF16) for i in range(2)]
        rsn = [kb.alloc(f"rsn{i}", [512], F32) for i in range(2)]
        P2 = [kb.alloc(f"P2_{i}", [2, 256], BF16) for i in range(4)]
        rden = [kb.alloc(f"rden{i}", [256], F32) for i in range(2)]
        ob = [kb.alloc(f"ob{i}", [256], BF16) for i in range(2)]
        wv3 = na_w_qkv[slot].rearrange("(kc p) (v f) -> p kc v f", p=128, v=3)
        blocks = []
        for qb in range(16):
            r0 = 4 * qb
            if qb == 0:
                cls, kr0s = 1, [0, 2, 4, 6]
            elif qb == 15:
                cls, kr0s = 2, [56, 58, 60, 62]
            else:
                cls, kr0s = 0, [r0 - 4 + 2 * c for c in range(6)]
            keys = [(NCTX + kr0 * 64, ci) for ci, kr0 in enumerate(kr0s)] + [(0, None), (128, None)]
            blocks.append((NCTX + r0 * 64, cls, keys))
        blocks.append((0, None, [(0, None), (128, None)]))
        cls_def = {0: (8, [4, 6, 8, 10, 12, 14]), 1: (0, [0, 2, 4, 6]), 2: (60, [56, 58, 60, 62])}
        nP = 0
        nblk = 0
        for hp in range(8):
            for v_ in range(3):
                load_w(wq, wq.ap[:, :, v_, :], wv3[:, :, v_, hp * 128:(hp + 1) * 128])
            for ti, (t0, n) in enumerate(TT):
                for which in range(2):
                    p = kb.ps()

                    def mm(pe, p=p, which=which, ti=ti, n=n):
                        for kc in range(KC):
                            ins = pe.matmul(p.ap[:, 0:n], lhsT=wq.ap[:, kc, which, :], rhs=HT[ti].ap[:, kc, 0:n], start=(kc == 0), stop=(kc == KC - 1))
                        return ins
                    kb.op("pe", mm, [wq, HT[ti]], [p])
                    i2 = which
                    kb.op("act", lambda a, p=p, n=n, i2=i2: a.activation(out=qf[i2].ap[:, 0:n], in_=p.ap[:, 0:n], func=AF.Copy), [p], [qf[i2]])
                    kb.op("act", lambda a, n=n, i2=i2: a.activation(out=sqn[i2].ap[:, 0:n], in_=qf[i2].ap[:, 0:n], func=AF.Square), [qf[i2]], [sqn[i2]])
                    p2 = kb.ps()
                    kb.op("pe", lambda pe, p2=p2, n=n, i2=i2: pe.matmul(p2.ap[:, 0:n], lhsT=ncb.ap[:, 0, :], rhs=sqn[i2].ap[:, 0:n], start=True, stop=True), [ncb, sqn[i2]], [p2])
                    kb.op("act", lambda a, p2=p2, n=n, i2=i2: a.activation(out=rsn[i2].ap[:, 0:n], in_=p2.ap[:, 0:n], func=AF.Sqrt, scale=1.0 / 64, bias=EPS_AP()), [p2, epsb], [rsn[i2]])
                    kb.op("dve", lambda v, n=n, i2=i2: v.reciprocal(out=rsn[i2].ap[:, 0:n], in_=rsn[i2].ap[:, 0:n]), [rsn[i2]], [rsn[i2]])
                    kb.op("dve", lambda v, n=n, i2=i2, which=which, t0=t0: v.scalar_tensor_tensor(
                        out=QK[which].ap[:, t0:t0 + n], in0=qf[i2].ap[:, 0:n], scalar=qkg.ap[:, which:which + 1], in1=rsn[i2].ap[:, 0:n],
                        op0=ALU.mult, op1=ALU.mult), [qf[i2], qkg, rsn[i2]], [QK[which]])
                for bi in range(n // 128):
                    blk = t0 // 128 + bi
                    p = kb.ps()

                    def mmv(pe, ti=ti, bi=bi, p=p):
                        for kc in range(KC):
                            ins = pe.matmul(p.ap[:, 0:128], lhsT=HT[ti].ap[:, kc, bi * 128:(bi + 1) * 128], rhs=wq.ap[:, kc, 2, :],
                                            start=(kc == 0), stop=(kc == KC - 1))
                        return ins
                    kb.op("pe", mmv, [wq, HT[ti]], [p])
                    kb.op("act", lambda a, p=p, blk=blk: a.activation(out=VAB[0].ap[:, blk, 0:64], in_=p.ap[:, 0:64], func=AF.Copy), [p], [VAB[0]])
                    kb.op("dve", lambda v, p=p, blk=blk: v.tensor_copy(out=VAB[1].ap[:, blk, 64:128], in_=p.ap[:, 64:128]), [p], [VAB[1]])
            for h2 in range(2):
                h = 2 * hp + h2
                for half in range(2):
                    kb.dma("sp", EBraw[h2].ap[half * 64:(half + 1) * 64], na_rp[h], writes=[EBraw[h2]])
                kb.op("act", lambda a, h2=h2: a.activation(out=EBraw[h2].ap, in_=EBraw[h2].ap, func=AF.Exp), [EBraw[h2]], [EBraw[h2]])
                kb.op("dve", lambda v, h2=h2: v.tensor_tensor(out=EBt[h2].ap, in0=EBraw[h2].ap, in1=cmask.ap, op=ALU.mult), [EBraw[h2], cmask], [EBt[h2]])
                for cls in range(3):
                    r0, kr0s = cls_def[cls]
                    mk = MK[h2][cls]
                    kb.op("pool", lambda g, mk=mk: g.memset(mk.ap, 0.0), [], [mk])

                    def copies(g, mk=mk, r0=r0, kr0s=kr0s, cls=cls, h2=h2):
                        ins = None
                        for ci, kr0 in enumerate(kr0s):
                            for a_ in range(2):
                                for bq in range(4):
                                    dr = kr0 + a_ - r0 - bq
                                    if cls == 0 and not (-4 <= dr <= 3):
                                        continue
                                    assert -7 <= dr <= 7
                                    ins = g.tensor_copy(out=mk.ap[a_ * 64:(a_ + 1) * 64, ci * 256 + bq * 64:ci * 256 + (bq + 1) * 64],
                                                        in_=EBt[h2].ap[a_ * 64:(a_ + 1) * 64, dr + 7, :])
                        return ins
                    kb.op("pool", copies, [EBt[h2]], [mk])
            for (q0, cls, keys) in blocks:
                num = kb.psum[4 + 2 * (nblk % 2)]
                den = kb.psum[5 + 2 * (nblk % 2)]
                i2 = nblk % 2
                nblk += 1
                npairs = len(keys) // 2
                total = 2 * npairs
                cnt = 0
                for h2 in range(2):
                    pb = 64 * h2
                    for pi in range(npairs):
                        (k0, c0), (k1, c1) = keys[2 * pi], keys[2 * pi + 1]
                        pS = kb.ps()

                        def mms(pe, pS=pS, pb=pb, k0=k0, k1=k1, q0=q0):
                            pe.matmul(pS.ap[:, 0:256], lhsT=QK[1].ap[pb:pb + 64, k0:k0 + 128], rhs=QK[0].ap[pb:pb + 64, q0:q0 + 256], start=True, stop=True)
                            return pe.matmul(pS.ap[:, 256:512], lhsT=QK[1].ap[pb:pb + 64, k1:k1 + 128], rhs=QK[0].ap[pb:pb + 64, q0:q0 + 256], start=True, stop=True)
                        kb.op("pe", mms, [QK[0], QK[1]], [pS])
                        P = P2[nP % 4]
                        nP += 1
                        kb.op("act", lambda a, pS=pS, P=P: a.activation(out=P.ap.rearrange("p a b -> p (a b)"), in_=pS.ap, func=AF.Exp, scale=0.125), [pS], [P])
                        if c0 is not None:
                            mk = MK[h2][cls]
                            kb.op("pool", lambda g, P=P, mk=mk, c0=c0: g.tensor_tensor(out=P.ap.rearrange("p a b -> p (a b)"), in0=P.ap.rearrange("p a b -> p (a b)"),
                                                                                       in1=mk.ap[:, c0 * 256:c0 * 256 + 512], op=ALU.mult), [P, mk], [P])
                        first = (cnt == 0)
                        last = (cnt == total - 1)
                        cnt += 1

                        def mmpv(pe, P=P, h2=h2, k0=k0, k1=k1, first=first, last=last, num=num, den=den):
                            for c, kk in ((0, k0), (1, k1)):
                                st = first and c == 0
                                sp_ = last and c == 1
                                pe.matmul(num.ap[:, 0:256], lhsT=VAB[h2].ap[:, kk // 128, :], rhs=P.ap[:, c, :], start=st, stop=sp_)
                                ins = pe.matmul(den.ap[:, 0:256], lhsT=ncb.ap[:, 1 + h2, :], rhs=P.ap[:, c, :], start=st, stop=sp_)
                            return ins
                        kb.op("pe", mmpv, [P, VAB[h2], ncb], [num, den])
                kb.op("dve", lambda v, den=den, i2=i2: v.reciprocal(out=rden[i2].ap, in_=den.ap[:, 0:256]), [den], [rden[i2]])
                kb.op("dve", lambda v, num=num, i2=i2: v.tensor_tensor(out=ob[i2].ap, in0=num.ap[:, 0:256], in1=rden[i2].ap, op=ALU.mult), [num, rden[i2]], [ob[i2]])
                kb.dma("sp", MT[hp * 128:(hp + 1) * 128, q0:q0 + 256], ob[i2].ap, reads=[ob[i2]], writes=[MTb[hp]])
        kb.ps_lo, kb.ps_hi = 0, 8
        out_proj(l, na_w_o[slot], 8)

    for ph, l in phases:
        if ph == "ffn":
            ffn_phase(l)
        elif ph == "mix":
            kind = l % 3
            if kind == 0:
                rglru_phase(l)
            elif kind == 2:
                hgrn2_phase(l)
            else:
                na_phase(l)
    kb.barrier()
    for ti, (t0, n) in enumerate(TT):
        if ti == 0 and not cfg.get("dump_ctx"):
            continue
    if cfg.get("dump_all"):
        outA = nc.dram_tensor("outA", [D, NTOK], F32, kind="ExternalOutput").ap()
        for ti, (t0, n) in enumerate(TT):
            kb.dma("sp", outA[:, t0:t0 + n], XT[:, t0:t0 + n], reads=XTb[ti])
    for ti, (t0, n) in enumerate(TT):
        if ti == 0:
            continue
        kb.dma("sp", outT[:, t0 - NCTX:t0 - NCTX + n], XT[:, t0:t0 + n], reads=XTb[ti])
    kb.finish()
    return kb


def host_inputs(inp, b):
    f = np.float32
    m = {}
    m["xT"] = np.ascontiguousarray(np.concatenate([inp["ctx"][b], inp["x"][b]], axis=0).T.astype(f))
    cT = np.stack([inp["c"][b].reshape(KC, 128).T, inp["c_ctx"].reshape(KC, 128).T], axis=-1)
    m["cT"] = np.ascontiguousarray(cT.astype(f))
    m["mod_w"] = np.ascontiguousarray(inp["mod_w"])
    m["mod_b"] = np.ascontiguousarray(inp["mod_b"].reshape(DEPTH, 48, 128).transpose(2, 0, 1))
    m["mod_brow"] = np.ascontiguousarray(np.broadcast_to(inp["mod_b"][:, None, :], (DEPTH, 2, 6 * D))).astype(f)
    ng = np.stack([inp["norm_mix_g"].reshape(DEPTH, KC, 128), inp["norm_ffn_g"].reshape(DEPTH, KC, 128)], axis=1)
    m["norm_g"] = np.ascontiguousarray(ng.transpose(3, 0, 1, 2))
    m["ffn_w_gu"] = np.ascontiguousarray(inp["ffn_w_gu"])
    m["ffn_w_down"] = np.ascontiguousarray(inp["ffn_w_down"])
    m["lru_w_in"] = np.ascontiguousarray(inp["lru_w_in"])
    cw = inp["lru_conv_w"].reshape(2, 4, NRC, 128)
    cb = inp["lru_conv_b"].reshape(2, 1, NRC, 128)
    cv = np.concatenate([cw, cb], axis=1)
    m["lru_conv"] = np.ascontiguousarray(cv.transpose(3, 0, 2, 1))
    m["lru_gate_w"] = np.ascontiguousarray(inp["lru_gate_w"])
    gb = inp["lru_gate_b"].reshape(2, 2, 2, NRC, 128)
    lam = inp["lru_lambda"].reshape(2, 2, 1, NRC, 128)
    gvv = np.concatenate([gb, lam], axis=2)
    m["lru_gvec"] = np.ascontiguousarray(gvv.transpose(4, 0, 1, 3, 2))
    m["lru_w_out"] = np.ascontiguousarray(inp["lru_w_out"])
    m["hg_w_in"] = np.ascontiguousarray(inp["hg_w_in"])
    m["hg_lb"] = np.ascontiguousarray(inp["hg_lb_logits"].reshape(DEPTH, KC, 128).transpose(2, 1, 0))
    m["hg_ng"] = np.ascontiguousarray(inp["hg_norm_g"].reshape(128, 1))
    m["hg_w_o"] = np.ascontiguousarray(inp["hg_w_o"])
    m["hgc"] = hg_consts()
    m["na_w_qkv"] = np.ascontiguousarray(inp["na_w_qkv"])
    m["na_w_o"] = np.ascontiguousarray(inp["na_w_o"])
    m["na_qkg"] = np.ascontiguousarray(np.stack([np.tile(inp["na_q_norm_g"][0], 2), np.tile(inp["na_k_norm_g"][0], 2)], axis=1).astype(f))
    kc_ = np.arange(64)[:, None]
    qc_ = np.arange(64)[None, :]
    co = kc_ - qc_ + 15
    ok = (co >= 0) & (co <= 30)
    rp = inp["na_rpb"][0][:, :, np.clip(co, 0, 30)]
    rp = np.where(ok[None, None], rp, 0.0).astype(f)
    m["na_rp"] = np.ascontiguousarray(rp.transpose(0, 2, 1, 3))
    m["na_cmask"], m["nac"] = na_consts()
    c = np.zeros((128, 512), f)
    c[:, 0:128] = np.eye(128, dtype=f)
    c[:, 128:256] = 1.0
    m["consts"] = c
    return m


def na_consts():
    kc = np.arange(64)[:, None]
    qc = np.arange(64)[None, :]
    cs = np.clip(qc - 8, 0, 48)
    cm = ((kc >= cs) & (kc < cs + 16)).astype(np.float32)
    cmask = np.ascontiguousarray(np.broadcast_to(np.tile(cm, (2, 1))[:, None, :], (128, 15, 64))).astype(np.float32)
    nac = np.zeros((128, 3, 128), np.float32)
    nac[0:64, 0, 0:64] = 1.0
    nac[64:128, 0, 64:128] = 1.0
    nac[:, 1, 0:64] = 1.0
    nac[:, 2, 64:128] = 1.0
    return cmask, nac


def hg_consts():
    c = np.zeros((128, 2, 128), np.float32)
    s_ = np.arange(128)[:, None]
    t_ = np.arange(128)[None, :]
    c[:, 0, :] = (s_ <= t_)
    c[:, 1, :] = (s_ >= t_)
    return c


FULL_PHASES = [(p, l) for l in range(DEPTH) for p in ("mix", "ffn")]


def kernel(**inputs):
    inp = {k: np.asarray(v) for k, v in inputs.items()}
    kb = build({"phases": FULL_PHASES})
    in_maps = [host_inputs(inp, b) for b in range(8)]
    res = run_bass_kernel_spmd(kb.nc, in_maps, core_ids=list(range(8)))
    out = np.stack([np.ascontiguousarray(r["outT"].T) for r in res.results], axis=0)
    return out.astype(np.float32)
```
